# Optimizing a Trainium2 kernel written in Bass

```python
import math
import jax, jax.numpy as jnp
from jax import lax
import numpy as np

D_MODEL = 2048
BATCH = 2
SEQ = 4096
DEPTH = 1

HEAD_DIM = 128
N_ATTN_HEADS = 8
N_DELTA_HEADS = 8
ATTN_WIDTH = N_ATTN_HEADS * HEAD_DIM
DELTA_WIDTH = N_DELTA_HEADS * HEAD_DIM
MIX_WIDTH = ATTN_WIDTH + DELTA_WIDTH
DILATED_PATTERNS = ((128, 1), (512, 4), (2048, 16))
Q_BLOCK = 128
ROPE_THETA = 500000.0
ROPE_DIM = HEAD_DIM // 4
CONV_WIDTH = 4
CHUNK = 64
D_FF = 5632
NORM_EPS = 1e-6
N_MOD = 9
IN_PROJ_WIDTH = 3 * ATTN_WIDTH + 4 * DELTA_WIDTH + 2 * N_DELTA_HEADS

kernel_name = "hybrid_dilated_attn_gated_deltanet_macaron_layer"


def rms_norm(x, gain):
    xf = x.astype(jnp.float32)
    y = xf * lax.rsqrt(jnp.mean(xf * xf, axis=-1, keepdims=True) + NORM_EPS)
    return (y * gain.astype(jnp.float32)).astype(x.dtype)


def l2_norm(x):
    xf = x.astype(jnp.float32)
    return xf * lax.rsqrt(jnp.sum(xf * xf, axis=-1, keepdims=True) + NORM_EPS)


def modulate(x, gain, shift, scale):
    return rms_norm(x, gain) * (1.0 + scale[:, None, :]) + shift[:, None, :]


def swiglu(h, w_gate, w_up, w_down):
    return (jax.nn.silu(h @ w_gate) * (h @ w_up)) @ w_down


def partial_rope(x, positions):
    half = ROPE_DIM // 2
    inv_freq = ROPE_THETA ** (-jnp.arange(half, dtype=jnp.float32) / half)
    ang = positions.astype(jnp.float32)[..., None] * inv_freq
    cos = jnp.cos(ang)[:, :, None, :]
    sin = jnp.sin(ang)[:, :, None, :]
    x1 = x[..., :half].astype(jnp.float32)
    x2 = x[..., half:ROPE_DIM].astype(jnp.float32)
    rot = jnp.concatenate([x1 * cos - x2 * sin, x2 * cos + x1 * sin], axis=-1).astype(x.dtype)
    return jnp.concatenate([rot, x[..., ROPE_DIM:]], axis=-1)


def dilated_window_branch(q, k, v, dilation, span):
    B, Sp, H, Dh = q.shape
    L = Sp // dilation
    nblk = L // Q_BLOCK

    def residue_blocks(t):
        t = t.reshape(B, L, dilation, H, Dh).transpose(0, 2, 1, 3, 4)
        return t.reshape(B, dilation, nblk, Q_BLOCK, H, Dh)

    def with_previous_block(t):
        prev = jnp.pad(t[:, :, :-1], ((0, 0), (0, 0), (1, 0), (0, 0), (0, 0), (0, 0)))
        return jnp.concatenate([prev, t], axis=3)

    qb = residue_blocks(q)
    kw = with_previous_block(residue_blocks(k))
    vw = with_previous_block(residue_blocks(v))
    s = jnp.einsum('brnqhd,brnkhd->brnhqk', qb, kw,
                   preferred_element_type=jnp.float32) * (HEAD_DIM ** -0.5)
    qi = jnp.arange(Q_BLOCK)[:, None]
    kj = jnp.arange(2 * Q_BLOCK)[None, :]
    dist = qi + Q_BLOCK - kj
    key_index = jnp.arange(nblk)[:, None, None] * Q_BLOCK - Q_BLOCK + kj[None]
    mask = (dist >= 0) & (dist <= span) & (key_index >= 0)
    s = jnp.where(mask[None, None, :, None], s, -jnp.inf)
    m = jnp.max(s, axis=-1, keepdims=True)
    p = jnp.exp(s - m)
    denom = jnp.sum(p, axis=-1)
    o = jnp.einsum('brnhqk,brnkhd->brnqhd', p, vw.astype(jnp.float32))
    o = o / jnp.swapaxes(denom, 3, 4)[..., None]
    lse = jnp.swapaxes(m[..., 0] + jnp.log(denom), 3, 4)

    def to_sequence(t):
        t = t.reshape(B, dilation, L, *t.shape[4:])
        return jnp.swapaxes(t, 1, 2).reshape(B, Sp, *t.shape[3:])

    return to_sequence(o), to_sequence(lse)


def dilated_attention(q, k, v):
    B, S, H, Dh = q.shape
    unit = Q_BLOCK
    for _, d in DILATED_PATTERNS:
        unit = unit * d // math.gcd(unit, d * Q_BLOCK) if False else math.lcm(unit, d * Q_BLOCK)
    Sp = -(-S // unit) * unit
    pad = ((0, 0), (0, Sp - S), (0, 0), (0, 0))
    q, k, v = (jnp.pad(t, pad) for t in (q, k, v))
    outs, lses = [], []
    for window, dilation in DILATED_PATTERNS:
        o, lse = dilated_window_branch(q, k, v, dilation, window // dilation)
        outs.append(o)
        lses.append(lse)
    wts = jax.nn.softmax(jnp.stack(lses, axis=0), axis=0)
    o = jnp.einsum('pbsh,pbshd->bshd', wts, jnp.stack(outs, axis=0))
    return o[:, :S].astype(q.dtype)


def causal_depthwise_conv(x, w):
    C = x.shape[-1]
    y = lax.conv_general_dilated(x, w[:, None, :].astype(x.dtype), window_strides=(1,),
                                 padding=[(CONV_WIDTH - 1, 0)],
                                 dimension_numbers=('NWC', 'WIO', 'NWC'),
                                 feature_group_count=C)
    return jax.nn.silu(y)


def gated_delta_rule(q, k, v, g, beta):
    B, S, H, Dk = q.shape
    Dv = v.shape[-1]
    N = S // CHUNK
    f32 = jnp.float32

    def chunk_vec(t):
        return t.astype(f32).reshape(B, N, CHUNK, H, t.shape[-1]).transpose(0, 3, 1, 2, 4)

    def chunk_scalar(t):
        return t.astype(f32).reshape(B, N, CHUNK, H).transpose(0, 3, 1, 2)

    q = chunk_vec(q) * (Dk ** -0.5)
    k, v = chunk_vec(k), chunk_vec(v)
    g, beta = chunk_scalar(g), chunk_scalar(beta)
    gc = jnp.cumsum(g, axis=-1)
    causal = jnp.tril(jnp.ones((CHUNK, CHUNK), bool))
    strict = jnp.tril(jnp.ones((CHUNK, CHUNK), bool), k=-1)
    gamma = jnp.exp(jnp.where(causal, gc[..., :, None] - gc[..., None, :], -jnp.inf))
    kb = k * beta[..., None]
    a = jnp.where(strict, jnp.einsum('bhnid,bhnjd->bhnij', kb, k) * gamma, 0.0)
    eye = jnp.eye(CHUNK, dtype=f32)
    t_inv = lax.linalg.triangular_solve(eye + a, jnp.broadcast_to(eye, a.shape),
                                        left_side=True, lower=True, unit_diagonal=True)
    u = t_inv @ (v * beta[..., None])
    w = t_inv @ (kb * jnp.exp(gc)[..., None])
    q_decay = q * jnp.exp(gc)[..., None]
    k_tail = k * jnp.exp(gc[..., -1:] - gc)[..., None]
    chunk_decay = jnp.exp(gc[..., -1])
    intra = jnp.einsum('bhnid,bhnjd->bhnij', q, k) * gamma

    def step(state, xs):
        u_c, w_c, qd_c, kt_c, intra_c, dec_c = xs
        v_new = u_c - jnp.einsum('bhcd,bhde->bhce', w_c, state)
        o_c = (jnp.einsum('bhcd,bhde->bhce', qd_c, state)
               + jnp.einsum('bhij,bhje->bhie', intra_c, v_new))
        state = state * dec_c[..., None, None] + jnp.einsum('bhcd,bhce->bhde', kt_c, v_new)
        return state, o_c

    xs = tuple(jnp.moveaxis(t, 2, 0) for t in (u, w, q_decay, k_tail, intra, chunk_decay))
    _, o = lax.scan(step, jnp.zeros((B, H, Dk, Dv), f32), xs)
    return o.transpose(1, 0, 3, 2, 4).reshape(B, S, H, Dv)


def hybrid_mixer(h, positions, w_in, conv_w, q_norm, k_norm, a_log, dt_bias, delta_out_norm, w_out):
    B, S, _ = h.shape
    proj = h @ w_in
    cuts = [3 * ATTN_WIDTH, 3 * ATTN_WIDTH + 3 * DELTA_WIDTH,
            3 * ATTN_WIDTH + 4 * DELTA_WIDTH, 3 * ATTN_WIDTH + 4 * DELTA_WIDTH + N_DELTA_HEADS]
    attn_qkv, delta_qkv, z, a_in, b_in = jnp.split(proj, cuts, axis=-1)

    qa, ka, va = (t.reshape(B, S, N_ATTN_HEADS, HEAD_DIM) for t in jnp.split(attn_qkv, 3, axis=-1))
    qa = partial_rope(rms_norm(qa, q_norm), positions)
    ka = partial_rope(rms_norm(ka, k_norm), positions)
    oa = dilated_attention(qa, ka, va)

    dqkv = causal_depthwise_conv(delta_qkv, conv_w)
    qd, kd, vd = (t.reshape(B, S, N_DELTA_HEADS, HEAD_DIM) for t in jnp.split(dqkv, 3, axis=-1))
    g = -jnp.exp(a_log.astype(jnp.float32)) * jax.nn.softplus(
        a_in.astype(jnp.float32) + dt_bias.astype(jnp.float32))
    beta = jax.nn.sigmoid(b_in.astype(jnp.float32))
    od = gated_delta_rule(l2_norm(qd), l2_norm(kd), vd, g, beta).astype(h.dtype)
    od = rms_norm(od, delta_out_norm) * jax.nn.silu(z.reshape(B, S, N_DELTA_HEADS, HEAD_DIM))

    o = jnp.concatenate([oa.reshape(B, S, ATTN_WIDTH), od.reshape(B, S, DELTA_WIDTH)], axis=-1)
    return o @ w_out


def setup_inputs(seed: int = 0) -> dict:
    key = jax.random.key(seed)
    ks = jax.random.split(key, 24)
    f32 = jnp.float32
    D = D_MODEL

    def nrm(k, shape, scale):
        return jax.random.normal(k, shape, f32) * scale

    def gain(k, shape):
        return 1.0 + 0.02 * jax.random.normal(k, shape, f32)

    start = jax.random.randint(ks[2], (BATCH, 1), 0, 1024, dtype=jnp.int32)
    positions = start + jnp.arange(SEQ, dtype=jnp.int32)[None, :]
    dt = jnp.exp(jax.random.uniform(ks[14], (DEPTH, N_DELTA_HEADS), f32,
                                    math.log(1e-3), math.log(1e-1)))
    return {
        "x": nrm(ks[0], (BATCH, SEQ, D), 1.0),
        "c": nrm(ks[1], (BATCH, D), 1.0),
        "positions": positions,
        "w_ada": nrm(ks[3], (DEPTH, D, N_MOD * D), 0.5 * D ** -0.5),
        "b_ada": nrm(ks[4], (DEPTH, N_MOD * D), 0.01),
        "ffn1_norm": gain(ks[5], (DEPTH, D)),
        "ffn1_w_gate": nrm(ks[6], (DEPTH, D, D_FF), D ** -0.5),
        "ffn1_w_up": nrm(ks[7], (DEPTH, D, D_FF), D ** -0.5),
        "ffn1_w_down": nrm(ks[8], (DEPTH, D_FF, D), D_FF ** -0.5),
        "mix_norm": gain(ks[9], (DEPTH, D)),
        "w_in": nrm(ks[10], (DEPTH, D, IN_PROJ_WIDTH), D ** -0.5),
        "conv_w": nrm(ks[11], (DEPTH, CONV_WIDTH, 3 * DELTA_WIDTH), CONV_WIDTH ** -0.5),
        "q_norm": gain(ks[12], (DEPTH, HEAD_DIM)),
        "k_norm": gain(ks[13], (DEPTH, HEAD_DIM)),
        "a_log": jnp.log(jax.random.uniform(ks[15], (DEPTH, N_DELTA_HEADS), f32, 1.0, 16.0)),
        "dt_bias": dt + jnp.log(-jnp.expm1(-dt)),
        "delta_out_norm": gain(ks[16], (DEPTH, HEAD_DIM)),
        "w_out": nrm(ks[17], (DEPTH, MIX_WIDTH, D), MIX_WIDTH ** -0.5),
        "ffn2_norm": gain(ks[18], (DEPTH, D)),
        "ffn2_w_gate": nrm(ks[19], (DEPTH, D, D_FF), D ** -0.5),
        "ffn2_w_up": nrm(ks[20], (DEPTH, D, D_FF), D ** -0.5),
        "ffn2_w_down": nrm(ks[21], (DEPTH, D_FF, D), D_FF ** -0.5),
    }


def reference(x, c, positions, w_ada, b_ada, ffn1_norm, ffn1_w_gate, ffn1_w_up, ffn1_w_down,
              mix_norm, w_in, conv_w, q_norm, k_norm, a_log, dt_bias, delta_out_norm, w_out,
              ffn2_norm, ffn2_w_gate, ffn2_w_up, ffn2_w_down):
    c_act = jax.nn.silu(c)
    for l in range(DEPTH):
        mod = c_act @ w_ada[l] + b_ada[l]
        (sh1, sc1, gt1, sh2, sc2, gt2, sh3, sc3, gt3) = jnp.split(mod, N_MOD, axis=-1)
        h = modulate(x, ffn1_norm[l], sh1, sc1)
        x = x + 0.5 * gt1[:, None, :] * swiglu(h, ffn1_w_gate[l], ffn1_w_up[l], ffn1_w_down[l])
        h = modulate(x, mix_norm[l], sh2, sc2)
        x = x + gt2[:, None, :] * hybrid_mixer(h, positions, w_in[l], conv_w[l], q_norm[l], k_norm[l],
                                                a_log[l], dt_bias[l], delta_out_norm[l], w_out[l])
        h = modulate(x, ffn2_norm[l], sh3, sc3)
        x = x + 0.5 * gt3[:, None, :] * swiglu(h, ffn2_w_gate[l], ffn2_w_up[l], ffn2_w_down[l])
    return x
```

```python
import contextlib
import bisect
import numpy as np
import ml_dtypes
import concourse.bass as bass
import concourse.mybir as mybir
from concourse.bass_utils import run_bass_kernel_spmd

F32 = mybir.dt.float32
BF16 = mybir.dt.bfloat16
I32 = mybir.dt.int32
AF = mybir.ActivationFunctionType
ALU = mybir.AluOpType

D = 2048
KC = 16
T = 1024
S = 4096
DFF = 5632
EPS = 1e-6
NCORES = 8
GROUPS4 = [[0, 1, 2, 3], [4, 5, 6, 7]]
GROUPS8 = [[0, 1, 2, 3, 4, 5, 6, 7]]


class Prog:
    def __init__(self, nc, stack):
        self.nc = nc
        self.stack = stack
        self.engs = {"pe": nc.tensor, "act": nc.scalar, "dve": nc.vector, "pool": nc.gpsimd, "sp": nc.sync}
        self.sems = {}
        self.cnt = {}
        self.waited = {}
        self.last_w = {}
        self.readers = {}
        self.pe_sig_idx = []
        self.pe_sig_val = []
        self.n_pe = 0
        self.nops = 0
        self.nwaits = 0

    def sem(self, name):
        if name not in self.sems:
            self.sems[name] = self.stack.enter_context(self.nc.semaphore(name))
            self.cnt[name] = 0
        return self.sems[name]

    def _need(self, rec, need):
        kind = rec[0]
        if kind == "dma":
            s = rec[1]
            v = self.cnt[s]
        elif kind == "pe":
            s = "c_pe"
            i = bisect.bisect_left(self.pe_sig_idx, rec[1])
            if i >= len(self.pe_sig_idx):
                ins = self.nc.tensor.nop()
                self.cnt[s] = self.cnt.get(s, 0) + 1
                ins.then_inc(self.sem(s), 1)
                self.pe_sig_idx.append(self.n_pe)
                self.pe_sig_val.append(self.cnt[s])
                self.n_pe += 1
                i = len(self.pe_sig_idx) - 1
            v = self.pe_sig_val[i]
        else:
            s = "c_" + kind
            v = rec[1]
        if v > need.get(s, 0):
            need[s] = v

    def op(self, eng, fn, reads=(), writes=(), dma=False, grp=None, inc=16, sig=True):
        writes = tuple(writes) + tuple(k for k in reads if k.startswith("ps") and k not in writes)
        need = {}
        seen = set()
        for k in reads:
            w = self.last_w.get(k)
            if w is not None and id(w) not in seen:
                seen.add(id(w))
                if not (w[0] == "pe" and eng == "pe" and not dma):
                    self._need(w, need)
        for k in writes:
            w = self.last_w.get(k)
            if w is not None and id(w) not in seen:
                seen.add(id(w))
                if not (w[0] == "pe" and eng == "pe" and not dma):
                    self._need(w, need)
            for r in self.readers.get(k, ()):
                if id(r) not in seen:
                    seen.add(id(r))
                    if not (r[0] == "pe" and eng == "pe" and not dma):
                        self._need(r, need)
        e = self.engs[eng]
        for s, v in need.items():
            if self.waited.get((eng, s), 0) >= v:
                continue
            self.waited[(eng, s)] = v
            e.wait_ge(self.sem(s), v)
            self.nwaits += 1
        ins = fn()
        self.nops += 1
        if dma:
            if grp is None:
                grp = "d_" + str(writes[0])
            self.sem(grp)
            self.cnt[grp] += inc
            ins.then_inc(self.sems[grp], inc)
            rec = ("dma", grp)
        elif eng == "pe":
            rec = ("pe", self.n_pe)
            if sig:
                self.sem("c_pe")
                self.cnt["c_pe"] += 1
                ins.then_inc(self.sems["c_pe"], 1)
                self.pe_sig_idx.append(self.n_pe)
                self.pe_sig_val.append(self.cnt["c_pe"])
            self.n_pe += 1
        else:
            s = "c_" + eng
            self.sem(s)
            self.cnt[s] += 1
            ins.then_inc(self.sems[s], 1)
            rec = (eng, self.cnt[s])
        for k in writes:
            self.last_w[k] = rec
            self.readers[k] = []
        for k in reads:
            if k not in writes:
                self.readers.setdefault(k, []).append(rec)
        return ins

    def dma(self, eng, out, in_, reads, writes, grp=None):
        e = self.engs[eng]
        return self.op(eng, lambda: e.dma_start(out=out, in_=in_), reads, writes, dma=True, grp=grp)

    def barrier(self):
        if self.n_pe and (not self.pe_sig_idx or self.pe_sig_idx[-1] != self.n_pe - 1):
            ins = self.nc.tensor.nop()
            self.sem("c_pe")
            self.cnt["c_pe"] += 1
            ins.then_inc(self.sems["c_pe"], 1)
            self.pe_sig_idx.append(self.n_pe)
            self.pe_sig_val.append(self.cnt["c_pe"])
            self.n_pe += 1
        for en, e in self.engs.items():
            for s, v in self.cnt.items():
                if v > 0 and self.waited.get((en, s), 0) < v:
                    self.waited[(en, s)] = v
                    e.wait_ge(self.sems[s], v)
        self.last_w = {}
        self.readers = {}

    def finish(self):
        self.barrier()


class Ctx:
    pass


def build(stage=99, dff=DFF, debug=()):
    nc = bass.Bass("TRN2", target_bir_lowering=False)
    NG = dff // 256

    def din(name, shape, dt=F32):
        return nc.dram_tensor(name, list(shape), dt, kind="ExternalInput").ap()

    def dout(name, shape, dt=F32):
        return nc.dram_tensor(name, list(shape), dt, kind="ExternalOutput").ap()

    def dint(name, shape, dt=F32):
        return nc.dram_tensor(name, list(shape), dt, kind="Internal").ap()

    xT_d = din("xT", [D, T])
    cT_d = din("cT", [128, KC, 2])
    sel_d = din("sel", [128, 2])
    wada_d = din("wada", [D, 4608])
    bada_d = din("bada", [128, 36])
    gains_d = din("gains", [128, 3, KC])
    w1g_d = din("w1g", [D, dff]); w1u_d = din("w1u", [D, dff]); w1d_d = din("w1d", [dff, D])
    w2g_d = din("w2g", [D, dff]); w2u_d = din("w2u", [D, dff]); w2d_d = din("w2d", [dff, D])
    consts_d = din("cf32", [128, 6, 128])
    outT_d = dout("outT", [D, T])
    dbg = {}

    with contextlib.ExitStack() as top:
        P = Prog(nc, top)
        E = top.enter_context

        def sb(st, name, shape, dt=F32):
            return st.enter_context(nc.sbuf_tensor("s_" + name, list(shape), dt))

        ps = [E(nc.psum_tensor(f"ps{i}", [128, 512], F32)) for i in range(8)]

        modsel = sb(top, "modsel", [128, 144])
        gains = sb(top, "gains_t", [128, 3, KC])
        cols = sb(top, "cols", [128, 8, KC])
        cf = sb(top, "cf", [128, 6, 128])
        epsc = sb(top, "epsc", [128, 4])
        xst = contextlib.ExitStack()
        X = {"x": sb(xst, "x", [128, KC, T])}
        x = X["x"]
        ident = cf[:, 0, :]
        ones = cf[:, 1, :]

        P.dma("sp", x[:], xT_d.rearrange("(kc p) t -> p kc t", p=128), [], ["x"])
        P.dma("sp", cf[:], consts_d, [], ["cf"])
        P.dma("sp", gains[:], gains_d, [], ["gains"])
        P.op("dve", lambda: nc.vector.memset(epsc[:, 0:1], EPS), [], ["epsc"])
        P.op("dve", lambda: nc.vector.memset(epsc[:, 1:2], float(np.pi / 2)), [], ["epsc"])
        P.op("dve", lambda: nc.vector.memset(epsc[:, 2:3], 1.0), [], ["epsc"])
        P.op("dve", lambda: nc.vector.memset(epsc[:, 3:4], 0.0), [], ["epsc"])

        mod_in = dint("mod_in", [128, 128])
        mod_out = dint("mod_out", [4 * 128, 128])
        with contextlib.ExitStack() as st:
            cT = sb(st, "cT", [128, KC, 2])
            cact = sb(st, "cact", [128, KC, 2])
            sel = sb(st, "sel", [128, 2])
            bada = sb(st, "bada", [128, 36])
            wa = [sb(st, f"wa{i}", [128, 4608]) for i in range(4)]
            modp = sb(st, "modp", [128, 36, 2])
            modall = sb(st, "modall", [128, 4, 36, 2])
            tmpm = sb(st, "tmpm", [128, 144])
            P.dma("sp", cT[:], cT_d, [], ["cT"])
            P.dma("sp", sel[:], sel_d, [], ["sel"])
            P.dma("sp", bada[:], bada_d, [], ["bada"])
            P.op("act", lambda: nc.scalar.activation(out=cact[:], in_=cT[:], func=AF.Silu), ["cT"], ["cact"])
            for kc in range(KC):
                w = wa[kc % 4]
                P.dma("sp", w[:], wada_d[kc * 128:(kc + 1) * 128, :], [], [f"wa{kc % 4}"])
                for j in range(36):
                    P.op("pe", (lambda w=w, j=j, kc=kc: nc.tensor.matmul(
                        ps[0][:, 2 * j:2 * j + 2], lhsT=w[:, j * 128:(j + 1) * 128], rhs=cact[:, kc, :],
                        start=(kc == 0 and j == 0), stop=(kc == KC - 1), skip_group_check=True)),
                        [f"wa{kc % 4}", "cact"], ["ps0"], sig=(j == 35))
            P.op("dve", lambda: nc.vector.tensor_tensor(
                out=modp[:], in0=ps[0][:, 0:72].rearrange("p (j b) -> p j b", b=2),
                in1=bada[:].unsqueeze(2).to_broadcast([128, 36, 2]), op=ALU.add), ["ps0", "bada"], ["modp"])
            P.dma("sp", mod_in[:, 0:72], modp[:].rearrange("p j b -> p (j b)"), ["modp"], ["mod_in"])
            if "nocc" in debug:
                for r in range(4):
                    P.dma("sp", mod_out[r * 128:(r + 1) * 128, :], mod_in, ["mod_in"], ["mod_out"])
            else:
                P.op("pool", lambda: nc.gpsimd.collective_compute(
                    "AllGather", ALU.bypass, replica_groups=GROUPS4, ins=[mod_in], outs=[mod_out]),
                    ["mod_in"], ["mod_out"], dma=True, grp="cc_mod", inc=1)
            P.dma("sp", modall[:].rearrange("p r j b -> p r (j b)"),
                  mod_out.rearrange("(r p) f -> p r f", p=128)[:, :, 0:72], ["mod_out"], ["modall"])
            P.op("dve", lambda: nc.vector.tensor_scalar(
                out=tmpm[:].rearrange("p (r j) -> p r j", j=36), in0=modall[:, :, :, 0], scalar1=sel[:, 0:1], scalar2=None,
                op0=ALU.mult), ["modall", "sel"], ["tmpm"])
            P.op("dve", lambda: nc.vector.scalar_tensor_tensor(
                out=modsel[:].rearrange("p (r j) -> p r j", j=36), in0=modall[:, :, :, 1], scalar=sel[:, 1:2],
                in1=tmpm[:].rearrange("p (r j) -> p r j", j=36), op0=ALU.mult, op1=ALU.add),
                ["modall", "sel", "tmpm"], ["modsel"])
            for k in range(3):
                sc = modsel[:, (3 * k + 1) * 16:(3 * k + 1) * 16 + 16]
                gt = modsel[:, (3 * k + 2) * 16:(3 * k + 2) * 16 + 16]
                P.op("dve", (lambda k=k, sc=sc: nc.vector.scalar_tensor_tensor(
                    out=cols[:, 2 * k, :], in0=sc, scalar=1.0, in1=gains[:, k, :],
                    op0=ALU.add, op1=ALU.mult)), ["modsel", "gains"], ["cols"])
                P.op("dve", (lambda k=k, gt=gt: nc.vector.tensor_scalar(
                    out=cols[:, 2 * k + 1, :], in0=gt,
                    scalar1=(1.0 if k == 1 else 0.5), scalar2=None, op0=ALU.mult)), ["modsel"], ["cols"])
            if "modsel" in debug:
                dbg["modsel"] = dout("dbg_modsel", [128, 144])
                P.dma("sp", dbg["modsel"], modsel[:], ["modsel"], ["dbg_modsel"])
            P.barrier()

        if stage <= 0:
            P.dma("sp", outT_d.rearrange("(kc p) t -> p kc t", p=128), x[:], ["x"], ["outT"])
            P.finish()
            return nc

        def scol(k):
            return cols[:, 2 * k, :]

        def gcol(k):
            return cols[:, 2 * k + 1, :]

        def shcol(k):
            return modsel[:, (3 * k) * 16:(3 * k) * 16 + 16]

        def modulate(st, k, h):
            x = X["x"]
            sq = [sb(st, f"sq{k}_{i}", [128, T]) for i in range(2)]
            ssum = sb(st, f"ssum{k}", [128, T])
            rstd = sb(st, f"rstd{k}", [128, T])
            for kc in range(KC):
                q = sq[kc % 2]
                if kc == 0:
                    P.op("act", lambda: nc.scalar.activation(out=ssum[:], in_=x[:, 0, :], func=AF.Square), ["x"], ["ssum"])
                else:
                    P.op("act", (lambda q=q, kc=kc: nc.scalar.activation(out=q[:], in_=x[:, kc, :], func=AF.Square)),
                         ["x"], [f"sq{kc % 2}"])
                    P.op("dve", (lambda q=q: nc.vector.tensor_tensor(out=ssum[:], in0=ssum[:], in1=q[:], op=ALU.add)),
                         [f"sq{kc % 2}", "ssum"], ["ssum"])
            for hf in range(2):
                P.op("pe", (lambda hf=hf: nc.tensor.matmul(ps[hf][:], lhsT=ones, rhs=ssum[:, hf * 512:(hf + 1) * 512],
                                                           start=True, stop=True)), ["ssum", "cf"], [f"ps{hf}"])
                P.op("act", (lambda hf=hf: nc.scalar.activation(out=rstd[:, hf * 512:(hf + 1) * 512], in_=ps[hf][:], func=AF.Sqrt,
                                                                bias=epsc[:, 0:1], scale=1.0 / D)), [f"ps{hf}", "epsc"], [f"rstd{hf}"])
                P.op("dve", (lambda hf=hf: nc.vector.reciprocal(out=rstd[:, hf * 512:(hf + 1) * 512], in_=rstd[:, hf * 512:(hf + 1) * 512])),
                     [f"rstd{hf}"], [f"rstd{hf}"])
            for kc in range(KC):
                q = sq[kc % 2]
                P.op("dve", (lambda q=q, kc=kc: nc.vector.scalar_tensor_tensor(
                    out=q[:], in0=x[:, kc, :], scalar=scol(k)[:, kc:kc + 1], in1=rstd[:], op0=ALU.mult, op1=ALU.mult)),
                    ["x", "cols", "rstd0", "rstd1"], [f"sq{kc % 2}"])
                P.op("act", (lambda q=q, kc=kc: nc.scalar.activation(out=h[:, kc, :], in_=q[:], func=AF.Identity,
                                                                     bias=shcol(k)[:, kc:kc + 1], scale=1.0)),
                     [f"sq{kc % 2}", "modsel"], [f"h_{kc}"])

        def ffn(st, k, h, Wg, Wu, Wd):
            x = X["x"]
            wg = [sb(st, f"wg{k}_{i}", [128, KC, 256], BF16) for i in range(2)]
            wu = [sb(st, f"wu{k}_{i}", [128, KC, 256], BF16) for i in range(2)]
            wd = [sb(st, f"wd{k}_{i}", [128, 2, D], BF16) for i in range(2)]
            aT = [sb(st, f"aT{k}_{i}", [128, 2, T], BF16) for i in range(2)]
            sg = [sb(st, f"sg{k}_{i}", [128, 512]) for i in range(2)]
            hkeys = [f"h_{kc}" for kc in range(KC)]

            def load_gu(g):
                s = g % 2
                P.dma("pool", wg[s][:], Wg[:, g * 256:(g + 1) * 256].rearrange("(kc p) f -> p kc f", p=128), [], [f"wg{s}"])
                P.dma("pool", wu[s][:], Wu[:, g * 256:(g + 1) * 256].rearrange("(kc p) f -> p kc f", p=128), [], [f"wu{s}"])

            def load_d(g):
                s = g % 2
                P.dma("pool", wd[s][:], Wd[g * 256:(g + 1) * 256, :].rearrange("(fc p) d -> p fc d", p=128), [], [f"wd{s}"])

            def gateup(g):
                s = g % 2
                for fc in range(2):
                    for hf in range(2):
                        bG, bU = 2 * hf, 2 * hf + 1
                        tsl = slice(hf * 512, (hf + 1) * 512)
                        for kc in range(KC):
                            P.op("pe", (lambda kc=kc, fc=fc, tsl=tsl, bG=bG: nc.tensor.matmul(
                                ps[bG][:], lhsT=wg[s][:, kc, fc * 128:(fc + 1) * 128], rhs=h[:, kc, tsl],
                                start=(kc == 0), stop=(kc == KC - 1))), [f"wg{s}", hkeys[kc]], [f"ps{bG}"], sig=(kc == KC - 1))
                        for kc in range(KC):
                            P.op("pe", (lambda kc=kc, fc=fc, tsl=tsl, bU=bU: nc.tensor.matmul(
                                ps[bU][:], lhsT=wu[s][:, kc, fc * 128:(fc + 1) * 128], rhs=h[:, kc, tsl],
                                start=(kc == 0), stop=(kc == KC - 1))), [f"wu{s}", hkeys[kc]], [f"ps{bU}"], sig=(kc == KC - 1))
                        P.op("act", (lambda hf=hf, bG=bG: nc.scalar.activation(out=sg[hf][:], in_=ps[bG][:], func=AF.Silu)),
                             [f"ps{bG}"], [f"sg{hf}"])
                        P.op("dve", (lambda hf=hf, bU=bU, fc=fc, tsl=tsl: nc.vector.tensor_tensor(
                            out=aT[s][:, fc, tsl], in0=sg[hf][:], in1=ps[bU][:], op=ALU.mult)),
                            [f"sg{hf}", f"ps{bU}"], [f"aT{s}_{fc}_{hf}"])

            dcount = [0]

            def down(g):
                s = g % 2
                for dc in range(KC):
                    for hf in range(2):
                        bk = 4 + dcount[0] % 4
                        dcount[0] += 1
                        tsl = slice(hf * 512, (hf + 1) * 512)
                        for fc in range(2):
                            P.op("pe", (lambda fc=fc, dc=dc, tsl=tsl, bk=bk: nc.tensor.matmul(
                                ps[bk][:], lhsT=wd[s][:, fc, dc * 128:(dc + 1) * 128], rhs=aT[s][:, fc, tsl],
                                start=(fc == 0), stop=(fc == 1))), [f"wd{s}", f"aT{s}_{fc}_{hf}"], [f"ps{bk}"], sig=(fc == 1))
                        P.op("dve", (lambda dc=dc, tsl=tsl, bk=bk: nc.vector.scalar_tensor_tensor(
                            out=x[:, dc, tsl], in0=ps[bk][:], scalar=gcol(k)[:, dc:dc + 1], in1=x[:, dc, tsl],
                            op0=ALU.mult, op1=ALU.add)), [f"ps{bk}", "cols", "x"], ["x"])

            load_gu(0); load_d(0)
            if NG > 1:
                load_gu(1); load_d(1)
            gateup(0)
            for g in range(NG):
                if g + 1 < NG:
                    gateup(g + 1)
                if g + 2 < NG:
                    load_gu(g + 2)
                down(g)
                if g + 2 < NG:
                    load_d(g + 2)

        with contextlib.ExitStack() as st:
            h = sb(st, "h1", [128, KC, T], BF16)
            modulate(st, 0, h)
            if "h1" in debug:
                dbg["h1"] = dout("dbg_h1", [128, KC, T], BF16)
                P.dma("sp", dbg["h1"], h[:], [f"h_{kc}" for kc in range(KC)], ["dbg_h1"])
            ffn(st, 0, h, w1g_d, w1u_d, w1d_d)
            P.barrier()

        if stage <= 1:
            P.dma("sp", outT_d.rearrange("(kc p) t -> p kc t", p=128), x[:], ["x"], ["outT"])
            P.finish()
            return nc

        ex1_in = [dint(f"ex1_in{p}", [512, T], BF16) for p in range(4)]
        ex1_out = [dint(f"ex1_out{p}", [4 * 512, T], BF16) for p in range(4)]
        ex2_in = [dint(f"ex2_in{q}", [512, T], BF16) for q in range(4)]
        ex2_out = dint("ex2_out", [4 * 2048, T], BF16)
        xspill = dint("xspill", [128, KC, T])
        with contextlib.ExitStack() as st:
            h2 = sb(st, "h2", [128, KC, T], BF16)
            modulate(st, 1, h2)
            for p in range(4):
                P.dma("sp", ex1_in[p].rearrange("(kc p) t -> p kc t", p=128), h2[:, 4 * p:4 * p + 4, :],
                      [f"h_{kc}" for kc in range(4 * p, 4 * p + 4)], [f"ex1in{p}"])
                P.op("pool", (lambda p=p: nc.gpsimd.collective_compute(
                    "AllGather", ALU.bypass, replica_groups=GROUPS4, ins=[ex1_in[p]], outs=[ex1_out[p]])),
                    [f"ex1in{p}"], [f"ex1out{p}"], dma=True, grp=f"cc_ex1_{p}", inc=1)
            P.dma("sp", xspill, X["x"][:], ["x"], ["xspill"])
            P.barrier()
        xst.close()

        def out_from_spill():
            with contextlib.ExitStack() as st:
                xt = sb(st, "xtmp", [128, 4, T])
                ov = outT_d.rearrange("(kc p) t -> p kc t", p=128)
                for i4 in range(4):
                    P.dma("sp", xt[:], xspill[:, 4 * i4:4 * i4 + 4, :], ["xspill"], ["xtmp"])
                    P.dma("sp", ov[:, 4 * i4:4 * i4 + 4, :], xt[:], ["xtmp"], ["outT"])
                P.finish()

        if "stopC" in debug:
            out_from_spill()
            return nc

        pos_d = din("pos", [1, S], I32)
        wA_d = din("wA", [2, D, 384])
        wD_d = din("wD", [2, D, 512])
        wab_d = din("wab", [D, 4])
        qkg_d = din("qkg", [128, 2])
        convc_d = din("convc", [128, 2, 3, 4])
        abp_d = din("abp", [4, 2])
        don_d = din("don", [128, 1])
        invf_d = din("invf", [128, 1])
        amask_d = din("amask", [128, 256], BF16)
        idx2_d = din("idx2", [128, 16], I32)
        wo_d = din("wo", [D, D])
        ISQ = float(128 ** -0.5)
        TWO_PI = float(2 * np.pi)
        C1 = 6.28125
        C2 = float(2 * np.pi - 6.28125)

        with contextlib.ExitStack() as mst:
            identb = sb(mst, "identb", [128, 128], BF16)
            onesb = sb(mst, "onesb", [128, 128], BF16)
            amask = sb(mst, "amask", [128, 256], BF16)
            qkg = sb(mst, "qkg", [128, 2])
            convc = sb(mst, "convc", [128, 2, 3, 4])
            abp = sb(mst, "abp", [4, 2])
            don = sb(mst, "don", [128, 1])
            invf = sb(mst, "invf", [128, 1])
            hb = [sb(mst, f"hb{i}", [128, KC, 512], BF16) for i in range(2)]
            ostg = sb(mst, "ostg", [128, T], BF16)
            P.op("act", lambda: nc.scalar.copy(out=identb[:], in_=ident), ["cf"], ["identb"])
            P.op("act", lambda: nc.scalar.copy(out=onesb[:], in_=ones), ["cf"], ["onesb"])
            P.dma("sp", amask[:], amask_d, [], ["amask"])
            P.dma("sp", qkg[:], qkg_d, [], ["qkg"])
            P.dma("sp", convc[:], convc_d, [], ["convc"])
            P.dma("sp", abp[:], abp_d, [], ["abp"])
            P.dma("sp", don[:], don_d, [], ["don"])
            P.dma("sp", invf[:], invf_d, [], ["invf"])

            def load_hb(blk):
                t = hb[blk % 2]
                q, hs = blk // 2, (blk % 2) * 512
                for p in range(4):
                    P.dma("sp", t[:, 4 * p:4 * p + 4, :],
                          ex1_out[p][q * 512:(q + 1) * 512, hs:hs + 512].rearrange("(kc p) t -> p kc t", p=128),
                          [f"ex1out{p}"], [f"hb{blk % 2}"])
                return t

            def inproj(bank, w, c0, nco, hbt, hkey, wkey):
                for kc in range(KC):
                    P.op("pe", (lambda kc=kc: nc.tensor.matmul(ps[bank][0:nco, :], lhsT=w[:, kc, c0:c0 + nco], rhs=hbt[:, kc, :],
                                                               start=(kc == 0), stop=(kc == KC - 1))),
                         [wkey, hkey], [f"ps{bank}"], sig=(kc == KC - 1))

            with contextlib.ExitStack() as rst:
                cosT = sb(rst, "cosT", [128, S])
                sinT = sb(rst, "sinT", [128, S])
                with contextlib.ExitStack() as st:
                    posi = sb(st, "posi", [128, T], I32)
                    ang = sb(st, "ang", [128, T])
                    kf_ = sb(st, "kf_", [128, T])
                    ki = sb(st, "ki", [128, T], I32)
                    for qb in range(4):
                        cs = slice(qb * T, (qb + 1) * T)
                        P.dma("sp", posi[:], pos_d[0:1, cs].partition_broadcast(128) if False else pos_d[0:1, cs].to_broadcast([128, T]), [], ["posi"])
                        P.op("dve", lambda: nc.vector.tensor_copy(out=ang[:], in_=posi[:]), ["posi"], ["ang"])
                        P.op("dve", lambda: nc.vector.tensor_scalar(out=ang[:], in0=ang[:], scalar1=invf[:, 0:1], scalar2=None, op0=ALU.mult),
                             ["ang", "invf"], ["ang"])
                        P.op("dve", lambda: nc.vector.tensor_scalar(out=ki[:], in0=ang[:], scalar1=1.0 / TWO_PI, scalar2=None, op0=ALU.mult),
                             ["ang"], ["ki"])
                        P.op("dve", lambda: nc.vector.tensor_copy(out=kf_[:], in_=ki[:]), ["ki"], ["kf_"])
                        P.op("dve", lambda: nc.vector.scalar_tensor_tensor(out=ang[:], in0=kf_[:], scalar=-C1, in1=ang[:], op0=ALU.mult, op1=ALU.add),
                             ["kf_", "ang"], ["ang"])
                        P.op("dve", lambda: nc.vector.scalar_tensor_tensor(out=ang[:], in0=kf_[:], scalar=-C2, in1=ang[:], op0=ALU.mult, op1=ALU.add),
                             ["kf_", "ang"], ["ang"])
                        P.op("dve", lambda: nc.vector.tensor_scalar(out=ang[:], in0=ang[:], scalar1=float(-np.pi), scalar2=float(np.pi),
                                                                    op0=ALU.max, op1=ALU.min), ["ang"], ["ang"])
                        P.op("act", (lambda cs=cs: nc.scalar.activation(out=sinT[:, cs], in_=ang[:], func=AF.Sin)), ["ang"], ["sinT"])
                        P.op("act", lambda: nc.scalar.activation(out=kf_[:], in_=ang[:], func=AF.Abs), ["ang"], ["kf_"])
                        P.op("act", (lambda cs=cs: nc.scalar.activation(out=cosT[:, cs], in_=kf_[:], func=AF.Sin, scale=-1.0, bias=epsc[:, 1:2])),
                             ["kf_", "epsc"], ["cosT"])
                    P.barrier()
                if "stopRope" in debug:
                    dbg["cos"] = dout("dbg_cos", [128, S]); dbg["sin"] = dout("dbg_sin", [128, S])
                    P.dma("sp", dbg["cos"], cosT[:], ["cosT"], ["dbg_cos"]); P.dma("sp", dbg["sin"], sinT[:], ["sinT"], ["dbg_sin"])
                    out_from_spill()
                    return nc

                for hl in range(2):
                    with contextlib.ExitStack() as st:
                        wA = sb(st, f"wA{hl}", [128, KC, 384], BF16)
                        qT = sb(st, f"qT{hl}", [128, S], BF16)
                        kT = sb(st, f"kT{hl}", [128, S], BF16)
                        vT = sb(st, f"vT{hl}", [128, S], BF16)
                        sqt = sb(st, f"sqt{hl}", [128, 512])
                        rst_ = sb(st, f"rst{hl}", [128, 512])
                        qn = sb(st, f"qn{hl}", [128, 512])
                        t1 = sb(st, f"t1{hl}", [128, 512])
                        t2 = sb(st, f"t2{hl}", [128, 512])
                        vd = sb(st, f"vd{hl}", [128, 32, 128], BF16)
                        accn = sb(st, f"accn{hl}", [128, S])
                        accd = sb(st, f"accd{hl}", [128, S])
                        pt = [sb(st, f"pt{hl}_{i}", [128, 256], BF16) for i in range(4)]
                        P.dma("pool", wA[:], wA_d[hl].rearrange("(kc p) f -> p kc f", p=128), [], ["wA"])
                        for blk in range(8):
                            hbt = load_hb(blk)
                            cs = slice(blk * 512, (blk + 1) * 512)
                            for t in range(3):
                                bank = t % 2
                                inproj(bank, wA, t * 128, 128, hbt, f"hb{blk % 2}", "wA")
                                if t == 2:
                                    P.op("act", (lambda cs=cs, bank=bank: nc.scalar.copy(out=vT[:, cs], in_=ps[bank][:])), [f"ps{bank}"], ["vT"])
                                    continue
                                dst = qT if t == 0 else kT
                                P.op("act", (lambda bank=bank: nc.scalar.activation(out=sqt[:], in_=ps[bank][:], func=AF.Square)), [f"ps{bank}"], ["sqt"])
                                P.op("pe", lambda: nc.tensor.matmul(ps[2][:], lhsT=ones, rhs=sqt[:], start=True, stop=True), ["sqt", "cf"], ["ps2"])
                                P.op("act", lambda: nc.scalar.activation(out=rst_[:], in_=ps[2][:], func=AF.Sqrt, bias=epsc[:, 0:1], scale=1.0 / 128),
                                     ["ps2", "epsc"], ["rst"])
                                P.op("dve", lambda: nc.vector.reciprocal(out=rst_[:], in_=rst_[:]), ["rst"], ["rst"])
                                P.op("dve", (lambda bank=bank, t=t: nc.vector.scalar_tensor_tensor(
                                    out=qn[:], in0=ps[bank][:], scalar=qkg[:, t:t + 1], in1=rst_[:], op0=ALU.mult, op1=ALU.mult)),
                                    [f"ps{bank}", "qkg", "rst"], ["qn"])
                                P.op("pe", lambda: nc.tensor.matmul(ps[3][:], lhsT=cf[:, 2, :], rhs=qn[:], start=True, stop=True), ["qn", "cf"], ["ps3"])
                                P.op("dve", (lambda cs=cs: nc.vector.tensor_tensor(out=t1[:], in0=qn[:], in1=cosT[:, cs], op=ALU.mult)), ["qn", "cosT"], ["t1"])
                                P.op("dve", (lambda cs=cs: nc.vector.tensor_tensor(out=t2[:], in0=ps[3][:], in1=sinT[:, cs], op=ALU.mult)), ["ps3", "sinT"], ["t2"])
                                P.op("dve", (lambda cs=cs, dst=dst: nc.vector.tensor_tensor(out=dst[:, cs], in0=t1[:], in1=t2[:], op=ALU.add)),
                                     ["t1", "t2"], ["qT" if t == 0 else "kT"])
                        if f"qk{hl}" in debug:
                            dbg["qT"] = dout("dbg_qT", [128, S], BF16); dbg["kT"] = dout("dbg_kT", [128, S], BF16)
                            P.dma("sp", dbg["qT"], qT[:], ["qT"], ["dbg_qT"]); P.dma("sp", dbg["kT"], kT[:], ["kT"], ["dbg_kT"])
                        if "stopInproj" in debug:
                            out_from_spill()
                            return nc
                        kcount = 0
                        for d in ((1,) if "d1only" in debug else (1, 4, 16)):
                            nbk = 32 // d
                            for bi0 in range(0, 32, 4):
                                for u in range(4):
                                    r, n = divmod(bi0 + u, nbk)
                                    a0 = r + d * 128 * n
                                    P.op("pe", (lambda u=u, a0=a0, d=d: nc.tensor.matmul(
                                        ps[6][:, u * 128:(u + 1) * 128], lhsT=vT[:, a0:a0 + d * 127 + 1:d], rhs=identb[:],
                                        start=True, stop=True)), ["vT", "identb"], ["ps6"], sig=(u == 3))
                                P.op("act", (lambda bi0=bi0: nc.scalar.copy(out=vd[:, bi0:bi0 + 4, :],
                                                                            in_=ps[6][:].rearrange("p (u e) -> p u e", e=128))),
                                     ["ps6"], [f"vd{bi0}"])
                            blocks = [(r, n) for r in range(d) for n in range(nbk)]

                            def emit_S(r, n, kc_):
                                a0 = r + d * 128 * n
                                nq = 256 if n + 1 < nbk else 128
                                sbank = 4 + kc_ % 2
                                ptt = pt[kc_ % 4]
                                pkey = f"pt{kc_ % 4}"
                                sap = ps[sbank][:, 0:nq]
                                skey = f"ps{sbank}"
                                P.op("pe", (lambda: nc.tensor.matmul(
                                    sap, lhsT=kT[:, a0:a0 + d * 127 + 1:d], rhs=qT[:, a0:a0 + d * (nq - 1) + 1:d],
                                    start=True, stop=False)), ["kT", "qT"], [skey], sig=False)
                                P.op("pe", (lambda: nc.tensor.matmul(
                                    sap, lhsT=identb[:], rhs=amask[:, 0:nq], start=False, stop=True)), ["identb", "amask"], [skey])
                                P.op("act", (lambda: nc.scalar.activation(out=ptt[:, 0:nq], in_=sap, func=AF.Exp, scale=ISQ)),
                                     [skey], [pkey])

                            def emit_PV(r, n, kc_):
                                a0 = r + d * 128 * n
                                bi = r * nbk + n
                                ptt = pt[kc_ % 4]
                                pkey = f"pt{kc_ % 4}"
                                nb_, db_ = n % 2, 2 + n % 2
                                P.op("pe", (lambda: nc.tensor.matmul(
                                    ps[nb_][:, 0:128], lhsT=vd[:, bi, :], rhs=ptt[:, 0:128],
                                    start=(n == 0), stop=True)), [f"vd{bi - bi % 4}", pkey], [f"ps{nb_}"])
                                P.op("pe", (lambda: nc.tensor.matmul(
                                    ps[db_][:, 0:128], lhsT=onesb[:], rhs=ptt[:, 0:128],
                                    start=(n == 0), stop=True)), ["onesb", pkey], [f"ps{db_}"])
                                if n + 1 < nbk:
                                    nb2, db2 = (n + 1) % 2, 2 + (n + 1) % 2
                                    P.op("pe", (lambda: nc.tensor.matmul(
                                        ps[nb2][:, 0:128], lhsT=vd[:, bi, :], rhs=ptt[:, 128:256],
                                        start=True, stop=False)), [f"vd{bi - bi % 4}", pkey], [f"ps{nb2}"], sig=False)
                                    P.op("pe", (lambda: nc.tensor.matmul(
                                        ps[db2][:, 0:128], lhsT=onesb[:], rhs=ptt[:, 128:256],
                                        start=True, stop=False)), ["onesb", pkey], [f"ps{db2}"], sig=False)
                                tsl = slice(a0, a0 + d * 127 + 1, d)
                                if d == 1:
                                    P.op("act", (lambda: nc.scalar.copy(out=accn[:, tsl], in_=ps[nb_][:, 0:128])),
                                         [f"ps{nb_}"], ["accn"])
                                    P.op("dve", (lambda: nc.vector.tensor_copy(out=accd[:, tsl], in_=ps[db_][:, 0:128])),
                                         [f"ps{db_}"], ["accd"])
                                else:
                                    P.op("dve", (lambda: nc.vector.tensor_tensor(
                                        out=accn[:, tsl], in0=accn[:, tsl], in1=ps[nb_][:, 0:128], op=ALU.add)),
                                        [f"ps{nb_}", "accn"], ["accn"])
                                    P.op("dve", (lambda: nc.vector.tensor_tensor(
                                        out=accd[:, tsl], in0=accd[:, tsl], in1=ps[db_][:, 0:128], op=ALU.add)),
                                        [f"ps{db_}", "accd"], ["accd"])

                            emit_S(*blocks[0], kcount)
                            for i_, (r, n) in enumerate(blocks):
                                if i_ + 1 < len(blocks):
                                    emit_S(*blocks[i_ + 1], kcount + i_ + 1)
                                emit_PV(r, n, kcount + i_)
                            kcount += len(blocks)
                        for qb in range(4):
                            cs = slice(qb * T, (qb + 1) * T)
                            P.op("dve", (lambda cs=cs: nc.vector.reciprocal(out=accd[:, cs], in_=accd[:, cs])), ["accd"], ["accd"])
                            P.op("dve", (lambda cs=cs: nc.vector.tensor_tensor(out=ostg[:], in0=accn[:, cs], in1=accd[:, cs], op=ALU.mult)),
                                 ["accn", "accd"], ["ostg"])
                            P.dma("sp", ex2_in[qb][hl * 128:(hl + 1) * 128, :], ostg[:], ["ostg"], [f"ex2in{qb}"], grp="w_ex2in")
                        P.barrier()
            if "stopAttn" in debug:
                out_from_spill()
                return nc
            with contextlib.ExitStack() as dst_:
                abrow = sb(dst_, "abrow", [4, S])
                G4 = sb(dst_, "G4", [4, S])
                wabt = sb(dst_, "wabt", [128, KC, 4], BF16)
                negexp = sb(dst_, "negexp", [4, 1])
                idt8 = sb(dst_, "idt8", [64, 8, 64])
                U64 = cf[0:64, 3, 0:64]
                pm_sl = cf[0:64, 4, 0:64]
                pm_ut = cf[0:64, 5, 0:64]
                id64 = cf[0:64, 0, 0:64]
                P.dma("pool", wabt[:], wab_d.rearrange("(kc p) f -> p kc f", p=128), [], ["wabt"])
                for i8 in range(8):
                    P.op("dve", (lambda i8=i8: nc.vector.tensor_copy(out=idt8[:, i8, :], in_=id64)), ["cf"], ["idt8"])
                for hl in range(2):
                    with contextlib.ExitStack() as st:
                        qf = sb(st, f"qf{hl}", [128, S])
                        kf = sb(st, f"kf{hl}", [128, S])
                        vf = sb(st, f"vf{hl}", [128, S])
                        sz = sb(st, f"sz{hl}", [128, S])
                        ist = contextlib.ExitStack()
                        wDt = sb(ist, f"wDt{hl}", [128, KC, 512], BF16)
                        raw = [sb(ist, f"raw{hl}_{t}", [128, 515]) for t in range(3)]
                        ctmp = sb(ist, f"ctmp{hl}", [128, 512])
                        ctm2 = sb(ist, f"ctm2{hl}", [128, 512])
                        sqd = sb(ist, f"sqd{hl}", [128, 512])
                        rsd = sb(ist, f"rsd{hl}", [128, 512])
                        P.dma("pool", wDt[:], wD_d[hl].rearrange("(kc p) f -> p kc f", p=128), [], ["wDt"])
                        for t in range(3):
                            P.op("dve", (lambda t=t: nc.vector.memset(raw[t][:, 0:3], 0.0)), [], [f"raw{t}"])
                        for blk in range(8):
                            hbt = load_hb(blk)
                            cs = slice(blk * 512, (blk + 1) * 512)
                            for t in range(4):
                                bank = t % 2
                                inproj(bank, wDt, t * 128, 128, hbt, f"hb{blk % 2}", "wDt")
                                if t == 3:
                                    P.op("act", (lambda cs=cs, bank=bank: nc.scalar.activation(out=sz[:, cs], in_=ps[bank][:], func=AF.Silu)),
                                         [f"ps{bank}"], ["sz"])
                                    continue
                                rw = raw[t]
                                cw = lambda j, t=t: convc[:, hl, t, j:j + 1]
                                P.op("act", (lambda rw=rw, bank=bank: nc.scalar.copy(out=rw[:, 3:515], in_=ps[bank][:])), [f"ps{bank}"], [f"raw{t}"])
                                P.op("act", (lambda rw=rw, cw=cw: nc.scalar.activation(out=ctmp[:], in_=rw[:, 3:515], func=AF.Copy, scale=cw(3))),
                                     [f"raw{t}", "convc"], ["ctmp"])
                                for j in (2, 1, 0):
                                    P.op("dve", (lambda rw=rw, cw=cw, j=j: nc.vector.scalar_tensor_tensor(
                                        out=ctmp[:], in0=rw[:, j:j + 512], scalar=cw(j), in1=ctmp[:], op0=ALU.mult, op1=ALU.add)),
                                        [f"raw{t}", "convc", "ctmp"], ["ctmp"])
                                P.op("act", (lambda rw=rw: nc.scalar.copy(out=rw[:, 0:3], in_=rw[:, 512:515])), [f"raw{t}"], [f"raw{t}"])
                                if t == 2:
                                    P.op("act", (lambda cs=cs: nc.scalar.activation(out=vf[:, cs], in_=ctmp[:], func=AF.Silu)), ["ctmp"], ["vf"])
                                    continue
                                dstt = qf if t == 0 else kf
                                P.op("act", lambda: nc.scalar.activation(out=ctm2[:], in_=ctmp[:], func=AF.Silu), ["ctmp"], ["ctm2"])
                                P.op("act", lambda: nc.scalar.activation(out=sqd[:], in_=ctm2[:], func=AF.Square), ["ctm2"], ["sqd"])
                                P.op("pe", lambda: nc.tensor.matmul(ps[2][:], lhsT=ones, rhs=sqd[:], start=True, stop=True), ["sqd", "cf"], ["ps2"])
                                P.op("act", lambda: nc.scalar.activation(out=rsd[:], in_=ps[2][:], func=AF.Sqrt, bias=epsc[:, 0:1], scale=1.0),
                                     ["ps2", "epsc"], ["rsd"])
                                P.op("dve", lambda: nc.vector.reciprocal(out=rsd[:], in_=rsd[:]), ["rsd"], ["rsd"])
                                P.op("dve", (lambda cs=cs, dstt=dstt: nc.vector.tensor_tensor(out=dstt[:, cs], in0=ctm2[:], in1=rsd[:], op=ALU.mult)),
                                     ["ctm2", "rsd"], ["qf" if t == 0 else "kf"])
                            if hl == 0:
                                for kc in range(KC):
                                    P.op("pe", (lambda kc=kc, hbt=hbt: nc.tensor.matmul(ps[3][0:4, :], lhsT=wabt[:, kc, 0:4], rhs=hbt[:, kc, :],
                                                                                       start=(kc == 0), stop=(kc == KC - 1))),
                                         ["wabt", f"hb{blk % 2}"], ["ps3"], sig=(kc == KC - 1))
                                P.op("act", (lambda cs=cs: nc.scalar.copy(out=abrow[:, cs], in_=ps[3][0:4, :])), ["ps3"], ["abrow"])
                        if hl == 0:
                            P.op("act", lambda: nc.scalar.activation(out=negexp[:], in_=abp[:, 0:1], func=AF.Exp), ["abp"], ["negexp"])
                            P.op("dve", lambda: nc.vector.tensor_scalar(out=negexp[:], in0=negexp[:], scalar1=-1.0, scalar2=None, op0=ALU.mult),
                                 ["negexp"], ["negexp"])
                            P.op("act", lambda: nc.scalar.activation(out=G4[:], in_=abrow[:], func=AF.Exp, bias=abp[:, 1:2], scale=1.0), ["abrow", "abp"], ["G4"])
                            P.op("act", lambda: nc.scalar.activation(out=G4[:], in_=G4[:], func=AF.Ln, bias=epsc[0:4, 2:3], scale=1.0), ["G4", "epsc"], ["G4"])
                            P.op("dve", lambda: nc.vector.tensor_scalar(out=G4[:], in0=G4[:], scalar1=negexp[:, 0:1], scalar2=None, op0=ALU.mult),
                                 ["G4", "negexp"], ["G4"])
                            P.op("act", lambda: nc.scalar.activation(out=abrow[:], in_=abrow[:], func=AF.Sigmoid), ["abrow"], ["abrow"])
                            if "gb" in debug:
                                dbg["G4"] = dout("dbg_G4", [4, S]); dbg["S4"] = dout("dbg_S4", [4, S]); S4 = abrow
                                P.dma("sp", dbg["G4"], G4[:], ["G4"], ["dbg_G4"]); P.dma("sp", dbg["S4"], abrow[:], ["abrow"], ["dbg_S4"])
                        if f"dqkv{hl}" in debug:
                            for nm, tt in (("qf", qf), ("kf", kf), ("vf", vf)):
                                dbg[nm] = dout("dbg_" + nm, [128, S])
                                P.dma("sp", dbg[nm], tt[:], [nm], ["dbg_" + nm])
                        P.barrier()
                        ist.close()
                        gb = sb(st, f"gb{hl}", [64, 128])
                        gct = sb(st, f"gct{hl}", [64, 64])
                        glb = sb(st, f"glb{hl}", [128, 64])
                        egc = sb(st, f"egc{hl}", [64, 64])
                        bgc = sb(st, f"bgc{hl}", [64, 64])
                        ekt = sb(st, f"ekt{hl}", [64, 64])
                        decb = sb(st, f"decb{hl}", [128, 64])
                        for c in range(64):
                            P.op("pe", (lambda c=c: nc.tensor.matmul(ps[0][0:64, 2 * c:2 * c + 1], lhsT=G4[0:4, c * 64:(c + 1) * 64],
                                                                     rhs=cf[0:4, 0, hl:hl + 1], start=True, stop=True)), ["G4", "cf"], ["ps0"], sig=False)
                            P.op("pe", (lambda c=c: nc.tensor.matmul(ps[0][0:64, 2 * c + 1:2 * c + 2], lhsT=abrow[0:4, c * 64:(c + 1) * 64],
                                                                     rhs=cf[0:4, 0, 2 + hl:3 + hl], start=True, stop=True)), ["abrow", "cf"], ["ps0"], sig=(c == 63))
                        P.op("dve", lambda: nc.vector.tensor_copy(out=gb[:], in_=ps[0][0:64, 0:128]), ["ps0"], ["gb"])
                        P.op("pe", lambda: nc.tensor.matmul(ps[1][0:64, 0:64], lhsT=U64, rhs=gb[:, 0:128:2], start=True, stop=True), ["gb", "cf"], ["ps1"])
                        P.op("pe", lambda: nc.tensor.matmul(ps[2][:, 0:64], lhsT=cf[0:64, 1, :], rhs=gb[:, 0:128:2], start=True, stop=True), ["gb", "cf"], ["ps2"])
                        P.op("dve", lambda: nc.vector.tensor_copy(out=gct[:], in_=ps[1][0:64, 0:64]), ["ps1"], ["gct"])
                        P.op("dve", lambda: nc.vector.tensor_copy(out=glb[:], in_=ps[2][:, 0:64]), ["ps2"], ["glb"])
                        P.op("act", lambda: nc.scalar.activation(out=egc[:], in_=gct[:], func=AF.Exp), ["gct"], ["egc"])
                        P.op("dve", lambda: nc.vector.tensor_tensor(out=bgc[:], in0=egc[:], in1=gb[:, 1:128:2], op=ALU.mult), ["egc", "gb"], ["bgc"])
                        P.op("dve", lambda: nc.vector.tensor_tensor(out=ekt[:], in0=glb[0:64, :], in1=gct[:], op=ALU.subtract), ["glb", "gct"], ["ekt"])
                        P.op("act", lambda: nc.scalar.activation(out=ekt[:], in_=ekt[:], func=AF.Exp), ["ekt"], ["ekt"])
                        P.op("act", lambda: nc.scalar.activation(out=decb[:], in_=glb[:], func=AF.Exp), ["glb"], ["decb"])
                        eg = sb(st, f"eg{hl}", [128, 512])
                        tl = sb(st, f"tl{hl}", [64, 512])
                        tu = sb(st, f"tu{hl}", [64, 512])
                        al = [sb(st, f"al{hl}_{i}", [64, 512]) for i in range(2)]
                        bm = [sb(st, f"bm{hl}_{i}", [64, 512]) for i in range(2)]
                        nn = [sb(st, f"nn{hl}_{i}", [64, 512]) for i in range(2)]
                        kbg = sb(st, f"kbg{hl}", [64, 8, 128])
                        vb = sb(st, f"vb{hl}", [64, 8, 128])
                        qdT = [sb(st, f"qdT{hl}_{i}", [128, 512]) for i in range(2)]
                        itT = [sb(st, f"itT{hl}_{i}", [64, 512]) for i in range(2)]
                        ktb = [sb(st, f"ktb{hl}_{i}", [64, 8, 128]) for i in range(2)]
                        ub = [sb(st, f"ub{hl}_{i}", [64, 8, 128]) for i in range(2)]
                        wTb = [sb(st, f"wTb{hl}_{i}", [128, 512]) for i in range(2)]
                        Sst = sb(st, f"Sst{hl}", [128, 128])
                        vn = [sb(st, f"vn{hl}_{i}", [64, 128]) for i in range(2)]
                        on = sb(st, f"on{hl}", [64, 128])
                        osq = sb(st, f"osq{hl}", [64, 128])
                        ssc = sb(st, f"ssc{hl}", [64, 2])
                        P.op("dve", lambda: nc.vector.memset(Sst[:], 0.0), [], ["Sst"])
                        pbank = [0]

                        def nbk_():
                            b = pbank[0] % 4
                            pbank[0] += 1
                            return b

                        binfo = {}

                        def prep_gen(bt):
                            pp = bt % 2
                            c0 = bt * 8
                            tc = lambda ci, bt=bt: slice(bt * 512 + ci * 64, bt * 512 + (ci + 1) * 64)
                            cc = lambda ci: slice(ci * 64, (ci + 1) * 64)
                            bs = slice(bt * 512, (bt + 1) * 512)
                            yield
                            bA = nbk_()
                            for ci in range(8):
                                P.op("pe", (lambda ci=ci, bA=bA: nc.tensor.matmul(
                                    ps[bA][:, cc(ci)], lhsT=gb[:, 2 * (c0 + ci):2 * (c0 + ci) + 1].to_broadcast([64, 128]), rhs=U64,
                                    start=True, stop=True)), ["gb", "cf"], [f"ps{bA}"], sig=(ci == 7))
                            P.op("act", (lambda bA=bA: nc.scalar.activation(out=eg[:], in_=ps[bA][:], func=AF.Exp)), [f"ps{bA}"], ["eg"])
                            P.op("dve", (lambda bs=bs, pp=pp: nc.vector.scalar_tensor_tensor(out=qdT[pp][:], in0=qf[:, bs], scalar=ISQ, in1=eg[:],
                                                                                            op0=ALU.mult, op1=ALU.mult)), ["qf", "eg"], [f"qdT{pp}"])
                            for ci in range(8):
                                P.op("dve", (lambda ci=ci, bA=bA: nc.vector.scalar_tensor_tensor(
                                    out=tl[:, cc(ci)], in0=ps[bA][0:64, cc(ci)], scalar=gct[:, c0 + ci:c0 + ci + 1], in1=pm_sl,
                                    op0=ALU.subtract, op1=ALU.add)), [f"ps{bA}", "gct", "cf"], ["tl"])
                                P.op("dve", (lambda ci=ci, bA=bA: nc.vector.scalar_tensor_tensor(
                                    out=tu[:, cc(ci)], in0=ps[bA][0:64, cc(ci)], scalar=gct[:, c0 + ci:c0 + ci + 1], in1=pm_ut,
                                    op0=ALU.subtract, op1=ALU.subtract)), [f"ps{bA}", "gct", "cf"], ["tu"])
                            P.op("act", lambda: nc.scalar.activation(out=tl[:], in_=tl[:], func=AF.Exp, scale=-1.0), ["tl"], ["tl"])
                            P.op("act", lambda: nc.scalar.activation(out=tu[:], in_=tu[:], func=AF.Exp), ["tu"], ["tu"])
                            yield
                            bB = nbk_()
                            for ci in range(8):
                                P.op("pe", (lambda ci=ci, bB=bB: nc.tensor.matmul(ps[bB][0:64, cc(ci)], lhsT=kf[:, tc(ci)], rhs=kf[:, tc(ci)],
                                                                                  start=True, stop=True)), ["kf"], [f"ps{bB}"], sig=(ci == 7))
                            for ci in range(8):
                                P.op("dve", (lambda ci=ci, bB=bB: nc.vector.scalar_tensor_tensor(
                                    out=al[0][:, cc(ci)], in0=ps[bB][0:64, cc(ci)], scalar=gb[:, 2 * (c0 + ci) + 1:2 * (c0 + ci) + 2], in1=tl[:, cc(ci)],
                                    op0=ALU.mult, op1=ALU.mult)), [f"ps{bB}", "gb", "tl"], ["al0"])
                            yield
                            bC = nbk_()
                            for ci in range(8):
                                P.op("pe", (lambda ci=ci, bC=bC: nc.tensor.matmul(ps[bC][0:64, cc(ci)], lhsT=kf[:, tc(ci)], rhs=qf[:, tc(ci)],
                                                                                  start=True, stop=True)), ["kf", "qf"], [f"ps{bC}"], sig=(ci == 7))
                            P.op("dve", (lambda bC=bC, pp=pp: nc.vector.scalar_tensor_tensor(out=itT[pp][:], in0=ps[bC][0:64, :], scalar=ISQ, in1=tu[:],
                                                                                            op0=ALU.mult, op1=ALU.mult)), [f"ps{bC}", "tu"], [f"itT{pp}"])
                            yield
                            bD = nbk_()
                            for ci in range(8):
                                P.op("pe", (lambda ci=ci, bD=bD: nc.tensor.matmul(ps[bD][0:64, cc(ci)], lhsT=al[0][:, cc(ci)], rhs=id64,
                                                                                  start=True, stop=True)), ["al0", "cf"], [f"ps{bD}"], sig=(ci == 7))
                            P.op("act", (lambda bD=bD: nc.scalar.copy(out=bm[0][:], in_=ps[bD][0:64, :])), [f"ps{bD}"], ["bm0"])
                            P.op("dve", (lambda bD=bD: nc.vector.scalar_tensor_tensor(out=nn[0][:], in0=ps[bD][0:64, :], scalar=-1.0,
                                                                                     in1=idt8[:].rearrange("p a b -> p (a b)"), op0=ALU.mult, op1=ALU.add)),
                                 [f"ps{bD}", "idt8"], ["nn0"])
                            yield
                            cur = 0
                            for s_ in range(1, 6):
                                nx = 1 - cur
                                bE = nbk_()
                                for ci in range(8):
                                    P.op("pe", (lambda ci=ci, bE=bE, cur=cur: nc.tensor.matmul(ps[bE][0:64, cc(ci)], lhsT=bm[cur][:, cc(ci)], rhs=al[cur][:, cc(ci)],
                                                                                              start=True, stop=True)), [f"bm{cur}", f"al{cur}"], [f"ps{bE}"], sig=(ci == 7))
                                P.op("act", (lambda bE=bE, nx=nx: nc.scalar.copy(out=al[nx][:], in_=ps[bE][0:64, :])), [f"ps{bE}"], [f"al{nx}"])
                                if s_ < 5:
                                    bF = nbk_()
                                    for ci in range(8):
                                        P.op("pe", (lambda ci=ci, bF=bF, cur=cur: nc.tensor.matmul(ps[bF][0:64, cc(ci)], lhsT=al[cur][:, cc(ci)], rhs=bm[cur][:, cc(ci)],
                                                                                                  start=True, stop=True)), [f"bm{cur}", f"al{cur}"], [f"ps{bF}"], sig=(ci == 7))
                                    P.op("dve", (lambda bF=bF, nx=nx: nc.vector.tensor_copy(out=bm[nx][:], in_=ps[bF][0:64, :])), [f"ps{bF}"], [f"bm{nx}"])
                                bG_ = nbk_()
                                for ci in range(8):
                                    P.op("pe", (lambda ci=ci, bG_=bG_, cur=cur, nx=nx: nc.tensor.matmul(ps[bG_][0:64, cc(ci)], lhsT=al[nx][:, cc(ci)], rhs=nn[cur][:, cc(ci)],
                                                                                                       start=True, stop=True)), [f"al{nx}", f"nn{cur}"], [f"ps{bG_}"], sig=(ci == 7))
                                P.op("dve", (lambda bG_=bG_, cur=cur, nx=nx: nc.vector.tensor_tensor(out=nn[nx][:], in0=nn[cur][:], in1=ps[bG_][0:64, :], op=ALU.add)),
                                     [f"ps{bG_}", f"nn{cur}"], [f"nn{nx}"])
                                cur = nx
                                yield
                            nfin = nn[cur]
                            nkey = f"nn{cur}"
                            yield
                            for half in range(2):
                                bH = nbk_()
                                for u in range(4):
                                    ci = half * 4 + u
                                    P.op("pe", (lambda ci=ci, u=u, bH=bH: nc.tensor.matmul(ps[bH][0:64, u * 128:(u + 1) * 128], lhsT=kf[:, tc(ci)], rhs=ident,
                                                                                          start=True, stop=True)), ["kf", "cf"], [f"ps{bH}"], sig=(u == 3))
                                for u in range(4):
                                    ci = half * 4 + u
                                    P.op("dve", (lambda ci=ci, u=u, bH=bH: nc.vector.tensor_scalar(
                                        out=kbg[:, ci, :], in0=ps[bH][0:64, u * 128:(u + 1) * 128], scalar1=bgc[:, c0 + ci:c0 + ci + 1], scalar2=None, op0=ALU.mult)),
                                        [f"ps{bH}", "bgc"], ["kbg"])
                                    P.op("act", (lambda ci=ci, u=u, bH=bH, pp=pp: nc.scalar.activation(
                                        out=ktb[pp][:, ci, :], in_=ps[bH][0:64, u * 128:(u + 1) * 128], func=AF.Copy, scale=ekt[:, c0 + ci:c0 + ci + 1])),
                                        [f"ps{bH}", "ekt"], [f"ktb{pp}"])
                                bH = nbk_()
                                for u in range(4):
                                    ci = half * 4 + u
                                    P.op("pe", (lambda ci=ci, u=u, bH=bH: nc.tensor.matmul(ps[bH][0:64, u * 128:(u + 1) * 128], lhsT=vf[:, tc(ci)], rhs=ident,
                                                                                          start=True, stop=True)), ["vf", "cf"], [f"ps{bH}"], sig=(u == 3))
                                for u in range(4):
                                    ci = half * 4 + u
                                    P.op("dve", (lambda ci=ci, u=u, bH=bH: nc.vector.tensor_scalar(
                                        out=vb[:, ci, :], in0=ps[bH][0:64, u * 128:(u + 1) * 128], scalar1=gb[:, 2 * (c0 + ci) + 1:2 * (c0 + ci) + 2], scalar2=None, op0=ALU.mult)),
                                        [f"ps{bH}", "gb"], ["vb"])
                            yield
                            for half in range(2):
                                bU_ = nbk_()
                                for u in range(4):
                                    ci = half * 4 + u
                                    P.op("pe", (lambda ci=ci, u=u, bU_=bU_: nc.tensor.matmul(ps[bU_][0:64, u * 128:(u + 1) * 128], lhsT=nfin[:, cc(ci)], rhs=vb[:, ci, :],
                                                                                            start=True, stop=True)), [nkey, "vb"], [f"ps{bU_}"], sig=(u == 3))
                                P.op("act", (lambda half=half, bU_=bU_, pp=pp: nc.scalar.copy(out=ub[pp][:, half * 4:half * 4 + 4, :],
                                                                                              in_=ps[bU_][0:64, :].rearrange("p (u e) -> p u e", e=128))),
                                     [f"ps{bU_}"], [f"ub{pp}"])
                            bW = nbk_()
                            for ci in range(8):
                                P.op("pe", (lambda ci=ci, bW=bW: nc.tensor.matmul(ps[bW][:, cc(ci)], lhsT=kbg[:, ci, :], rhs=nfin[:, cc(ci)],
                                                                                  start=True, stop=True)), [nkey, "kbg"], [f"ps{bW}"], sig=(ci == 7))
                            P.op("dve", (lambda bW=bW, pp=pp: nc.vector.tensor_copy(out=wTb[pp][:], in_=ps[bW][:])), [f"ps{bW}"], [f"wTb{pp}"])
                            binfo[bt] = (nfin, nkey)
                            yield

                        def scan_gen(bt):
                            pp = bt % 2
                            c0 = bt * 8
                            cc = lambda ci: slice(ci * 64, (ci + 1) * 64)
                            for ci in range(8):
                                c = c0 + ci
                                vv = vn[c % 2]
                                vk = f"vn{c % 2}"
                                P.op("pe", (lambda ci=ci, pp=pp: nc.tensor.matmul(ps[4][0:64, 0:128], lhsT=wTb[pp][:, cc(ci)], rhs=Sst[:], start=True, stop=True)),
                                     [f"wTb{pp}", "Sst"], ["ps4"])
                                P.op("dve", (lambda ci=ci, pp=pp, vv=vv: nc.vector.tensor_tensor(out=vv[:], in0=ub[pp][:, ci, :], in1=ps[4][0:64, 0:128], op=ALU.subtract)),
                                     ["ps4", f"ub{pp}"], [vk])
                                yield
                                P.op("pe", (lambda ci=ci, pp=pp: nc.tensor.matmul(ps[5][0:64, 0:128], lhsT=qdT[pp][:, cc(ci)], rhs=Sst[:], start=True, stop=False)),
                                     [f"qdT{pp}", "Sst"], ["ps5"], sig=False)
                                P.op("pe", (lambda ci=ci, pp=pp, vv=vv: nc.tensor.matmul(ps[5][0:64, 0:128], lhsT=itT[pp][:, cc(ci)], rhs=vv[:], start=False, stop=True)),
                                     [f"itT{pp}", vk], ["ps5"])
                                P.op("pe", (lambda ci=ci, pp=pp, vv=vv: nc.tensor.matmul(ps[6][:, 0:128], lhsT=ktb[pp][:, ci, :], rhs=vv[:], start=True, stop=True)),
                                     [f"ktb{pp}", vk], ["ps6"])
                                P.op("dve", (lambda c=c: nc.vector.scalar_tensor_tensor(out=Sst[:], in0=Sst[:], scalar=decb[:, c:c + 1], in1=ps[6][:, 0:128],
                                                                                        op0=ALU.mult, op1=ALU.add)), ["ps6", "decb", "Sst"], ["Sst"])
                                yield
                                P.op("act", lambda: nc.scalar.activation(out=osq[:], in_=ps[5][0:64, 0:128], func=AF.Square, accum_out=ssc[:, 0:1]),
                                     ["ps5"], ["osq", "ssc"])
                                P.op("act", lambda: nc.scalar.activation(out=ssc[:, 1:2], in_=ssc[:, 0:1], func=AF.Sqrt, bias=epsc[0:64, 0:1], scale=1.0 / 128),
                                     ["ssc", "epsc"], ["ssc"])
                                P.op("dve", lambda: nc.vector.reciprocal(out=ssc[:, 1:2], in_=ssc[:, 1:2]), ["ssc"], ["ssc"])
                                P.op("dve", lambda: nc.vector.tensor_scalar(out=on[:], in0=ps[5][0:64, 0:128], scalar1=ssc[:, 1:2], scalar2=None, op0=ALU.mult),
                                     ["ps5", "ssc"], ["on"])
                                P.op("pe", lambda: nc.tensor.matmul(ps[7][:, 0:64], lhsT=on[:], rhs=id64, start=True, stop=True), ["on", "cf"], ["ps7"])
                                ocs = slice((c % 16) * 64, (c % 16) * 64 + 64)
                                gcs = slice(c * 64, (c + 1) * 64)
                                P.op("dve", (lambda ocs=ocs, gcs=gcs: nc.vector.scalar_tensor_tensor(out=ostg[:, ocs], in0=ps[7][:, 0:64], scalar=don[:, 0:1], in1=sz[:, gcs],
                                                                                                     op0=ALU.mult, op1=ALU.mult)), ["ps7", "don", "sz"], ["ostg"])
                                if c % 16 == 15:
                                    qb = c // 16
                                    P.dma("sp", ex2_in[qb][(2 + hl) * 128:(3 + hl) * 128, :], ostg[:], ["ostg"], [f"ex2in{qb}"], grp="w_ex2in")
                                yield

                        for _ in prep_gen(0):
                            pass
                        for bt in range(8):
                            sg_ = scan_gen(bt)
                            pg_ = prep_gen(bt + 1) if bt + 1 < 8 else None
                            alive_s, alive_p = True, pg_ is not None
                            while alive_s or alive_p:
                                if alive_s:
                                    try:
                                        next(sg_)
                                    except StopIteration:
                                        alive_s = False
                                if alive_p:
                                    try:
                                        next(pg_)
                                    except StopIteration:
                                        alive_p = False
                        P.barrier()
            if "stopDelta" in debug:
                out_from_spill()
                return nc
            for qb in range(4):
                P.op("pool", (lambda qb=qb: nc.gpsimd.collective_compute(
                    "AllGather", ALU.bypass, replica_groups=GROUPS4, ins=[ex2_in[qb]], outs=[ex2_out[qb * 2048:(qb + 1) * 2048, :]])),
                    [f"ex2in{qb}"], ["ex2out"], dma=True, grp=f"cc_ex2_{qb}", inc=1)
            P.barrier()
        xst2 = contextlib.ExitStack()
        X["x"] = sb(xst2, "x2", [128, KC, T])
        x = X["x"]
        P.dma("sp", x[:], xspill, ["xspill"], ["x"])
        with contextlib.ExitStack() as st:
            idx2 = sb(st, "idx2", [128, 16], I32)
            og = [sb(st, f"og{j}", [128, T], BF16) for j in range(16)]
            wo = [sb(st, f"wo{i}", [128, KC, 256], BF16) for i in range(2)]
            P.dma("sp", idx2[:], idx2_d, [], ["idx2"])
            for j in range(16):
                P.op("pool", (lambda j=j: nc.gpsimd.indirect_dma_start(
                    out=og[j][:], out_offset=None, in_=ex2_out, in_offset=bass.IndirectOffsetOnAxis(ap=idx2[:, j:j + 1], axis=0))),
                    ["ex2out", "idx2"], [f"og{j}"], dma=True, grp="gath2")
            if "og" in debug:
                dbg["og"] = dout("dbg_og", [16, 128, T], BF16)
                for j in range(16):
                    P.dma("sp", dbg["og"][j], og[j][:], [f"og{j}"], ["dbg_og"])
            for g in range(8):
                s = g % 2
                P.dma("pool", wo[s][:], wo_d[:, g * 256:(g + 1) * 256].rearrange("(kc p) f -> p kc f", p=128), [], [f"wo{s}"])
                for dcl in range(2):
                    dc = 2 * g + dcl
                    for hf in range(2):
                        bk = (dcl * 2 + hf) % 4
                        tsl = slice(hf * 512, (hf + 1) * 512)
                        for j in range(16):
                            P.op("pe", (lambda j=j, dcl=dcl, tsl=tsl, bk=bk, s=s: nc.tensor.matmul(
                                ps[bk][:], lhsT=wo[s][:, j, dcl * 128:(dcl + 1) * 128], rhs=og[j][:, tsl],
                                start=(j == 0), stop=(j == 15))), [f"wo{s}", f"og{j}"], [f"ps{bk}"], sig=(j == 15))
                        P.op("dve", (lambda dc=dc, tsl=tsl, bk=bk: nc.vector.scalar_tensor_tensor(
                            out=x[:, dc, tsl], in0=ps[bk][:], scalar=gcol(1)[:, dc:dc + 1], in1=x[:, dc, tsl],
                            op0=ALU.mult, op1=ALU.add)), [f"ps{bk}", "cols", "x"], ["x"])
            P.barrier()
        if "x2" in debug:
            dbg["x2"] = dout("dbg_x2", [128, KC, T])
            P.dma("sp", dbg["x2"], x[:], ["x"], ["dbg_x2"])
        if stage >= 3:
            with contextlib.ExitStack() as st:
                h3 = sb(st, "h3", [128, KC, T], BF16)
                modulate(st, 2, h3)
                ffn(st, 2, h3, w2g_d, w2u_d, w2d_d)
                P.barrier()
        P.dma("sp", outT_d.rearrange("(kc p) t -> p kc t", p=128), x[:], ["x"], ["outT"])
        P.finish()
        xst2.close()
    return nc


def _consts():
    c = np.zeros((128, 6, 128), np.float32)
    c[:, 0, :] = np.eye(128, dtype=np.float32)
    c[:, 1, :] = 1.0
    permT = np.zeros((128, 128), np.float32)
    for m in range(16):
        permT[m + 16, m] = -1.0
        permT[m, m + 16] = 1.0
    c[:, 2, :] = permT
    U = np.zeros((128, 128), np.float32)
    U[:64, :64] = np.triu(np.ones((64, 64), np.float32))
    c[:, 3, :] = U
    sl = np.zeros((128, 128), np.float32)
    i = np.arange(64)[:, None]; j = np.arange(64)[None, :]
    sl[:64, :64] = np.where(i > j, 0.0, 1e30)
    c[:, 4, :] = sl
    ut = np.zeros((128, 128), np.float32)
    ut[:64, :64] = np.where(j >= i, 0.0, 1e30)
    c[:, 5, :] = ut
    return c


def make_in_maps(inputs, dff=DFF):
    x = np.asarray(inputs["x"], np.float32)
    c = np.asarray(inputs["c"], np.float32)
    w_ada = np.asarray(inputs["w_ada"])[0]
    b_ada = np.asarray(inputs["b_ada"])[0]
    gains = np.stack([np.asarray(inputs[k])[0].reshape(KC, 128).T for k in ("ffn1_norm", "mix_norm", "ffn2_norm")], axis=1)
    cT = np.ascontiguousarray(c.reshape(2, KC, 128).transpose(2, 1, 0))
    cf = _consts()
    maps = []
    for cid in range(NCORES):
        b, tq = cid // 4, cid % 4
        sel = np.zeros((128, 2), np.float32); sel[:, b] = 1.0
        m = {
            "xT": np.ascontiguousarray(x[b, tq * T:(tq + 1) * T, :].T),
            "cT": cT, "sel": sel,
            "wada": np.ascontiguousarray(w_ada[:, tq * 4608:(tq + 1) * 4608]),
            "bada": np.ascontiguousarray(b_ada[tq * 4608:(tq + 1) * 4608].reshape(36, 128).T),
            "gains": np.ascontiguousarray(gains),
            "w1g": np.asarray(inputs["ffn1_w_gate"])[0][:, :dff], "w1u": np.asarray(inputs["ffn1_w_up"])[0][:, :dff],
            "w1d": np.asarray(inputs["ffn1_w_down"])[0][:dff],
            "w2g": np.asarray(inputs["ffn2_w_gate"])[0][:, :dff], "w2u": np.asarray(inputs["ffn2_w_up"])[0][:, :dff],
            "w2d": np.asarray(inputs["ffn2_w_down"])[0][:dff],
            "cf32": cf,
        }
        hp = tq
        w_in = np.asarray(inputs["w_in"])[0]
        heads = [2 * hp, 2 * hp + 1]
        m["pos"] = np.ascontiguousarray(np.asarray(inputs["positions"])[b].reshape(1, S).astype(np.int32))
        m["wA"] = np.stack([np.concatenate([w_in[:, t * 1024 + h * 128:t * 1024 + (h + 1) * 128] for t in range(3)], axis=1) for h in heads])
        m["wD"] = np.stack([np.concatenate([w_in[:, 3072 + t * 1024 + h * 128:3072 + t * 1024 + (h + 1) * 128] for t in range(4)], axis=1) for h in heads])
        m["wab"] = np.ascontiguousarray(np.stack([w_in[:, 7168 + heads[0]], w_in[:, 7168 + heads[1]],
                                                  w_in[:, 7176 + heads[0]], w_in[:, 7176 + heads[1]]], axis=1))
        m["qkg"] = np.ascontiguousarray(np.stack([np.asarray(inputs["q_norm"])[0], np.asarray(inputs["k_norm"])[0]], axis=1))
        cw = np.asarray(inputs["conv_w"])[0]
        convc = np.zeros((128, 2, 3, 4), np.float32)
        for hl, h in enumerate(heads):
            for t in range(3):
                convc[:, hl, t, :] = cw[:, t * 1024 + h * 128:t * 1024 + (h + 1) * 128].T
        m["convc"] = convc
        abp = np.zeros((4, 2), np.float32)
        for hl, h in enumerate(heads):
            abp[hl, 0] = np.asarray(inputs["a_log"])[0, h]
            abp[hl, 1] = np.asarray(inputs["dt_bias"])[0, h]
        m["abp"] = abp
        m["don"] = np.ascontiguousarray(np.asarray(inputs["delta_out_norm"])[0].reshape(128, 1))
        invf = np.zeros((128, 1), np.float32)
        invf[:32, 0] = np.tile((np.float32(500000.0) ** (-np.arange(16, dtype=np.float32) / np.float32(16))).astype(np.float32), 2)
        m["invf"] = invf
        kj = np.arange(128)[:, None]; qi = np.arange(128)[None, :]
        am = np.concatenate([np.where(qi >= kj, 0.0, -30000.0), np.where(kj >= qi, 0.0, -30000.0)], axis=1).astype(np.float32)
        m["amask"] = am.astype(ml_dtypes.bfloat16)
        m["idx2"] = (tq * 2048 + np.arange(16)[None, :] * 128 + np.arange(128)[:, None]).astype(np.int32)
        w_out = np.asarray(inputs["w_out"])[0]
        rows = []
        for r in range(4):
            for k in range(4):
                base = (2 * r + k) * 128 if k < 2 else 1024 + (2 * r + k - 2) * 128
                rows.append(w_out[base:base + 128])
        m["wo"] = np.ascontiguousarray(np.concatenate(rows, axis=0))
        maps.append(m)
    return maps


def assemble(results, key="outT"):
    out = np.zeros((2, S, D), np.float32)
    for cid in range(NCORES):
        b, tq = cid // 4, cid % 4
        out[b, tq * T:(tq + 1) * T, :] = np.asarray(results[cid][key]).T
    return out


def kernel(**inputs):
    nc = build(stage=3)
    maps = make_in_maps(inputs)
    res = run_bass_kernel_spmd(nc, maps, core_ids=list(range(NCORES)))
    return assemble(res.results)
```

```python
import contextlib
import bisect
import numpy as np
import ml_dtypes
import concourse.bass as bass
import concourse.mybir as mybir
from concourse.bass_utils import run_bass_kernel_spmd

F32 = mybir.dt.float32
BF16 = mybir.dt.bfloat16
I32 = mybir.dt.int32
AF = mybir.ActivationFunctionType
ALU = mybir.AluOpType

D = 2048
KC = 16
T = 1024
S = 4096
DFF = 5632
EPS = 1e-6
NCORES = 8
GROUPS4 = [[0, 1, 2, 3], [4, 5, 6, 7]]
GROUPS8 = [[0, 1, 2, 3, 4, 5, 6, 7]]


class Prog:
    def __init__(self, nc, stack):
        self.nc = nc
        self.stack = stack
        self.engs = {"pe": nc.tensor, "act": nc.scalar, "dve": nc.vector, "pool": nc.gpsimd, "sp": nc.sync}
        self.sems = {}
        self.cnt = {}
        self.waited = {}
        self.last_w = {}
        self.readers = {}
        self.pe_sig_idx = []
        self.pe_sig_val = []
        self.n_pe = 0
        self.nops = 0
        self.nwaits = 0

    def sem(self, name):
        if name not in self.sems:
            self.sems[name] = self.stack.enter_context(self.nc.semaphore(name))
            self.cnt[name] = 0
        return self.sems[name]

    def _need(self, rec, need):
        kind = rec[0]
        if kind == "dma":
            s = rec[1]
            v = self.cnt[s]
        elif kind == "pe":
            s = "c_pe"
            i = bisect.bisect_left(self.pe_sig_idx, rec[1])
            if i >= len(self.pe_sig_idx):
                ins = self.nc.tensor.nop()
                self.cnt[s] = self.cnt.get(s, 0) + 1
                ins.then_inc(self.sem(s), 1)
                self.pe_sig_idx.append(self.n_pe)
                self.pe_sig_val.append(self.cnt[s])
                self.n_pe += 1
                i = len(self.pe_sig_idx) - 1
            v = self.pe_sig_val[i]
        else:
            s = "c_" + kind
            v = rec[1]
        if v > need.get(s, 0):
            need[s] = v

    def op(self, eng, fn, reads=(), writes=(), dma=False, grp=None, inc=16, sig=True):
        writes = tuple(writes) + tuple(k for k in reads if k.startswith("ps") and k not in writes)
        need = {}
        seen = set()
        for k in reads:
            w = self.last_w.get(k)
            if w is not None and id(w) not in seen:
                seen.add(id(w))
                if not (w[0] == "pe" and eng == "pe" and not dma):
                    self._need(w, need)
        for k in writes:
            w = self.last_w.get(k)
            if w is not None and id(w) not in seen:
                seen.add(id(w))
                if not (w[0] == "pe" and eng == "pe" and not dma):
                    self._need(w, need)
            for r in self.readers.get(k, ()):
                if id(r) not in seen:
                    seen.add(id(r))
                    if not (r[0] == "pe" and eng == "pe" and not dma):
                        self._need(r, need)
        e = self.engs[eng]
        for s, v in need.items():
            if self.waited.get((eng, s), 0) >= v:
                continue
            self.waited[(eng, s)] = v
            e.wait_ge(self.sem(s), v)
            self.nwaits += 1
        ins = fn()
        self.nops += 1
        if dma:
            if grp is None:
                grp = "d_" + str(writes[0])
            self.sem(grp)
            self.cnt[grp] += inc
            ins.then_inc(self.sems[grp], inc)
            rec = ("dma", grp)
        elif eng == "pe":
            rec = ("pe", self.n_pe)
            if sig:
                self.sem("c_pe")
                self.cnt["c_pe"] += 1
                ins.then_inc(self.sems["c_pe"], 1)
                self.pe_sig_idx.append(self.n_pe)
                self.pe_sig_val.append(self.cnt["c_pe"])
            self.n_pe += 1
        else:
            s = "c_" + eng
            self.sem(s)
            self.cnt[s] += 1
            ins.then_inc(self.sems[s], 1)
            rec = (eng, self.cnt[s])
        for k in writes:
            self.last_w[k] = rec
            self.readers[k] = []
        for k in reads:
            if k not in writes:
                self.readers.setdefault(k, []).append(rec)
        return ins

    def dma(self, eng, out, in_, reads, writes, grp=None):
        e = self.engs[eng]
        return self.op(eng, lambda: e.dma_start(out=out, in_=in_), reads, writes, dma=True, grp=grp)

    def barrier(self, skip_cc=False):
        if self.n_pe and (not self.pe_sig_idx or self.pe_sig_idx[-1] != self.n_pe - 1):
            ins = self.nc.tensor.nop()
            self.sem("c_pe")
            self.cnt["c_pe"] += 1
            ins.then_inc(self.sems["c_pe"], 1)
            self.pe_sig_idx.append(self.n_pe)
            self.pe_sig_val.append(self.cnt["c_pe"])
            self.n_pe += 1
        for en, e in self.engs.items():
            for s, v in self.cnt.items():
                if skip_cc and s.startswith("cc_"):
                    continue
                if v > 0 and self.waited.get((en, s), 0) < v:
                    self.waited[(en, s)] = v
                    e.wait_ge(self.sems[s], v)
        if skip_cc:
            self.last_w = {k: r for k, r in self.last_w.items() if r[0] == "dma" and r[1].startswith("cc_")}
        else:
            self.last_w = {}
        self.readers = {}

    def finish(self):
        self.barrier()


class Ctx:
    pass


def build(stage=99, dff=DFF, debug=()):
    nc = bass.Bass("TRN2", target_bir_lowering=False)
    NG = dff // 256

    def din(name, shape, dt=F32):
        return nc.dram_tensor(name, list(shape), dt, kind="ExternalInput").ap()

    def dout(name, shape, dt=F32):
        return nc.dram_tensor(name, list(shape), dt, kind="ExternalOutput").ap()

    def dint(name, shape, dt=F32):
        return nc.dram_tensor(name, list(shape), dt, kind="Internal").ap()

    xT_d = din("xT", [D, T])
    cT_d = din("cT", [128, KC, 2])
    sel_d = din("sel", [128, 2])
    wada_d = din("wada", [D, 4608])
    bada_d = din("bada", [128, 36])
    gains_d = din("gains", [128, 3, KC])
    w1g_d = din("w1g", [D, dff]); w1u_d = din("w1u", [D, dff]); w1d_d = din("w1d", [dff, D])
    w2g_d = din("w2g", [D, dff]); w2u_d = din("w2u", [D, dff]); w2d_d = din("w2d", [dff, D])
    consts_d = din("cf32", [128, 6, 128])
    outT_d = dout("outT", [D, T])
    dbg = {}

    with contextlib.ExitStack() as top:
        P = Prog(nc, top)
        E = top.enter_context

        def sb(st, name, shape, dt=F32):
            return st.enter_context(nc.sbuf_tensor("s_" + name, list(shape), dt))

        ps = [E(nc.psum_tensor(f"ps{i}", [128, 512], F32)) for i in range(8)]

        modsel = sb(top, "modsel", [128, 144])
        gains = sb(top, "gains_t", [128, 3, KC])
        cols = sb(top, "cols", [128, 8, KC])
        cf = sb(top, "cf", [128, 6, 128])
        epsc = sb(top, "epsc", [128, 4])
        xst = contextlib.ExitStack()
        X = {"x": sb(xst, "x", [128, KC, T])}
        x = X["x"]
        ident = cf[:, 0, :]
        ones = cf[:, 1, :]

        P.dma("sp", x[:], xT_d.rearrange("(kc p) t -> p kc t", p=128), [], ["x"])
        P.dma("sp", cf[:], consts_d, [], ["cf"])
        P.dma("sp", gains[:], gains_d, [], ["gains"])
        P.op("dve", lambda: nc.vector.memset(epsc[:, 0:1], EPS), [], ["epsc"])
        P.op("dve", lambda: nc.vector.memset(epsc[:, 1:2], float(np.pi / 2)), [], ["epsc"])
        P.op("dve", lambda: nc.vector.memset(epsc[:, 2:3], 1.0), [], ["epsc"])
        P.op("dve", lambda: nc.vector.memset(epsc[:, 3:4], 0.0), [], ["epsc"])

        mod_in = dint("mod_in", [128, 128])
        mod_out = dint("mod_out", [4 * 128, 128])
        with contextlib.ExitStack() as st:
            cT = sb(st, "cT", [128, KC, 2])
            cact = sb(st, "cact", [128, KC, 2])
            sel = sb(st, "sel", [128, 2])
            bada = sb(st, "bada", [128, 36])
            wa = [sb(st, f"wa{i}", [128, 2304]) for i in range(4)]
            modp = sb(st, "modp", [128, 64, 2])
            modrow = sb(st, "modrow", [2, 4608])
            modall = sb(st, "modall", [128, 4, 36, 2])
            tmpm = sb(st, "tmpm", [128, 144])
            P.dma("sp", cT[:], cT_d, [], ["cT"])
            P.dma("sp", sel[:], sel_d, [], ["sel"])
            P.dma("sp", bada[:], bada_d, [], ["bada"])
            P.op("dve", lambda: nc.vector.memset(modp[:], 0.0), [], ["modp"])
            P.op("act", lambda: nc.scalar.activation(out=cact[:], in_=cT[:], func=AF.Silu), ["cT"], ["cact"])
            wi = 0
            for half in range(2):
                for kc in range(KC):
                    w = wa[wi % 4]
                    wk = f"wa{wi % 4}"
                    wi += 1
                    P.dma("sp", w[:], wada_d[kc * 128:(kc + 1) * 128, half * 2304:(half + 1) * 2304], [], [wk])
                    for sl in range(5):
                        n_ = 512 if sl < 4 else 256
                        P.op("pe", (lambda w=w, sl=sl, kc=kc, n_=n_: nc.tensor.matmul(
                            ps[sl][0:2, 0:n_], lhsT=cact[:, kc, :], rhs=w[:, sl * 512:sl * 512 + n_],
                            start=(kc == 0), stop=(kc == KC - 1))), [wk, "cact"], [f"ps{sl}"], sig=(kc == KC - 1 or sl == 4))
                for sl in range(5):
                    n_ = 512 if sl < 4 else 256
                    c0_ = half * 2304 + sl * 512
                    if sl % 2 == 0:
                        P.op("act", (lambda sl=sl, n_=n_, c0_=c0_: nc.scalar.copy(out=modrow[:, c0_:c0_ + n_], in_=ps[sl][0:2, 0:n_])),
                             [f"ps{sl}"], ["modrow"])
                    else:
                        P.op("dve", (lambda sl=sl, n_=n_, c0_=c0_: nc.vector.tensor_copy(out=modrow[:, c0_:c0_ + n_], in_=ps[sl][0:2, 0:n_])),
                             [f"ps{sl}"], ["modrow"])
            for j in range(36):
                P.op("pe", (lambda j=j: nc.tensor.matmul(ps[5][:, 2 * j:2 * j + 2], lhsT=modrow[0:2, j * 128:(j + 1) * 128],
                                                         rhs=cf[0:2, 0, 0:2], start=True, stop=True)), ["modrow", "cf"], ["ps5"], sig=(j == 35))
            P.op("dve", lambda: nc.vector.tensor_tensor(
                out=modp[:, 0:36, :], in0=ps[5][:, 0:72].rearrange("p (j b) -> p j b", b=2),
                in1=bada[:].unsqueeze(2).to_broadcast([128, 36, 2]), op=ALU.add), ["ps5", "bada", "modp"], ["modp"])
            P.dma("sp", mod_in, modp[:].rearrange("p j b -> p (j b)"), ["modp"], ["mod_in"])
            if "nocc" in debug:
                for r in range(4):
                    P.dma("sp", mod_out[r * 128:(r + 1) * 128, :], mod_in, ["mod_in"], ["mod_out"])
            else:
                P.op("pool", lambda: nc.gpsimd.collective_compute(
                    "AllGather", ALU.bypass, replica_groups=GROUPS4, ins=[mod_in], outs=[mod_out]),
                    ["mod_in"], ["mod_out"], dma=True, grp="cc_mod", inc=1)
            P.dma("sp", modall[:].rearrange("p r j b -> p r (j b)"),
                  mod_out.rearrange("(r p) f -> p r f", p=128)[:, :, 0:72], ["mod_out"], ["modall"])
            P.op("dve", lambda: nc.vector.tensor_scalar(
                out=tmpm[:].rearrange("p (r j) -> p r j", j=36), in0=modall[:, :, :, 0], scalar1=sel[:, 0:1], scalar2=None,
                op0=ALU.mult), ["modall", "sel"], ["tmpm"])
            P.op("dve", lambda: nc.vector.scalar_tensor_tensor(
                out=modsel[:].rearrange("p (r j) -> p r j", j=36), in0=modall[:, :, :, 1], scalar=sel[:, 1:2],
                in1=tmpm[:].rearrange("p (r j) -> p r j", j=36), op0=ALU.mult, op1=ALU.add),
                ["modall", "sel", "tmpm"], ["modsel"])
            for k in range(3):
                sc = modsel[:, (3 * k + 1) * 16:(3 * k + 1) * 16 + 16]
                gt = modsel[:, (3 * k + 2) * 16:(3 * k + 2) * 16 + 16]
                P.op("dve", (lambda k=k, sc=sc: nc.vector.scalar_tensor_tensor(
                    out=cols[:, 2 * k, :], in0=sc, scalar=1.0, in1=gains[:, k, :],
                    op0=ALU.add, op1=ALU.mult)), ["modsel", "gains"], ["cols"])
                P.op("dve", (lambda k=k, gt=gt: nc.vector.tensor_scalar(
                    out=cols[:, 2 * k + 1, :], in0=gt,
                    scalar1=(1.0 if k == 1 else 0.5), scalar2=None, op0=ALU.mult)), ["modsel"], ["cols"])
            if "modsel" in debug:
                dbg["modsel"] = dout("dbg_modsel", [128, 144])
                P.dma("sp", dbg["modsel"], modsel[:], ["modsel"], ["dbg_modsel"])
            P.barrier()

        if stage <= 0:
            P.dma("sp", outT_d.rearrange("(kc p) t -> p kc t", p=128), x[:], ["x"], ["outT"])
            P.finish()
            xst.close()
            return nc

        def scol(k):
            return cols[:, 2 * k, :]

        def gcol(k):
            return cols[:, 2 * k + 1, :]

        def shcol(k):
            return modsel[:, (3 * k) * 16:(3 * k) * 16 + 16]

        def modulate(st, k, h):
            x = X["x"]
            sq = [sb(st, f"sq{k}_{i}", [128, T]) for i in range(2)]
            ssum = sb(st, f"ssum{k}", [128, T])
            rstd = sb(st, f"rstd{k}", [128, T])
            for kc in range(KC):
                q = sq[kc % 2]
                if kc == 0:
                    P.op("act", lambda: nc.scalar.activation(out=ssum[:], in_=x[:, 0, :], func=AF.Square), ["x"], ["ssum"])
                else:
                    P.op("act", (lambda q=q, kc=kc: nc.scalar.activation(out=q[:], in_=x[:, kc, :], func=AF.Square)),
                         ["x"], [f"sq{kc % 2}"])
                    P.op("dve", (lambda q=q: nc.vector.tensor_tensor(out=ssum[:], in0=ssum[:], in1=q[:], op=ALU.add)),
                         [f"sq{kc % 2}", "ssum"], ["ssum"])
            for hf in range(2):
                P.op("pe", (lambda hf=hf: nc.tensor.matmul(ps[hf][:], lhsT=ones, rhs=ssum[:, hf * 512:(hf + 1) * 512],
                                                           start=True, stop=True)), ["ssum", "cf"], [f"ps{hf}"])
                P.op("act", (lambda hf=hf: nc.scalar.activation(out=rstd[:, hf * 512:(hf + 1) * 512], in_=ps[hf][:], func=AF.Ln,
                                                                bias=epsc[:, 0:1], scale=1.0 / D)), [f"ps{hf}", "epsc"], [f"rstd{hf}"])
                P.op("act", (lambda hf=hf: nc.scalar.activation(out=rstd[:, hf * 512:(hf + 1) * 512], in_=rstd[:, hf * 512:(hf + 1) * 512],
                                                                func=AF.Exp, scale=-0.5)), [f"rstd{hf}"], [f"rstd{hf}"])
            for kc in range(KC):
                q = sq[kc % 2]
                P.op("dve", (lambda q=q, kc=kc: nc.vector.scalar_tensor_tensor(
                    out=q[:], in0=x[:, kc, :], scalar=scol(k)[:, kc:kc + 1], in1=rstd[:], op0=ALU.mult, op1=ALU.mult)),
                    ["x", "cols", "rstd0", "rstd1"], [f"sq{kc % 2}"])
                P.op("act", (lambda q=q, kc=kc: nc.scalar.activation(out=h[:, kc, :], in_=q[:], func=AF.Identity,
                                                                     bias=shcol(k)[:, kc:kc + 1], scale=1.0)),
                     [f"sq{kc % 2}", "modsel"], [f"h_{kc}"])

        def ffn(st, k, h, Wg, Wu, Wd):
            x = X["x"]
            wg = [sb(st, f"wg{k}_{i}", [128, KC, 256], BF16) for i in range(2)]
            wu = [sb(st, f"wu{k}_{i}", [128, KC, 256], BF16) for i in range(2)]
            wd = [sb(st, f"wd{k}_{i}", [128, 2, D], BF16) for i in range(2)]
            aT = [sb(st, f"aT{k}_{i}", [128, 2, T], BF16) for i in range(2)]
            sg = [sb(st, f"sg{k}_{i}", [128, 512]) for i in range(2)]
            hkeys = [f"h_{kc}" for kc in range(KC)]

            def load_gu(g):
                s = g % 2
                P.dma("pool", wg[s][:], Wg[:, g * 256:(g + 1) * 256].rearrange("(kc p) f -> p kc f", p=128), [], [f"wg{s}"])
                P.dma("pool", wu[s][:], Wu[:, g * 256:(g + 1) * 256].rearrange("(kc p) f -> p kc f", p=128), [], [f"wu{s}"])

            def load_d(g):
                s = g % 2
                P.dma("pool", wd[s][:], Wd[g * 256:(g + 1) * 256, :].rearrange("(fc p) d -> p fc d", p=128), [], [f"wd{s}"])

            def gateup(g):
                s = g % 2
                for fc in range(2):
                    for hf in range(2):
                        bG, bU = 2 * hf, 2 * hf + 1
                        tsl = slice(hf * 512, (hf + 1) * 512)
                        for kc in range(KC):
                            P.op("pe", (lambda kc=kc, fc=fc, tsl=tsl, bG=bG: nc.tensor.matmul(
                                ps[bG][:], lhsT=wg[s][:, kc, fc * 128:(fc + 1) * 128], rhs=h[:, kc, tsl],
                                start=(kc == 0), stop=(kc == KC - 1))), [f"wg{s}", hkeys[kc]], [f"ps{bG}"], sig=(kc == KC - 1))
                        for kc in range(KC):
                            P.op("pe", (lambda kc=kc, fc=fc, tsl=tsl, bU=bU: nc.tensor.matmul(
                                ps[bU][:], lhsT=wu[s][:, kc, fc * 128:(fc + 1) * 128], rhs=h[:, kc, tsl],
                                start=(kc == 0), stop=(kc == KC - 1))), [f"wu{s}", hkeys[kc]], [f"ps{bU}"], sig=(kc == KC - 1))
                        P.op("act", (lambda hf=hf, bG=bG: nc.scalar.activation(out=sg[hf][:], in_=ps[bG][:], func=AF.Silu)),
                             [f"ps{bG}"], [f"sg{hf}"])
                        P.op("dve", (lambda hf=hf, bU=bU, fc=fc, tsl=tsl: nc.vector.tensor_tensor(
                            out=aT[s][:, fc, tsl], in0=sg[hf][:], in1=ps[bU][:], op=ALU.mult)),
                            [f"sg{hf}", f"ps{bU}"], [f"aT{s}_{fc}_{hf}"])

            dcount = [0]

            def down(g):
                s = g % 2
                for dc in range(KC):
                    for hf in range(2):
                        bk = 4 + dcount[0] % 4
                        dcount[0] += 1
                        tsl = slice(hf * 512, (hf + 1) * 512)
                        for fc in range(2):
                            P.op("pe", (lambda fc=fc, dc=dc, tsl=tsl, bk=bk: nc.tensor.matmul(
                                ps[bk][:], lhsT=wd[s][:, fc, dc * 128:(dc + 1) * 128], rhs=aT[s][:, fc, tsl],
                                start=(fc == 0), stop=(fc == 1))), [f"wd{s}", f"aT{s}_{fc}_{hf}"], [f"ps{bk}"], sig=(fc == 1))
                        P.op("dve", (lambda dc=dc, tsl=tsl, bk=bk: nc.vector.scalar_tensor_tensor(
                            out=x[:, dc, tsl], in0=ps[bk][:], scalar=gcol(k)[:, dc:dc + 1], in1=x[:, dc, tsl],
                            op0=ALU.mult, op1=ALU.add)), [f"ps{bk}", "cols", "x"], ["x"])

            load_gu(0); load_d(0)
            if NG > 1:
                load_gu(1); load_d(1)
            gateup(0)
            for g in range(NG):
                if g + 1 < NG:
                    gateup(g + 1)
                if g + 2 < NG:
                    load_gu(g + 2)
                down(g)
                if g + 2 < NG:
                    load_d(g + 2)

        with contextlib.ExitStack() as st:
            h = sb(st, "h1", [128, KC, T], BF16)
            modulate(st, 0, h)
            if "h1" in debug:
                dbg["h1"] = dout("dbg_h1", [128, KC, T], BF16)
                P.dma("sp", dbg["h1"], h[:], [f"h_{kc}" for kc in range(KC)], ["dbg_h1"])
            ffn(st, 0, h, w1g_d, w1u_d, w1d_d)
            P.barrier()

        if stage <= 1:
            P.dma("sp", outT_d.rearrange("(kc p) t -> p kc t", p=128), x[:], ["x"], ["outT"])
            P.finish()
            xst.close()
            return nc

        ex1_in = [dint(f"ex1_in{p}", [512, T], BF16) for p in range(4)]
        ex1_out = [dint(f"ex1_out{p}", [4 * 512, T], BF16) for p in range(4)]
        ex2_in = [dint(f"ex2_in{q}", [512, T], BF16) for q in range(4)]
        ex2_out = dint("ex2_out", [4 * 2048, T], BF16)
        xspill = dint("xspill", [128, KC, T])
        with contextlib.ExitStack() as st:
            h2 = sb(st, "h2", [128, KC, T], BF16)
            modulate(st, 1, h2)
            for p in range(4):
                P.dma("sp", ex1_in[p].rearrange("(kc p) t -> p kc t", p=128), h2[:, 4 * p:4 * p + 4, :],
                      [f"h_{kc}" for kc in range(4 * p, 4 * p + 4)], [f"ex1in{p}"])
                P.op("pool", (lambda p=p: nc.gpsimd.collective_compute(
                    "AllGather", ALU.bypass, replica_groups=GROUPS4, ins=[ex1_in[p]], outs=[ex1_out[p]])),
                    [f"ex1in{p}"], [f"ex1out{p}"], dma=True, grp=f"cc_ex1_{p}", inc=1)
            P.dma("sp", xspill, X["x"][:], ["x"], ["xspill"])
            P.barrier(skip_cc=True)
        xst.close()

        def out_from_spill():
            with contextlib.ExitStack() as st:
                xt = sb(st, "xtmp", [128, 4, T])
                ov = outT_d.rearrange("(kc p) t -> p kc t", p=128)
                for i4 in range(4):
                    P.dma("sp", xt[:], xspill[:, 4 * i4:4 * i4 + 4, :], ["xspill"], ["xtmp"])
                    P.dma("sp", ov[:, 4 * i4:4 * i4 + 4, :], xt[:], ["xtmp"], ["outT"])
                P.finish()

        if "stopC" in debug:
            out_from_spill()
            return nc

        pos_d = din("pos", [1, S], I32)
        wA_d = din("wA", [2, D, 384])
        wD_d = din("wD", [2, D, 512])
        wab_d = din("wab", [D, 4])
        qkg_d = din("qkg", [128, 2])
        convc_d = din("convc", [128, 2, 3, 4])
        abp_d = din("abp", [4, 2])
        don_d = din("don", [128, 1])
        invf_d = din("invf", [128, 1])
        amask_d = din("amask", [128, 256], BF16)
        idx2_d = din("idx2", [128, 16], I32)
        wo_d = din("wo", [D, D])
        ISQ = float(128 ** -0.5)
        TWO_PI = float(2 * np.pi)
        C1 = 6.28125
        C2 = float(2 * np.pi - 6.28125)

        with contextlib.ExitStack() as mst:
            identb = sb(mst, "identb", [128, 128], BF16)
            onesb = sb(mst, "onesb", [128, 128], BF16)
            amask = sb(mst, "amask", [128, 256], BF16)
            qkg = sb(mst, "qkg", [128, 2])
            convc = sb(mst, "convc", [128, 2, 3, 4])
            abp = sb(mst, "abp", [4, 2])
            don = sb(mst, "don", [128, 1])
            invf = sb(mst, "invf", [128, 1])
            hb = [sb(mst, f"hb{i}", [128, KC, 512], BF16) for i in range(2)]
            ostg = sb(mst, "ostg", [128, T], BF16)
            P.op("act", lambda: nc.scalar.copy(out=identb[:], in_=ident), ["cf"], ["identb"])
            P.op("act", lambda: nc.scalar.copy(out=onesb[:], in_=ones), ["cf"], ["onesb"])
            P.dma("sp", amask[:], amask_d, [], ["amask"])
            P.dma("sp", qkg[:], qkg_d, [], ["qkg"])
            P.dma("sp", convc[:], convc_d, [], ["convc"])
            P.dma("sp", abp[:], abp_d, [], ["abp"])
            P.dma("sp", don[:], don_d, [], ["don"])
            P.dma("sp", invf[:], invf_d, [], ["invf"])

            def load_hb(blk):
                t = hb[blk % 2]
                q, hs = blk // 2, (blk % 2) * 512
                for p in range(4):
                    P.dma("sp", t[:, 4 * p:4 * p + 4, :],
                          ex1_out[p][q * 512:(q + 1) * 512, hs:hs + 512].rearrange("(kc p) t -> p kc t", p=128),
                          [f"ex1out{p}"], [f"hb{blk % 2}_{p}"], grp=f"d_hb{blk % 2}")
                return t

            def inproj(bank, w, c0, nco, hbt, hkey, wkey):
                for kc in range(KC):
                    P.op("pe", (lambda kc=kc: nc.tensor.matmul(ps[bank][0:nco, :], lhsT=w[:, kc, c0:c0 + nco], rhs=hbt[:, kc, :],
                                                               start=(kc == 0), stop=(kc == KC - 1))),
                         [wkey, f"{hkey}_{kc // 4}"], [f"ps{bank}"], sig=(kc == KC - 1))

            with contextlib.ExitStack() as rst:
                cosT = sb(rst, "cosT", [128, S])
                sinT = sb(rst, "sinT", [128, S])
                with contextlib.ExitStack() as st:
                    posi = sb(st, "posi", [128, T], I32)
                    ang = sb(st, "ang", [128, T])
                    kf_ = sb(st, "kf_", [128, T])
                    ki = sb(st, "ki", [128, T], I32)
                    for qb in range(4):
                        cs = slice(qb * T, (qb + 1) * T)
                        P.dma("sp", posi[:], pos_d[0:1, cs].partition_broadcast(128) if False else pos_d[0:1, cs].to_broadcast([128, T]), [], ["posi"])
                        P.op("dve", lambda: nc.vector.tensor_copy(out=ang[:], in_=posi[:]), ["posi"], ["ang"])
                        P.op("dve", lambda: nc.vector.tensor_scalar(out=ang[:], in0=ang[:], scalar1=invf[:, 0:1], scalar2=None, op0=ALU.mult),
                             ["ang", "invf"], ["ang"])
                        P.op("dve", lambda: nc.vector.tensor_scalar(out=ki[:], in0=ang[:], scalar1=1.0 / TWO_PI, scalar2=None, op0=ALU.mult),
                             ["ang"], ["ki"])
                        P.op("dve", lambda: nc.vector.tensor_copy(out=kf_[:], in_=ki[:]), ["ki"], ["kf_"])
                        P.op("dve", lambda: nc.vector.scalar_tensor_tensor(out=ang[:], in0=kf_[:], scalar=-C1, in1=ang[:], op0=ALU.mult, op1=ALU.add),
                             ["kf_", "ang"], ["ang"])
                        P.op("dve", lambda: nc.vector.scalar_tensor_tensor(out=ang[:], in0=kf_[:], scalar=-C2, in1=ang[:], op0=ALU.mult, op1=ALU.add),
                             ["kf_", "ang"], ["ang"])
                        P.op("dve", lambda: nc.vector.tensor_scalar(out=ang[:], in0=ang[:], scalar1=float(-np.pi), scalar2=float(np.pi),
                                                                    op0=ALU.max, op1=ALU.min), ["ang"], ["ang"])
                        P.op("act", (lambda cs=cs: nc.scalar.activation(out=sinT[:, cs], in_=ang[:], func=AF.Sin)), ["ang"], ["sinT"])
                        P.op("act", lambda: nc.scalar.activation(out=kf_[:], in_=ang[:], func=AF.Abs), ["ang"], ["kf_"])
                        P.op("act", (lambda cs=cs: nc.scalar.activation(out=cosT[:, cs], in_=kf_[:], func=AF.Sin, scale=-1.0, bias=epsc[:, 1:2])),
                             ["kf_", "epsc"], ["cosT"])
                    P.barrier(skip_cc=True)
                if "stopRope" in debug:
                    dbg["cos"] = dout("dbg_cos", [128, S]); dbg["sin"] = dout("dbg_sin", [128, S])
                    P.dma("sp", dbg["cos"], cosT[:], ["cosT"], ["dbg_cos"]); P.dma("sp", dbg["sin"], sinT[:], ["sinT"], ["dbg_sin"])
                    out_from_spill()
                    return nc

                for hl in range(2):
                    with contextlib.ExitStack() as st:
                        wA = sb(st, f"wA{hl}", [128, KC, 384], BF16)
                        qT = sb(st, f"qT{hl}", [128, S], BF16)
                        kT = sb(st, f"kT{hl}", [128, S], BF16)
                        vT = sb(st, f"vT{hl}", [128, S], BF16)
                        sqt_ = [sb(st, f"sqt{hl}_{i}", [128, 512]) for i in range(2)]
                        rst__ = [sb(st, f"rst{hl}_{i}", [128, 512]) for i in range(2)]
                        qn_ = [sb(st, f"qn{hl}_{i}", [128, 512]) for i in range(2)]
                        t1_ = [sb(st, f"t1{hl}_{i}", [128, 512]) for i in range(2)]
                        t2_ = [sb(st, f"t2{hl}_{i}", [128, 512]) for i in range(2)]
                        vd = sb(st, f"vd{hl}", [128, 32, 128], BF16)
                        accn = sb(st, f"accn{hl}", [128, S])
                        accd = sb(st, f"accd{hl}", [128, S])
                        pt = [sb(st, f"pt{hl}_{i}", [128, 256], BF16) for i in range(4)]
                        P.dma("pool", wA[:], wA_d[hl].rearrange("(kc p) f -> p kc f", p=128), [], ["wA"])
                        for blk in range(8):
                            hbt = load_hb(blk)
                            cs = slice(blk * 512, (blk + 1) * 512)
                            for t in range(3):
                                inproj(t, wA, t * 128, 128, hbt, f"hb{blk % 2}", "wA")
                            P.op("act", (lambda cs=cs: nc.scalar.copy(out=vT[:, cs], in_=ps[2][:])), ["ps2"], ["vT"])
                            for t in range(2):
                                P.op("act", (lambda t=t: nc.scalar.activation(out=sqt_[t][:], in_=ps[t][:], func=AF.Square)), [f"ps{t}"], [f"sqt{t}"])
                            for t in range(2):
                                P.op("pe", (lambda t=t: nc.tensor.matmul(ps[3 + t][:], lhsT=ones, rhs=sqt_[t][:], start=True, stop=True)), [f"sqt{t}", "cf"], [f"ps{3 + t}"])
                            for t in range(2):
                                P.op("act", (lambda t=t: nc.scalar.activation(out=rst__[t][:], in_=ps[3 + t][:], func=AF.Ln, bias=epsc[:, 0:1], scale=1.0 / 128)),
                                     [f"ps{3 + t}", "epsc"], [f"rst{t}"])
                            for t in range(2):
                                P.op("act", (lambda t=t: nc.scalar.activation(out=rst__[t][:], in_=rst__[t][:], func=AF.Exp, scale=-0.5)), [f"rst{t}"], [f"rst{t}"])
                            for t in range(2):
                                P.op("dve", (lambda t=t: nc.vector.scalar_tensor_tensor(
                                    out=qn_[t][:], in0=ps[t][:], scalar=qkg[:, t:t + 1], in1=rst__[t][:], op0=ALU.mult, op1=ALU.mult)),
                                    [f"ps{t}", "qkg", f"rst{t}"], [f"qn{t}"])
                            for t in range(2):
                                P.op("pe", (lambda t=t: nc.tensor.matmul(ps[3 + t][:], lhsT=cf[:, 2, :], rhs=qn_[t][:], start=True, stop=True)), [f"qn{t}", "cf"], [f"ps{3 + t}"])
                            for t in range(2):
                                P.op("dve", (lambda cs=cs, t=t: nc.vector.tensor_tensor(out=t1_[t][:], in0=qn_[t][:], in1=cosT[:, cs], op=ALU.mult)), [f"qn{t}", "cosT"], [f"t1{t}"])
                            for t in range(2):
                                P.op("dve", (lambda cs=cs, t=t: nc.vector.tensor_tensor(out=t2_[t][:], in0=ps[3 + t][:], in1=sinT[:, cs], op=ALU.mult)), [f"ps{3 + t}", "sinT"], [f"t2{t}"])
                            for t in range(2):
                                dst = qT if t == 0 else kT
                                P.op("dve", (lambda cs=cs, dst=dst, t=t: nc.vector.tensor_tensor(out=dst[:, cs], in0=t1_[t][:], in1=t2_[t][:], op=ALU.add)),
                                     [f"t1{t}", f"t2{t}"], ["qT" if t == 0 else "kT"])
                        if f"qk{hl}" in debug:
                            dbg["qT"] = dout("dbg_qT", [128, S], BF16); dbg["kT"] = dout("dbg_kT", [128, S], BF16)
                            P.dma("sp", dbg["qT"], qT[:], ["qT"], ["dbg_qT"]); P.dma("sp", dbg["kT"], kT[:], ["kT"], ["dbg_kT"])
                        if "stopInproj" in debug:
                            out_from_spill()
                            return nc
                        kcount = 0
                        for d in ((1,) if "d1only" in debug else (1, 4, 16)):
                            nbk = 32 // d
                            for bi0 in range(0, 32, 4):
                                for u in range(4):
                                    r, n = divmod(bi0 + u, nbk)
                                    a0 = r + d * 128 * n
                                    P.op("pe", (lambda u=u, a0=a0, d=d: nc.tensor.matmul(
                                        ps[6][:, u * 128:(u + 1) * 128], lhsT=vT[:, a0:a0 + d * 127 + 1:d], rhs=identb[:],
                                        start=True, stop=True)), ["vT", "identb"], ["ps6"], sig=(u == 3))
                                P.op("act", (lambda bi0=bi0: nc.scalar.copy(out=vd[:, bi0:bi0 + 4, :],
                                                                            in_=ps[6][:].rearrange("p (u e) -> p u e", e=128))),
                                     ["ps6"], [f"vd{bi0}"])
                            blocks = [(r, n) for r in range(d) for n in range(nbk)]

                            def emit_S(r, n, kc_):
                                a0 = r + d * 128 * n
                                nq = 256 if n + 1 < nbk else 128
                                sbank = 4 + kc_ % 2
                                ptt = pt[kc_ % 4]
                                pkey = f"pt{kc_ % 4}"
                                sap = ps[sbank][:, 0:nq]
                                skey = f"ps{sbank}"
                                P.op("pe", (lambda: nc.tensor.matmul(
                                    sap, lhsT=kT[:, a0:a0 + d * 127 + 1:d], rhs=qT[:, a0:a0 + d * (nq - 1) + 1:d],
                                    start=True, stop=False)), ["kT", "qT"], [skey], sig=False)
                                P.op("pe", (lambda: nc.tensor.matmul(
                                    sap, lhsT=identb[:], rhs=amask[:, 0:nq], start=False, stop=True)), ["identb", "amask"], [skey])
                                P.op("act", (lambda: nc.scalar.activation(out=ptt[:, 0:nq], in_=sap, func=AF.Exp, scale=ISQ)),
                                     [skey], [pkey])

                            def emit_PV(r, n, kc_):
                                a0 = r + d * 128 * n
                                bi = r * nbk + n
                                ptt = pt[kc_ % 4]
                                pkey = f"pt{kc_ % 4}"
                                nb_, db_ = n % 2, 2 + n % 2
                                P.op("pe", (lambda: nc.tensor.matmul(
                                    ps[nb_][:, 0:128], lhsT=vd[:, bi, :], rhs=ptt[:, 0:128],
                                    start=(n == 0), stop=True)), [f"vd{bi - bi % 4}", pkey], [f"ps{nb_}"])
                                P.op("pe", (lambda: nc.tensor.matmul(
                                    ps[db_][:, 0:128], lhsT=onesb[:], rhs=ptt[:, 0:128],
                                    start=(n == 0), stop=True)), ["onesb", pkey], [f"ps{db_}"])
                                if n + 1 < nbk:
                                    nb2, db2 = (n + 1) % 2, 2 + (n + 1) % 2
                                    P.op("pe", (lambda: nc.tensor.matmul(
                                        ps[nb2][:, 0:128], lhsT=vd[:, bi, :], rhs=ptt[:, 128:256],
                                        start=True, stop=False)), [f"vd{bi - bi % 4}", pkey], [f"ps{nb2}"], sig=False)
                                    P.op("pe", (lambda: nc.tensor.matmul(
                                        ps[db2][:, 0:128], lhsT=onesb[:], rhs=ptt[:, 128:256],
                                        start=True, stop=False)), ["onesb", pkey], [f"ps{db2}"], sig=False)
                                tsl = slice(a0, a0 + d * 127 + 1, d)
                                if d == 1:
                                    P.op("act", (lambda: nc.scalar.copy(out=accn[:, tsl], in_=ps[nb_][:, 0:128])),
                                         [f"ps{nb_}"], ["accn"])
                                    P.op("dve", (lambda: nc.vector.tensor_copy(out=accd[:, tsl], in_=ps[db_][:, 0:128])),
                                         [f"ps{db_}"], ["accd"])
                                else:
                                    P.op("dve", (lambda: nc.vector.tensor_tensor(
                                        out=accn[:, tsl], in0=accn[:, tsl], in1=ps[nb_][:, 0:128], op=ALU.add)),
                                        [f"ps{nb_}", "accn"], ["accn"])
                                    P.op("dve", (lambda: nc.vector.tensor_tensor(
                                        out=accd[:, tsl], in0=accd[:, tsl], in1=ps[db_][:, 0:128], op=ALU.add)),
                                        [f"ps{db_}", "accd"], ["accd"])

                            emit_S(*blocks[0], kcount)
                            for i_, (r, n) in enumerate(blocks):
                                if i_ + 1 < len(blocks):
                                    emit_S(*blocks[i_ + 1], kcount + i_ + 1)
                                emit_PV(r, n, kcount + i_)
                            kcount += len(blocks)
                        for qb in range(4):
                            cs = slice(qb * T, (qb + 1) * T)
                            P.op("dve", (lambda cs=cs: nc.vector.reciprocal(out=accd[:, cs], in_=accd[:, cs])), ["accd"], ["accd"])
                            P.op("dve", (lambda cs=cs: nc.vector.tensor_tensor(out=ostg[:], in0=accn[:, cs], in1=accd[:, cs], op=ALU.mult)),
                                 ["accn", "accd"], ["ostg"])
                            P.dma("sp", ex2_in[qb][hl * 128:(hl + 1) * 128, :], ostg[:], ["ostg"], [f"ex2in{qb}"], grp="w_ex2in")
                        P.barrier()
            if "stopAttn" in debug:
                out_from_spill()
                return nc
            with contextlib.ExitStack() as dst_:
                abrow = sb(dst_, "abrow", [4, S])
                G4 = sb(dst_, "G4", [4, S])
                wabt = sb(dst_, "wabt", [128, KC, 4], BF16)
                negexp = sb(dst_, "negexp", [4, 1])
                idt8 = sb(dst_, "idt8", [64, 8, 64])
                U64 = cf[0:64, 3, 0:64]
                pm_sl = cf[0:64, 4, 0:64]
                pm_ut = cf[0:64, 5, 0:64]
                id64 = cf[0:64, 0, 0:64]
                P.dma("pool", wabt[:], wab_d.rearrange("(kc p) f -> p kc f", p=128), [], ["wabt"])
                for i8 in range(8):
                    P.op("dve", (lambda i8=i8: nc.vector.tensor_copy(out=idt8[:, i8, :], in_=id64)), ["cf"], ["idt8"])
                for hl in range(2):
                    with contextlib.ExitStack() as st:
                        qf = sb(st, f"qf{hl}", [128, S])
                        kf = sb(st, f"kf{hl}", [128, S])
                        vf = sb(st, f"vf{hl}", [128, S])
                        sz = sb(st, f"sz{hl}", [128, S])
                        ist = contextlib.ExitStack()
                        wDt = sb(ist, f"wDt{hl}", [128, KC, 512], BF16)
                        raw = [sb(ist, f"raw{hl}_{t}", [128, 515]) for t in range(3)]
                        ctmp_ = [sb(ist, f"ctmp{hl}_{i}", [128, 512]) for i in range(3)]
                        ctm2_ = [sb(ist, f"ctm2{hl}_{i}", [128, 512]) for i in range(2)]
                        sqd_ = [sb(ist, f"sqd{hl}_{i}", [128, 512]) for i in range(2)]
                        rsd_ = [sb(ist, f"rsd{hl}_{i}", [128, 512]) for i in range(2)]
                        P.dma("pool", wDt[:], wD_d[hl].rearrange("(kc p) f -> p kc f", p=128), [], ["wDt"])
                        for t in range(3):
                            P.op("dve", (lambda t=t: nc.vector.memset(raw[t][:, 0:3], 0.0)), [], [f"raw{t}"])
                        for blk in range(8):
                            hbt = load_hb(blk)
                            cs = slice(blk * 512, (blk + 1) * 512)
                            for t in range(4):
                                inproj(t, wDt, t * 128, 128, hbt, f"hb{blk % 2}", "wDt")
                            if hl == 0:
                                for kc in range(KC):
                                    P.op("pe", (lambda kc=kc, hbt=hbt: nc.tensor.matmul(ps[6][0:4, :], lhsT=wabt[:, kc, 0:4], rhs=hbt[:, kc, :],
                                                                                       start=(kc == 0), stop=(kc == KC - 1))),
                                         ["wabt", f"hb{blk % 2}_{kc // 4}"], ["ps6"], sig=(kc == KC - 1))
                            cw = lambda t, j: convc[:, hl, t, j:j + 1]
                            for t in range(3):
                                P.op("act", (lambda t=t: nc.scalar.copy(out=raw[t][:, 3:515], in_=ps[t][:])), [f"ps{t}"], [f"raw{t}"])
                            for t in range(3):
                                P.op("act", (lambda t=t: nc.scalar.activation(out=ctmp_[t][:], in_=raw[t][:, 3:515], func=AF.Copy, scale=cw(t, 3))),
                                     [f"raw{t}", "convc"], [f"ctmp{t}"])
                            if hl == 0:
                                P.op("act", (lambda cs=cs: nc.scalar.copy(out=abrow[:, cs], in_=ps[6][0:4, :])), ["ps6"], ["abrow"])
                            for j in (2, 1, 0):
                                for t in range(3):
                                    P.op("dve", (lambda t=t, j=j: nc.vector.scalar_tensor_tensor(
                                        out=ctmp_[t][:], in0=raw[t][:, j:j + 512], scalar=cw(t, j), in1=ctmp_[t][:], op0=ALU.mult, op1=ALU.add)),
                                        [f"raw{t}", "convc", f"ctmp{t}"], [f"ctmp{t}"])
                            for t in range(3):
                                P.op("act", (lambda t=t: nc.scalar.copy(out=raw[t][:, 0:3], in_=raw[t][:, 512:515])), [f"raw{t}"], [f"raw{t}"])
                            P.op("act", (lambda cs=cs: nc.scalar.activation(out=sz[:, cs], in_=ps[3][:], func=AF.Silu)), ["ps3"], ["sz"])
                            P.op("act", (lambda cs=cs: nc.scalar.activation(out=vf[:, cs], in_=ctmp_[2][:], func=AF.Silu)), ["ctmp2"], ["vf"])
                            for t in range(2):
                                P.op("act", (lambda t=t: nc.scalar.activation(out=ctm2_[t][:], in_=ctmp_[t][:], func=AF.Silu)), [f"ctmp{t}"], [f"ctm2{t}"])
                            for t in range(2):
                                P.op("act", (lambda t=t: nc.scalar.activation(out=sqd_[t][:], in_=ctm2_[t][:], func=AF.Square)), [f"ctm2{t}"], [f"sqd{t}"])
                            for t in range(2):
                                P.op("pe", (lambda t=t: nc.tensor.matmul(ps[4 + t][:], lhsT=ones, rhs=sqd_[t][:], start=True, stop=True)), [f"sqd{t}", "cf"], [f"ps{4 + t}"])
                            for t in range(2):
                                P.op("act", (lambda t=t: nc.scalar.activation(out=rsd_[t][:], in_=ps[4 + t][:], func=AF.Ln, bias=epsc[:, 0:1], scale=1.0)),
                                     [f"ps{4 + t}", "epsc"], [f"rsd{t}"])
                            for t in range(2):
                                P.op("act", (lambda t=t: nc.scalar.activation(out=rsd_[t][:], in_=rsd_[t][:], func=AF.Exp, scale=-0.5)), [f"rsd{t}"], [f"rsd{t}"])
                            for t in range(2):
                                dstt = qf if t == 0 else kf
                                P.op("dve", (lambda cs=cs, dstt=dstt, t=t: nc.vector.tensor_tensor(out=dstt[:, cs], in0=ctm2_[t][:], in1=rsd_[t][:], op=ALU.mult)),
                                     [f"ctm2{t}", f"rsd{t}"], ["qf" if t == 0 else "kf"])
                        if hl == 0:
                            P.op("act", lambda: nc.scalar.activation(out=negexp[:], in_=abp[:, 0:1], func=AF.Exp), ["abp"], ["negexp"])
                            P.op("dve", lambda: nc.vector.tensor_scalar(out=negexp[:], in0=negexp[:], scalar1=-1.0, scalar2=None, op0=ALU.mult),
                                 ["negexp"], ["negexp"])
                            P.op("act", lambda: nc.scalar.activation(out=G4[:], in_=abrow[:], func=AF.Exp, bias=abp[:, 1:2], scale=1.0), ["abrow", "abp"], ["G4"])
                            P.op("act", lambda: nc.scalar.activation(out=G4[:], in_=G4[:], func=AF.Ln, bias=epsc[0:4, 2:3], scale=1.0), ["G4", "epsc"], ["G4"])
                            P.op("dve", lambda: nc.vector.tensor_scalar(out=G4[:], in0=G4[:], scalar1=negexp[:, 0:1], scalar2=None, op0=ALU.mult),
                                 ["G4", "negexp"], ["G4"])
                            P.op("act", lambda: nc.scalar.activation(out=abrow[:], in_=abrow[:], func=AF.Sigmoid), ["abrow"], ["abrow"])
                            if "gb" in debug:
                                dbg["G4"] = dout("dbg_G4", [4, S]); dbg["S4"] = dout("dbg_S4", [4, S]); S4 = abrow
                                P.dma("sp", dbg["G4"], G4[:], ["G4"], ["dbg_G4"]); P.dma("sp", dbg["S4"], abrow[:], ["abrow"], ["dbg_S4"])
                        if f"dqkv{hl}" in debug:
                            for nm, tt in (("qf", qf), ("kf", kf), ("vf", vf)):
                                dbg[nm] = dout("dbg_" + nm, [128, S])
                                P.dma("sp", dbg[nm], tt[:], [nm], ["dbg_" + nm])
                        P.barrier()
                        ist.close()
                        gb = sb(st, f"gb{hl}", [64, 128])
                        gct = sb(st, f"gct{hl}", [64, 64])
                        glb = sb(st, f"glb{hl}", [128, 64])
                        egc = sb(st, f"egc{hl}", [64, 64])
                        bgc = sb(st, f"bgc{hl}", [64, 64])
                        ekt = sb(st, f"ekt{hl}", [64, 64])
                        decb = sb(st, f"decb{hl}", [128, 64])
                        for c in range(64):
                            P.op("pe", (lambda c=c: nc.tensor.matmul(ps[0][0:64, 2 * c:2 * c + 1], lhsT=G4[0:4, c * 64:(c + 1) * 64],
                                                                     rhs=cf[0:4, 0, hl:hl + 1], start=True, stop=True)), ["G4", "cf"], ["ps0"], sig=False)
                            P.op("pe", (lambda c=c: nc.tensor.matmul(ps[0][0:64, 2 * c + 1:2 * c + 2], lhsT=abrow[0:4, c * 64:(c + 1) * 64],
                                                                     rhs=cf[0:4, 0, 2 + hl:3 + hl], start=True, stop=True)), ["abrow", "cf"], ["ps0"], sig=(c == 63))
                        P.op("dve", lambda: nc.vector.tensor_copy(out=gb[:], in_=ps[0][0:64, 0:128]), ["ps0"], ["gb"])
                        P.op("pe", lambda: nc.tensor.matmul(ps[1][0:64, 0:64], lhsT=U64, rhs=gb[:, 0:128:2], start=True, stop=True), ["gb", "cf"], ["ps1"])
                        P.op("pe", lambda: nc.tensor.matmul(ps[2][:, 0:64], lhsT=cf[0:64, 1, :], rhs=gb[:, 0:128:2], start=True, stop=True), ["gb", "cf"], ["ps2"])
                        P.op("dve", lambda: nc.vector.tensor_copy(out=gct[:], in_=ps[1][0:64, 0:64]), ["ps1"], ["gct"])
                        P.op("dve", lambda: nc.vector.tensor_copy(out=glb[:], in_=ps[2][:, 0:64]), ["ps2"], ["glb"])
                        P.op("act", lambda: nc.scalar.activation(out=egc[:], in_=gct[:], func=AF.Exp), ["gct"], ["egc"])
                        P.op("dve", lambda: nc.vector.tensor_tensor(out=bgc[:], in0=egc[:], in1=gb[:, 1:128:2], op=ALU.mult), ["egc", "gb"], ["bgc"])
                        P.op("dve", lambda: nc.vector.tensor_tensor(out=ekt[:], in0=glb[0:64, :], in1=gct[:], op=ALU.subtract), ["glb", "gct"], ["ekt"])
                        P.op("act", lambda: nc.scalar.activation(out=ekt[:], in_=ekt[:], func=AF.Exp), ["ekt"], ["ekt"])
                        P.op("act", lambda: nc.scalar.activation(out=decb[:], in_=glb[:], func=AF.Exp), ["glb"], ["decb"])
                        eg = sb(st, f"eg{hl}", [128, 512])
                        tl = sb(st, f"tl{hl}", [64, 512])
                        tu = sb(st, f"tu{hl}", [64, 512])
                        al = [sb(st, f"al{hl}_{i}", [64, 512]) for i in range(2)]
                        bm = [sb(st, f"bm{hl}_{i}", [64, 512]) for i in range(2)]
                        nn = [sb(st, f"nn{hl}_{i}", [64, 512]) for i in range(2)]
                        kbg = sb(st, f"kbg{hl}", [64, 8, 128])
                        vb = sb(st, f"vb{hl}", [64, 8, 128])
                        qdT = [sb(st, f"qdT{hl}_{i}", [128, 512], BF16) for i in range(2)]
                        itT = [sb(st, f"itT{hl}_{i}", [64, 512], BF16) for i in range(2)]
                        ktb = [sb(st, f"ktb{hl}_{i}", [64, 8, 128], BF16) for i in range(2)]
                        ub = [sb(st, f"ub{hl}_{i}", [64, 8, 128]) for i in range(2)]
                        wTb = [sb(st, f"wTb{hl}_{i}", [128, 512], BF16) for i in range(2)]
                        Sst = sb(st, f"Sst{hl}", [128, 128])
                        vn = [sb(st, f"vn{hl}_{i}", [64, 128], BF16) for i in range(2)]
                        Sbf = sb(st, f"Sbf{hl}", [128, 128], BF16)
                        on = sb(st, f"on{hl}", [64, 128])
                        osq = sb(st, f"osq{hl}", [64, 128])
                        ssc = sb(st, f"ssc{hl}", [64, 2])
                        P.op("dve", lambda: nc.vector.memset(Sst[:], 0.0), [], ["Sst"])
                        P.op("dve", lambda: nc.vector.memset(Sbf[:], 0.0), [], ["Sbf"])
                        pbank = [0]

                        def nbk_():
                            b = pbank[0] % 4
                            pbank[0] += 1
                            return b

                        binfo = {}

                        def prep_gen(bt):
                            pp = bt % 2
                            c0 = bt * 8
                            tc = lambda ci, bt=bt: slice(bt * 512 + ci * 64, bt * 512 + (ci + 1) * 64)
                            cc = lambda ci: slice(ci * 64, (ci + 1) * 64)
                            bs = slice(bt * 512, (bt + 1) * 512)
                            yield
                            bA = nbk_()
                            for ci in range(8):
                                P.op("pe", (lambda ci=ci, bA=bA: nc.tensor.matmul(
                                    ps[bA][:, cc(ci)], lhsT=gb[:, 2 * (c0 + ci):2 * (c0 + ci) + 1].to_broadcast([64, 128]), rhs=U64,
                                    start=True, stop=True)), ["gb", "cf"], [f"ps{bA}"], sig=(ci == 7))
                            P.op("act", (lambda bA=bA: nc.scalar.activation(out=eg[:], in_=ps[bA][:], func=AF.Exp)), [f"ps{bA}"], ["eg"])
                            P.op("dve", (lambda bs=bs, pp=pp: nc.vector.scalar_tensor_tensor(out=qdT[pp][:], in0=qf[:, bs], scalar=ISQ, in1=eg[:],
                                                                                            op0=ALU.mult, op1=ALU.mult)), ["qf", "eg"], [f"qdT{pp}"])
                            for ci in range(8):
                                P.op("dve", (lambda ci=ci, bA=bA: nc.vector.scalar_tensor_tensor(
                                    out=tl[:, cc(ci)], in0=ps[bA][0:64, cc(ci)], scalar=gct[:, c0 + ci:c0 + ci + 1], in1=pm_sl,
                                    op0=ALU.subtract, op1=ALU.add)), [f"ps{bA}", "gct", "cf"], ["tl"])
                                P.op("dve", (lambda ci=ci, bA=bA: nc.vector.scalar_tensor_tensor(
                                    out=tu[:, cc(ci)], in0=ps[bA][0:64, cc(ci)], scalar=gct[:, c0 + ci:c0 + ci + 1], in1=pm_ut,
                                    op0=ALU.subtract, op1=ALU.subtract)), [f"ps{bA}", "gct", "cf"], ["tu"])
                            P.op("act", lambda: nc.scalar.activation(out=tl[:], in_=tl[:], func=AF.Exp, scale=-1.0), ["tl"], ["tl"])
                            P.op("act", lambda: nc.scalar.activation(out=tu[:], in_=tu[:], func=AF.Exp), ["tu"], ["tu"])
                            yield
                            bB = nbk_()
                            for ci in range(8):
                                P.op("pe", (lambda ci=ci, bB=bB: nc.tensor.matmul(ps[bB][0:64, cc(ci)], lhsT=kf[:, tc(ci)], rhs=kf[:, tc(ci)],
                                                                                  start=True, stop=True)), ["kf"], [f"ps{bB}"], sig=(ci == 7))
                            for ci in range(8):
                                P.op("dve", (lambda ci=ci, bB=bB: nc.vector.scalar_tensor_tensor(
                                    out=al[0][:, cc(ci)], in0=ps[bB][0:64, cc(ci)], scalar=gb[:, 2 * (c0 + ci) + 1:2 * (c0 + ci) + 2], in1=tl[:, cc(ci)],
                                    op0=ALU.mult, op1=ALU.mult)), [f"ps{bB}", "gb", "tl"], ["al0"])
                            yield
                            bC = nbk_()
                            for ci in range(8):
                                P.op("pe", (lambda ci=ci, bC=bC: nc.tensor.matmul(ps[bC][0:64, cc(ci)], lhsT=kf[:, tc(ci)], rhs=qf[:, tc(ci)],
                                                                                  start=True, stop=True)), ["kf", "qf"], [f"ps{bC}"], sig=(ci == 7))
                            P.op("dve", (lambda bC=bC, pp=pp: nc.vector.scalar_tensor_tensor(out=itT[pp][:], in0=ps[bC][0:64, :], scalar=ISQ, in1=tu[:],
                                                                                            op0=ALU.mult, op1=ALU.mult)), [f"ps{bC}", "tu"], [f"itT{pp}"])
                            yield
                            bD = nbk_()
                            for ci in range(8):
                                P.op("pe", (lambda ci=ci, bD=bD: nc.tensor.matmul(ps[bD][0:64, cc(ci)], lhsT=al[0][:, cc(ci)], rhs=id64,
                                                                                  start=True, stop=True)), ["al0", "cf"], [f"ps{bD}"], sig=(ci == 7))
                            P.op("act", (lambda bD=bD: nc.scalar.copy(out=bm[0][:], in_=ps[bD][0:64, :])), [f"ps{bD}"], ["bm0"])
                            P.op("dve", (lambda bD=bD: nc.vector.scalar_tensor_tensor(out=nn[0][:], in0=ps[bD][0:64, :], scalar=-1.0,
                                                                                     in1=idt8[:].rearrange("p a b -> p (a b)"), op0=ALU.mult, op1=ALU.add)),
                                 [f"ps{bD}", "idt8"], ["nn0"])
                            yield
                            cur = 0
                            for s_ in range(1, 6):
                                nx = 1 - cur
                                bE = nbk_()
                                for ci in range(8):
                                    P.op("pe", (lambda ci=ci, bE=bE, cur=cur: nc.tensor.matmul(ps[bE][0:64, cc(ci)], lhsT=bm[cur][:, cc(ci)], rhs=al[cur][:, cc(ci)],
                                                                                              start=True, stop=True)), [f"bm{cur}", f"al{cur}"], [f"ps{bE}"], sig=(ci == 7))
                                P.op("act", (lambda bE=bE, nx=nx: nc.scalar.copy(out=al[nx][:], in_=ps[bE][0:64, :])), [f"ps{bE}"], [f"al{nx}"])
                                if s_ < 5:
                                    bF = nbk_()
                                    for ci in range(8):
                                        P.op("pe", (lambda ci=ci, bF=bF, cur=cur: nc.tensor.matmul(ps[bF][0:64, cc(ci)], lhsT=al[cur][:, cc(ci)], rhs=bm[cur][:, cc(ci)],
                                                                                                  start=True, stop=True)), [f"bm{cur}", f"al{cur}"], [f"ps{bF}"], sig=(ci == 7))
                                    P.op("dve", (lambda bF=bF, nx=nx: nc.vector.tensor_copy(out=bm[nx][:], in_=ps[bF][0:64, :])), [f"ps{bF}"], [f"bm{nx}"])
                                bG_ = nbk_()
                                for ci in range(8):
                                    P.op("pe", (lambda ci=ci, bG_=bG_, cur=cur, nx=nx: nc.tensor.matmul(ps[bG_][0:64, cc(ci)], lhsT=al[nx][:, cc(ci)], rhs=nn[cur][:, cc(ci)],
                                                                                                       start=True, stop=True)), [f"al{nx}", f"nn{cur}"], [f"ps{bG_}"], sig=(ci == 7))
                                P.op("dve", (lambda bG_=bG_, cur=cur, nx=nx: nc.vector.tensor_tensor(out=nn[nx][:], in0=nn[cur][:], in1=ps[bG_][0:64, :], op=ALU.add)),
                                     [f"ps{bG_}", f"nn{cur}"], [f"nn{nx}"])
                                cur = nx
                                yield
                            nfin = nn[cur]
                            nkey = f"nn{cur}"
                            yield
                            for half in range(2):
                                bH = nbk_()
                                for u in range(4):
                                    ci = half * 4 + u
                                    P.op("pe", (lambda ci=ci, u=u, bH=bH: nc.tensor.matmul(ps[bH][0:64, u * 128:(u + 1) * 128], lhsT=kf[:, tc(ci)], rhs=ident,
                                                                                          start=True, stop=True)), ["kf", "cf"], [f"ps{bH}"], sig=(u == 3))
                                for u in range(4):
                                    ci = half * 4 + u
                                    P.op("dve", (lambda ci=ci, u=u, bH=bH: nc.vector.tensor_scalar(
                                        out=kbg[:, ci, :], in0=ps[bH][0:64, u * 128:(u + 1) * 128], scalar1=bgc[:, c0 + ci:c0 + ci + 1], scalar2=None, op0=ALU.mult)),
                                        [f"ps{bH}", "bgc"], ["kbg"])
                                    P.op("act", (lambda ci=ci, u=u, bH=bH, pp=pp: nc.scalar.activation(
                                        out=ktb[pp][:, ci, :], in_=ps[bH][0:64, u * 128:(u + 1) * 128], func=AF.Copy, scale=ekt[:, c0 + ci:c0 + ci + 1])),
                                        [f"ps{bH}", "ekt"], [f"ktb{pp}"])
                                bH = nbk_()
                                for u in range(4):
                                    ci = half * 4 + u
                                    P.op("pe", (lambda ci=ci, u=u, bH=bH: nc.tensor.matmul(ps[bH][0:64, u * 128:(u + 1) * 128], lhsT=vf[:, tc(ci)], rhs=ident,
                                                                                          start=True, stop=True)), ["vf", "cf"], [f"ps{bH}"], sig=(u == 3))
                                for u in range(4):
                                    ci = half * 4 + u
                                    P.op("dve", (lambda ci=ci, u=u, bH=bH: nc.vector.tensor_scalar(
                                        out=vb[:, ci, :], in0=ps[bH][0:64, u * 128:(u + 1) * 128], scalar1=gb[:, 2 * (c0 + ci) + 1:2 * (c0 + ci) + 2], scalar2=None, op0=ALU.mult)),
                                        [f"ps{bH}", "gb"], ["vb"])
                            yield
                            for half in range(2):
                                bU_ = nbk_()
                                for u in range(4):
                                    ci = half * 4 + u
                                    P.op("pe", (lambda ci=ci, u=u, bU_=bU_: nc.tensor.matmul(ps[bU_][0:64, u * 128:(u + 1) * 128], lhsT=nfin[:, cc(ci)], rhs=vb[:, ci, :],
                                                                                            start=True, stop=True)), [nkey, "vb"], [f"ps{bU_}"], sig=(u == 3))
                                P.op("act", (lambda half=half, bU_=bU_, pp=pp: nc.scalar.copy(out=ub[pp][:, half * 4:half * 4 + 4, :],
                                                                                              in_=ps[bU_][0:64, :].rearrange("p (u e) -> p u e", e=128))),
                                     [f"ps{bU_}"], [f"ub{pp}"])
                            bW = nbk_()
                            for ci in range(8):
                                P.op("pe", (lambda ci=ci, bW=bW: nc.tensor.matmul(ps[bW][:, cc(ci)], lhsT=kbg[:, ci, :], rhs=nfin[:, cc(ci)],
                                                                                  start=True, stop=True)), [nkey, "kbg"], [f"ps{bW}"], sig=(ci == 7))
                            P.op("dve", (lambda bW=bW, pp=pp: nc.vector.tensor_copy(out=wTb[pp][:], in_=ps[bW][:])), [f"ps{bW}"], [f"wTb{pp}"])
                            binfo[bt] = (nfin, nkey)
                            yield

                        def scan_gen(bt):
                            pp = bt % 2
                            c0 = bt * 8
                            cc = lambda ci: slice(ci * 64, (ci + 1) * 64)
                            for ci in range(8):
                                c = c0 + ci
                                vv = vn[c % 2]
                                vk = f"vn{c % 2}"
                                P.op("pe", (lambda ci=ci, pp=pp: nc.tensor.matmul(ps[4][0:64, 0:128], lhsT=wTb[pp][:, cc(ci)], rhs=Sbf[:], start=True, stop=True)),
                                     [f"wTb{pp}", "Sbf"], ["ps4"])
                                P.op("dve", (lambda ci=ci, pp=pp, vv=vv: nc.vector.tensor_tensor(out=vv[:], in0=ub[pp][:, ci, :], in1=ps[4][0:64, 0:128], op=ALU.subtract)),
                                     ["ps4", f"ub{pp}"], [vk])
                                yield
                                P.op("pe", (lambda ci=ci, pp=pp: nc.tensor.matmul(ps[5][0:64, 0:128], lhsT=qdT[pp][:, cc(ci)], rhs=Sbf[:], start=True, stop=False)),
                                     [f"qdT{pp}", "Sbf"], ["ps5"], sig=False)
                                P.op("pe", (lambda ci=ci, pp=pp, vv=vv: nc.tensor.matmul(ps[5][0:64, 0:128], lhsT=itT[pp][:, cc(ci)], rhs=vv[:], start=False, stop=True)),
                                     [f"itT{pp}", vk], ["ps5"])
                                P.op("pe", (lambda ci=ci, pp=pp, vv=vv: nc.tensor.matmul(ps[6][:, 0:128], lhsT=ktb[pp][:, ci, :], rhs=vv[:], start=True, stop=True)),
                                     [f"ktb{pp}", vk], ["ps6"])
                                P.op("dve", (lambda c=c: nc.vector.scalar_tensor_tensor(out=Sbf[:], in0=Sst[:], scalar=decb[:, c:c + 1], in1=ps[6][:, 0:128],
                                                                                        op0=ALU.mult, op1=ALU.add)), ["ps6", "decb", "Sst"], ["Sbf"])
                                P.op("dve", (lambda c=c: nc.vector.scalar_tensor_tensor(out=Sst[:], in0=Sst[:], scalar=decb[:, c:c + 1], in1=ps[6][:, 0:128],
                                                                                        op0=ALU.mult, op1=ALU.add)), ["ps6", "decb", "Sst"], ["Sst"])
                                yield
                                P.op("act", lambda: nc.scalar.activation(out=osq[:], in_=ps[5][0:64, 0:128], func=AF.Square, accum_out=ssc[:, 0:1]),
                                     ["ps5"], ["osq", "ssc"])
                                P.op("act", lambda: nc.scalar.activation(out=ssc[:, 1:2], in_=ssc[:, 0:1], func=AF.Sqrt, bias=epsc[0:64, 0:1], scale=1.0 / 128),
                                     ["ssc", "epsc"], ["ssc"])
                                P.op("dve", lambda: nc.vector.reciprocal(out=ssc[:, 1:2], in_=ssc[:, 1:2]), ["ssc"], ["ssc"])
                                P.op("dve", lambda: nc.vector.tensor_scalar(out=on[:], in0=ps[5][0:64, 0:128], scalar1=ssc[:, 1:2], scalar2=None, op0=ALU.mult),
                                     ["ps5", "ssc"], ["on"])
                                P.op("pe", lambda: nc.tensor.matmul(ps[7][:, 0:64], lhsT=on[:], rhs=id64, start=True, stop=True), ["on", "cf"], ["ps7"])
                                ocs = slice((c % 16) * 64, (c % 16) * 64 + 64)
                                gcs = slice(c * 64, (c + 1) * 64)
                                P.op("dve", (lambda ocs=ocs, gcs=gcs: nc.vector.scalar_tensor_tensor(out=ostg[:, ocs], in0=ps[7][:, 0:64], scalar=don[:, 0:1], in1=sz[:, gcs],
                                                                                                     op0=ALU.mult, op1=ALU.mult)), ["ps7", "don", "sz"], ["ostg"])
                                if c % 16 == 15:
                                    qb = c // 16
                                    P.dma("sp", ex2_in[qb][(2 + hl) * 128:(3 + hl) * 128, :], ostg[:], ["ostg"], [f"ex2in{qb}"], grp="w_ex2in")
                                    if hl == 1:
                                        P.op("pool", (lambda qb=qb: nc.gpsimd.collective_compute(
                                            "AllGather", ALU.bypass, replica_groups=GROUPS4, ins=[ex2_in[qb]],
                                            outs=[ex2_out[qb * 2048:(qb + 1) * 2048, :]])),
                                            [f"ex2in{qb}"], ["ex2out"], dma=True, grp=f"cc_ex2_{qb}", inc=1)
                                yield

                        for _ in prep_gen(0):
                            pass
                        for bt in range(8):
                            sg_ = scan_gen(bt)
                            pg_ = prep_gen(bt + 1) if bt + 1 < 8 else None
                            alive_s, alive_p = True, pg_ is not None
                            while alive_s or alive_p:
                                if alive_s:
                                    try:
                                        next(sg_)
                                    except StopIteration:
                                        alive_s = False
                                if alive_p:
                                    try:
                                        next(pg_)
                                    except StopIteration:
                                        alive_p = False
                        P.barrier()
            if "stopDelta" in debug:
                out_from_spill()
                return nc
            P.barrier()
        xst2 = contextlib.ExitStack()
        X["x"] = sb(xst2, "x2", [128, KC, T])
        x = X["x"]
        P.dma("sp", x[:], xspill, ["xspill"], ["x"])
        with contextlib.ExitStack() as st:
            idx2 = sb(st, "idx2", [128, 16], I32)
            og = [sb(st, f"og{j}", [128, T], BF16) for j in range(16)]
            wo = [sb(st, f"wo{i}", [128, KC, 256], BF16) for i in range(2)]
            P.dma("sp", idx2[:], idx2_d, [], ["idx2"])
            for j in range(16):
                P.op("pool", (lambda j=j: nc.gpsimd.indirect_dma_start(
                    out=og[j][:], out_offset=None, in_=ex2_out, in_offset=bass.IndirectOffsetOnAxis(ap=idx2[:, j:j + 1], axis=0))),
                    ["ex2out", "idx2"], [f"og{j}"], dma=True, grp="gath2")
            if "og" in debug:
                dbg["og"] = dout("dbg_og", [16, 128, T], BF16)
                for j in range(16):
                    P.dma("sp", dbg["og"][j], og[j][:], [f"og{j}"], ["dbg_og"])
            for g in range(8):
                s = g % 2
                P.dma("pool", wo[s][:], wo_d[:, g * 256:(g + 1) * 256].rearrange("(kc p) f -> p kc f", p=128), [], [f"wo{s}"])
                for dcl in range(2):
                    dc = 2 * g + dcl
                    for hf in range(2):
                        bk = (dcl * 2 + hf) % 4
                        tsl = slice(hf * 512, (hf + 1) * 512)
                        for j in range(16):
                            P.op("pe", (lambda j=j, dcl=dcl, tsl=tsl, bk=bk, s=s: nc.tensor.matmul(
                                ps[bk][:], lhsT=wo[s][:, j, dcl * 128:(dcl + 1) * 128], rhs=og[j][:, tsl],
                                start=(j == 0), stop=(j == 15))), [f"wo{s}", f"og{j}"], [f"ps{bk}"], sig=(j == 15))
                        P.op("dve", (lambda dc=dc, tsl=tsl, bk=bk: nc.vector.scalar_tensor_tensor(
                            out=x[:, dc, tsl], in0=ps[bk][:], scalar=gcol(1)[:, dc:dc + 1], in1=x[:, dc, tsl],
                            op0=ALU.mult, op1=ALU.add)), [f"ps{bk}", "cols", "x"], ["x"])
            P.barrier()
        if "x2" in debug:
            dbg["x2"] = dout("dbg_x2", [128, KC, T])
            P.dma("sp", dbg["x2"], x[:], ["x"], ["dbg_x2"])
        if stage >= 3:
            with contextlib.ExitStack() as st:
                h3 = sb(st, "h3", [128, KC, T], BF16)
                modulate(st, 2, h3)
                ffn(st, 2, h3, w2g_d, w2u_d, w2d_d)
                P.barrier()
        P.dma("sp", outT_d.rearrange("(kc p) t -> p kc t", p=128), x[:], ["x"], ["outT"])
        P.finish()
        xst2.close()
    return nc


def _consts():
    c = np.zeros((128, 6, 128), np.float32)
    c[:, 0, :] = np.eye(128, dtype=np.float32)
    c[:, 1, :] = 1.0
    permT = np.zeros((128, 128), np.float32)
    for m in range(16):
        permT[m + 16, m] = -1.0
        permT[m, m + 16] = 1.0
    c[:, 2, :] = permT
    U = np.zeros((128, 128), np.float32)
    U[:64, :64] = np.triu(np.ones((64, 64), np.float32))
    c[:, 3, :] = U
    sl = np.zeros((128, 128), np.float32)
    i = np.arange(64)[:, None]; j = np.arange(64)[None, :]
    sl[:64, :64] = np.where(i > j, 0.0, 1e30)
    c[:, 4, :] = sl
    ut = np.zeros((128, 128), np.float32)
    ut[:64, :64] = np.where(j >= i, 0.0, 1e30)
    c[:, 5, :] = ut
    return c


def make_in_maps(inputs, dff=DFF):
    x = np.asarray(inputs["x"], np.float32)
    c = np.asarray(inputs["c"], np.float32)
    w_ada = np.asarray(inputs["w_ada"])[0]
    b_ada = np.asarray(inputs["b_ada"])[0]
    gains = np.stack([np.asarray(inputs[k])[0].reshape(KC, 128).T for k in ("ffn1_norm", "mix_norm", "ffn2_norm")], axis=1)
    cT = np.ascontiguousarray(c.reshape(2, KC, 128).transpose(2, 1, 0))
    cf = _consts()
    maps = []
    for cid in range(NCORES):
        b, tq = cid // 4, cid % 4
        sel = np.zeros((128, 2), np.float32); sel[:, b] = 1.0
        m = {
            "xT": np.ascontiguousarray(x[b, tq * T:(tq + 1) * T, :].T),
            "cT": cT, "sel": sel,
            "wada": np.ascontiguousarray(w_ada[:, tq * 4608:(tq + 1) * 4608]),
            "bada": np.ascontiguousarray(b_ada[tq * 4608:(tq + 1) * 4608].reshape(36, 128).T),
            "gains": np.ascontiguousarray(gains),
            "w1g": np.asarray(inputs["ffn1_w_gate"])[0][:, :dff], "w1u": np.asarray(inputs["ffn1_w_up"])[0][:, :dff],
            "w1d": np.asarray(inputs["ffn1_w_down"])[0][:dff],
            "w2g": np.asarray(inputs["ffn2_w_gate"])[0][:, :dff], "w2u": np.asarray(inputs["ffn2_w_up"])[0][:, :dff],
            "w2d": np.asarray(inputs["ffn2_w_down"])[0][:dff],
            "cf32": cf,
        }
        hp = tq
        w_in = np.asarray(inputs["w_in"])[0]
        heads = [2 * hp, 2 * hp + 1]
        m["pos"] = np.ascontiguousarray(np.asarray(inputs["positions"])[b].reshape(1, S).astype(np.int32))
        m["wA"] = np.stack([np.concatenate([w_in[:, t * 1024 + h * 128:t * 1024 + (h + 1) * 128] for t in range(3)], axis=1) for h in heads])
        m["wD"] = np.stack([np.concatenate([w_in[:, 3072 + t * 1024 + h * 128:3072 + t * 1024 + (h + 1) * 128] for t in range(4)], axis=1) for h in heads])
        m["wab"] = np.ascontiguousarray(np.stack([w_in[:, 7168 + heads[0]], w_in[:, 7168 + heads[1]],
                                                  w_in[:, 7176 + heads[0]], w_in[:, 7176 + heads[1]]], axis=1))
        m["qkg"] = np.ascontiguousarray(np.stack([np.asarray(inputs["q_norm"])[0], np.asarray(inputs["k_norm"])[0]], axis=1))
        cw = np.asarray(inputs["conv_w"])[0]
        convc = np.zeros((128, 2, 3, 4), np.float32)
        for hl, h in enumerate(heads):
            for t in range(3):
                convc[:, hl, t, :] = cw[:, t * 1024 + h * 128:t * 1024 + (h + 1) * 128].T
        m["convc"] = convc
        abp = np.zeros((4, 2), np.float32)
        for hl, h in enumerate(heads):
            abp[hl, 0] = np.asarray(inputs["a_log"])[0, h]
            abp[hl, 1] = np.asarray(inputs["dt_bias"])[0, h]
        m["abp"] = abp
        m["don"] = np.ascontiguousarray(np.asarray(inputs["delta_out_norm"])[0].reshape(128, 1))
        invf = np.zeros((128, 1), np.float32)
        invf[:32, 0] = np.tile((np.float32(500000.0) ** (-np.arange(16, dtype=np.float32) / np.float32(16))).astype(np.float32), 2)
        m["invf"] = invf
        kj = np.arange(128)[:, None]; qi = np.arange(128)[None, :]
        am = np.concatenate([np.where(qi >= kj, 0.0, -30000.0), np.where(kj >= qi, 0.0, -30000.0)], axis=1).astype(np.float32)
        m["amask"] = am.astype(ml_dtypes.bfloat16)
        m["idx2"] = (tq * 2048 + np.arange(16)[None, :] * 128 + np.arange(128)[:, None]).astype(np.int32)
        w_out = np.asarray(inputs["w_out"])[0]
        rows = []
        for r in range(4):
            for k in range(4):
                base = (2 * r + k) * 128 if k < 2 else 1024 + (2 * r + k - 2) * 128
                rows.append(w_out[base:base + 128])
        m["wo"] = np.ascontiguousarray(np.concatenate(rows, axis=0))
        maps.append(m)
    return maps


def assemble(results, key="outT"):
    out = np.zeros((2, S, D), np.float32)
    for cid in range(NCORES):
        b, tq = cid // 4, cid % 4
        out[b, tq * T:(tq + 1) * T, :] = np.asarray(results[cid][key]).T
    return out


def kernel(**inputs):
    nc = build(stage=3)
    maps = make_in_maps(inputs)
    res = run_bass_kernel_spmd(nc, maps, core_ids=list(range(NCORES)))
    return assemble(res.results)
```

```python
import contextlib
import bisect
import numpy as np
import ml_dtypes
import concourse.bass as bass
import concourse.mybir as mybir
from concourse.bass_utils import run_bass_kernel_spmd

F32 = mybir.dt.float32
BF16 = mybir.dt.bfloat16
I32 = mybir.dt.int32
AF = mybir.ActivationFunctionType
ALU = mybir.AluOpType

D = 2048
KC = 16
T = 1024
S = 4096
DFF = 5632
EPS = 1e-6
NCORES = 8
GROUPS4 = [[0, 1, 2, 3], [4, 5, 6, 7]]
GROUPS8 = [[0, 1, 2, 3, 4, 5, 6, 7]]


class Prog:
    def __init__(self, nc, stack):
        self.nc = nc
        self.stack = stack
        self.engs = {"pe": nc.tensor, "act": nc.scalar, "dve": nc.vector, "pool": nc.gpsimd, "sp": nc.sync}
        self.sems = {}
        self.cnt = {}
        self.waited = {}
        self.last_w = {}
        self.readers = {}
        self.pe_sig_idx = []
        self.pe_sig_val = []
        self.n_pe = 0
        self.nops = 0
        self.nwaits = 0

    def sem(self, name):
        if name not in self.sems:
            self.sems[name] = self.stack.enter_context(self.nc.semaphore(name))
            self.cnt[name] = 0
        return self.sems[name]

    def _need(self, rec, need):
        kind = rec[0]
        if kind == "dma":
            s = rec[1]
            v = self.cnt[s]
        elif kind == "pe":
            s = "c_pe"
            i = bisect.bisect_left(self.pe_sig_idx, rec[1])
            if i >= len(self.pe_sig_idx):
                ins = self.nc.tensor.nop()
                self.cnt[s] = self.cnt.get(s, 0) + 1
                ins.then_inc(self.sem(s), 1)
                self.pe_sig_idx.append(self.n_pe)
                self.pe_sig_val.append(self.cnt[s])
                self.n_pe += 1
                i = len(self.pe_sig_idx) - 1
            v = self.pe_sig_val[i]
        else:
            s = "c_" + kind
            v = rec[1]
        if v > need.get(s, 0):
            need[s] = v

    def op(self, eng, fn, reads=(), writes=(), dma=False, grp=None, inc=16, sig=True):
        writes = tuple(writes) + tuple(k for k in reads if k.startswith("ps") and k not in writes)
        need = {}
        seen = set()
        for k in reads:
            w = self.last_w.get(k)
            if w is not None and id(w) not in seen:
                seen.add(id(w))
                if not (w[0] == "pe" and eng == "pe" and not dma):
                    self._need(w, need)
        for k in writes:
            w = self.last_w.get(k)
            if w is not None and id(w) not in seen:
                seen.add(id(w))
                if not (w[0] == "pe" and eng == "pe" and not dma):
                    self._need(w, need)
            for r in self.readers.get(k, ()):
                if id(r) not in seen:
                    seen.add(id(r))
                    if not (r[0] == "pe" and eng == "pe" and not dma):
                        self._need(r, need)
        e = self.engs[eng]
        for s, v in need.items():
            if self.waited.get((eng, s), 0) >= v:
                continue
            self.waited[(eng, s)] = v
            e.wait_ge(self.sem(s), v)
            self.nwaits += 1
        ins = fn()
        self.nops += 1
        if dma:
            if grp is None:
                grp = "d_" + str(writes[0])
            self.sem(grp)
            self.cnt[grp] += inc
            ins.then_inc(self.sems[grp], inc)
            rec = ("dma", grp)
        elif eng == "pe":
            rec = ("pe", self.n_pe)
            if sig:
                self.sem("c_pe")
                self.cnt["c_pe"] += 1
                ins.then_inc(self.sems["c_pe"], 1)
                self.pe_sig_idx.append(self.n_pe)
                self.pe_sig_val.append(self.cnt["c_pe"])
            self.n_pe += 1
        else:
            s = "c_" + eng
            self.sem(s)
            self.cnt[s] += 1
            ins.then_inc(self.sems[s], 1)
            rec = (eng, self.cnt[s])
        for k in writes:
            self.last_w[k] = rec
            self.readers[k] = []
        for k in reads:
            if k not in writes:
                self.readers.setdefault(k, []).append(rec)
        return ins

    def dma(self, eng, out, in_, reads, writes, grp=None):
        e = self.engs[eng]
        return self.op(eng, lambda: e.dma_start(out=out, in_=in_), reads, writes, dma=True, grp=grp)

    def barrier(self, skip_cc=False):
        if self.n_pe and (not self.pe_sig_idx or self.pe_sig_idx[-1] != self.n_pe - 1):
            ins = self.nc.tensor.nop()
            self.sem("c_pe")
            self.cnt["c_pe"] += 1
            ins.then_inc(self.sems["c_pe"], 1)
            self.pe_sig_idx.append(self.n_pe)
            self.pe_sig_val.append(self.cnt["c_pe"])
            self.n_pe += 1
        for en, e in self.engs.items():
            for s, v in self.cnt.items():
                if skip_cc and s.startswith("cc_"):
                    continue
                if v > 0 and self.waited.get((en, s), 0) < v:
                    self.waited[(en, s)] = v
                    e.wait_ge(self.sems[s], v)
        if skip_cc:
            self.last_w = {k: r for k, r in self.last_w.items() if r[0] == "dma" and r[1].startswith("cc_")}
        else:
            self.last_w = {}
        self.readers = {}

    def finish(self):
        self.barrier()


class Ctx:
    pass


def build(stage=99, dff=DFF, debug=()):
    nc = bass.Bass("TRN2", target_bir_lowering=False)
    NG = dff // 256

    def din(name, shape, dt=F32):
        return nc.dram_tensor(name, list(shape), dt, kind="ExternalInput").ap()

    def dout(name, shape, dt=F32):
        return nc.dram_tensor(name, list(shape), dt, kind="ExternalOutput").ap()

    def dint(name, shape, dt=F32):
        return nc.dram_tensor(name, list(shape), dt, kind="Internal").ap()

    xT_d = din("xT", [D, T])
    cT_d = din("cT", [128, KC, 2])
    sel_d = din("sel", [128, 2])
    wada_d = din("wada", [D, 4608])
    bada_d = din("bada", [128, 36])
    gains_d = din("gains", [128, 3, KC])
    w1g_d = din("w1g", [D, dff]); w1u_d = din("w1u", [D, dff]); w1d_d = din("w1d", [dff, D])
    w2g_d = din("w2g", [D, dff]); w2u_d = din("w2u", [D, dff]); w2d_d = din("w2d", [dff, D])
    consts_d = din("cf32", [128, 6, 128])
    outT_d = dout("outT", [D, T])
    dbg = {}

    with contextlib.ExitStack() as top:
        P = Prog(nc, top)
        E = top.enter_context

        def sb(st, name, shape, dt=F32):
            return st.enter_context(nc.sbuf_tensor("s_" + name, list(shape), dt))

        ps = [E(nc.psum_tensor(f"ps{i}", [128, 512], F32)) for i in range(8)]

        modsel = sb(top, "modsel", [128, 144])
        gains = sb(top, "gains_t", [128, 3, KC])
        cols = sb(top, "cols", [128, 8, KC])
        cf = sb(top, "cf", [128, 6, 128])
        epsc = sb(top, "epsc", [128, 4])
        xst = contextlib.ExitStack()
        X = {"x": sb(xst, "x", [128, KC, T])}
        x = X["x"]
        ident = cf[:, 0, :]
        ones = cf[:, 1, :]

        P.dma("sp", x[:], xT_d.rearrange("(kc p) t -> p kc t", p=128), [], ["x"])
        P.dma("sp", cf[:], consts_d, [], ["cf"])
        P.dma("sp", gains[:], gains_d, [], ["gains"])
        P.op("dve", lambda: nc.vector.memset(epsc[:, 0:1], EPS), [], ["epsc"])
        P.op("dve", lambda: nc.vector.memset(epsc[:, 1:2], float(np.pi / 2)), [], ["epsc"])
        P.op("dve", lambda: nc.vector.memset(epsc[:, 2:3], 1.0), [], ["epsc"])
        P.op("dve", lambda: nc.vector.memset(epsc[:, 3:4], 0.0), [], ["epsc"])

        mod_in = dint("mod_in", [128, 128])
        mod_out = dint("mod_out", [4 * 128, 128])
        with contextlib.ExitStack() as st:
            cT = sb(st, "cT", [128, KC, 2])
            cact = sb(st, "cact", [128, KC, 2])
            sel = sb(st, "sel", [128, 2])
            bada = sb(st, "bada", [128, 36])
            wa = [sb(st, f"wa{i}", [128, 2304]) for i in range(4)]
            modp = sb(st, "modp", [128, 64, 2])
            modrow = sb(st, "modrow", [2, 4608])
            modall = sb(st, "modall", [128, 4, 36, 2])
            tmpm = sb(st, "tmpm", [128, 144])
            P.dma("sp", cT[:], cT_d, [], ["cT"])
            P.dma("sp", sel[:], sel_d, [], ["sel"])
            P.dma("sp", bada[:], bada_d, [], ["bada"])
            P.op("dve", lambda: nc.vector.memset(modp[:], 0.0), [], ["modp"])
            P.op("act", lambda: nc.scalar.activation(out=cact[:], in_=cT[:], func=AF.Silu), ["cT"], ["cact"])
            wi = 0
            for half in range(2):
                for kc in range(KC):
                    w = wa[wi % 4]
                    wk = f"wa{wi % 4}"
                    wi += 1
                    P.dma("sp", w[:], wada_d[kc * 128:(kc + 1) * 128, half * 2304:(half + 1) * 2304], [], [wk])
                    for sl in range(5):
                        n_ = 512 if sl < 4 else 256
                        P.op("pe", (lambda w=w, sl=sl, kc=kc, n_=n_: nc.tensor.matmul(
                            ps[sl][0:2, 0:n_], lhsT=cact[:, kc, :], rhs=w[:, sl * 512:sl * 512 + n_],
                            start=(kc == 0), stop=(kc == KC - 1))), [wk, "cact"], [f"ps{sl}"], sig=(kc == KC - 1 or sl == 4))
                for sl in range(5):
                    n_ = 512 if sl < 4 else 256
                    c0_ = half * 2304 + sl * 512
                    if sl % 2 == 0:
                        P.op("act", (lambda sl=sl, n_=n_, c0_=c0_: nc.scalar.copy(out=modrow[:, c0_:c0_ + n_], in_=ps[sl][0:2, 0:n_])),
                             [f"ps{sl}"], ["modrow"])
                    else:
                        P.op("dve", (lambda sl=sl, n_=n_, c0_=c0_: nc.vector.tensor_copy(out=modrow[:, c0_:c0_ + n_], in_=ps[sl][0:2, 0:n_])),
                             [f"ps{sl}"], ["modrow"])
            for j in range(36):
                P.op("pe", (lambda j=j: nc.tensor.matmul(ps[5][:, 2 * j:2 * j + 2], lhsT=modrow[0:2, j * 128:(j + 1) * 128],
                                                         rhs=cf[0:2, 0, 0:2], start=True, stop=True)), ["modrow", "cf"], ["ps5"], sig=(j == 35))
            P.op("dve", lambda: nc.vector.tensor_tensor(
                out=modp[:, 0:36, :], in0=ps[5][:, 0:72].rearrange("p (j b) -> p j b", b=2),
                in1=bada[:].unsqueeze(2).to_broadcast([128, 36, 2]), op=ALU.add), ["ps5", "bada", "modp"], ["modp"])
            P.dma("sp", mod_in, modp[:].rearrange("p j b -> p (j b)"), ["modp"], ["mod_in"])
            if "nocc" in debug:
                for r in range(4):
                    P.dma("sp", mod_out[r * 128:(r + 1) * 128, :], mod_in, ["mod_in"], ["mod_out"])
            else:
                P.op("pool", lambda: nc.gpsimd.collective_compute(
                    "AllGather", ALU.bypass, replica_groups=GROUPS4, ins=[mod_in], outs=[mod_out]),
                    ["mod_in"], ["mod_out"], dma=True, grp="cc_mod", inc=1)
            P.dma("sp", modall[:].rearrange("p r j b -> p r (j b)"),
                  mod_out.rearrange("(r p) f -> p r f", p=128)[:, :, 0:72], ["mod_out"], ["modall"])
            P.op("dve", lambda: nc.vector.tensor_scalar(
                out=tmpm[:].rearrange("p (r j) -> p r j", j=36), in0=modall[:, :, :, 0], scalar1=sel[:, 0:1], scalar2=None,
                op0=ALU.mult), ["modall", "sel"], ["tmpm"])
            P.op("dve", lambda: nc.vector.scalar_tensor_tensor(
                out=modsel[:].rearrange("p (r j) -> p r j", j=36), in0=modall[:, :, :, 1], scalar=sel[:, 1:2],
                in1=tmpm[:].rearrange("p (r j) -> p r j", j=36), op0=ALU.mult, op1=ALU.add),
                ["modall", "sel", "tmpm"], ["modsel"])
            for k in range(3):
                sc = modsel[:, (3 * k + 1) * 16:(3 * k + 1) * 16 + 16]
                gt = modsel[:, (3 * k + 2) * 16:(3 * k + 2) * 16 + 16]
                P.op("dve", (lambda k=k, sc=sc: nc.vector.scalar_tensor_tensor(
                    out=cols[:, 2 * k, :], in0=sc, scalar=1.0, in1=gains[:, k, :],
                    op0=ALU.add, op1=ALU.mult)), ["modsel", "gains"], ["cols"])
                P.op("dve", (lambda k=k, gt=gt: nc.vector.tensor_scalar(
                    out=cols[:, 2 * k + 1, :], in0=gt,
                    scalar1=(1.0 if k == 1 else 0.5), scalar2=None, op0=ALU.mult)), ["modsel"], ["cols"])
            if "modsel" in debug:
                dbg["modsel"] = dout("dbg_modsel", [128, 144])
                P.dma("sp", dbg["modsel"], modsel[:], ["modsel"], ["dbg_modsel"])
            P.barrier()

        if stage <= 0:
            P.dma("sp", outT_d.rearrange("(kc p) t -> p kc t", p=128), x[:], ["x"], ["outT"])
            P.finish()
            xst.close()
            return nc

        def scol(k):
            return cols[:, 2 * k, :]

        def gcol(k):
            return cols[:, 2 * k + 1, :]

        def shcol(k):
            return modsel[:, (3 * k) * 16:(3 * k) * 16 + 16]

        def modulate(st, k, h):
            x = X["x"]
            sq = [sb(st, f"sq{k}_{i}", [128, T]) for i in range(2)]
            ssum = sb(st, f"ssum{k}", [128, T])
            rstd = sb(st, f"rstd{k}", [128, T])
            for kc in range(KC):
                q = sq[kc % 2]
                if kc == 0:
                    P.op("act", lambda: nc.scalar.activation(out=ssum[:], in_=x[:, 0, :], func=AF.Square), ["x"], ["ssum"])
                else:
                    P.op("act", (lambda q=q, kc=kc: nc.scalar.activation(out=q[:], in_=x[:, kc, :], func=AF.Square)),
                         ["x"], [f"sq{kc % 2}"])
                    P.op("dve", (lambda q=q: nc.vector.tensor_tensor(out=ssum[:], in0=ssum[:], in1=q[:], op=ALU.add)),
                         [f"sq{kc % 2}", "ssum"], ["ssum"])
            for hf in range(2):
                P.op("pe", (lambda hf=hf: nc.tensor.matmul(ps[hf][:], lhsT=ones, rhs=ssum[:, hf * 512:(hf + 1) * 512],
                                                           start=True, stop=True)), ["ssum", "cf"], [f"ps{hf}"])
                P.op("act", (lambda hf=hf: nc.scalar.activation(out=rstd[:, hf * 512:(hf + 1) * 512], in_=ps[hf][:], func=AF.Ln,
                                                                bias=epsc[:, 0:1], scale=1.0 / D)), [f"ps{hf}", "epsc"], [f"rstd{hf}"])
                P.op("act", (lambda hf=hf: nc.scalar.activation(out=rstd[:, hf * 512:(hf + 1) * 512], in_=rstd[:, hf * 512:(hf + 1) * 512],
                                                                func=AF.Exp, scale=-0.5)), [f"rstd{hf}"], [f"rstd{hf}"])
            for kc in range(KC):
                q = sq[kc % 2]
                P.op("dve", (lambda q=q, kc=kc: nc.vector.scalar_tensor_tensor(
                    out=q[:], in0=x[:, kc, :], scalar=scol(k)[:, kc:kc + 1], in1=rstd[:], op0=ALU.mult, op1=ALU.mult)),
                    ["x", "cols", "rstd0", "rstd1"], [f"sq{kc % 2}"])
                P.op("act", (lambda q=q, kc=kc: nc.scalar.activation(out=h[:, kc, :], in_=q[:], func=AF.Identity,
                                                                     bias=shcol(k)[:, kc:kc + 1], scale=1.0)),
                     [f"sq{kc % 2}", "modsel"], [f"h_{kc}"])

        def ffn(st, k, h, Wg, Wu, Wd):
            x = X["x"]
            wg = [sb(st, f"wg{k}_{i}", [128, KC, 256], BF16) for i in range(2)]
            wu = [sb(st, f"wu{k}_{i}", [128, KC, 256], BF16) for i in range(2)]
            wd = [sb(st, f"wd{k}_{i}", [128, 2, D], BF16) for i in range(2)]
            aT = [sb(st, f"aT{k}_{i}", [128, 2, T], BF16) for i in range(2)]
            sg = [sb(st, f"sg{k}_{i}", [128, 512]) for i in range(2)]
            hkeys = [f"h_{kc}" for kc in range(KC)]

            def load_gu(g):
                s = g % 2
                P.dma("pool", wg[s][:], Wg[:, g * 256:(g + 1) * 256].rearrange("(kc p) f -> p kc f", p=128), [], [f"wg{s}"])
                P.dma("pool", wu[s][:], Wu[:, g * 256:(g + 1) * 256].rearrange("(kc p) f -> p kc f", p=128), [], [f"wu{s}"])

            def load_d(g):
                s = g % 2
                P.dma("pool", wd[s][:], Wd[g * 256:(g + 1) * 256, :].rearrange("(fc p) d -> p fc d", p=128), [], [f"wd{s}"])

            def gateup(g):
                s = g % 2
                for fc in range(2):
                    for hf in range(2):
                        bG, bU = 2 * hf, 2 * hf + 1
                        tsl = slice(hf * 512, (hf + 1) * 512)
                        for kc in range(KC):
                            P.op("pe", (lambda kc=kc, fc=fc, tsl=tsl, bG=bG: nc.tensor.matmul(
                                ps[bG][:], lhsT=wg[s][:, kc, fc * 128:(fc + 1) * 128], rhs=h[:, kc, tsl],
                                start=(kc == 0), stop=(kc == KC - 1))), [f"wg{s}", hkeys[kc]], [f"ps{bG}"], sig=(kc == KC - 1))
                        for kc in range(KC):
                            P.op("pe", (lambda kc=kc, fc=fc, tsl=tsl, bU=bU: nc.tensor.matmul(
                                ps[bU][:], lhsT=wu[s][:, kc, fc * 128:(fc + 1) * 128], rhs=h[:, kc, tsl],
                                start=(kc == 0), stop=(kc == KC - 1))), [f"wu{s}", hkeys[kc]], [f"ps{bU}"], sig=(kc == KC - 1))
                        P.op("act", (lambda hf=hf, bG=bG: nc.scalar.activation(out=sg[hf][:], in_=ps[bG][:], func=AF.Silu)),
                             [f"ps{bG}"], [f"sg{hf}"])
                        P.op("dve", (lambda hf=hf, bU=bU, fc=fc, tsl=tsl: nc.vector.tensor_tensor(
                            out=aT[s][:, fc, tsl], in0=sg[hf][:], in1=ps[bU][:], op=ALU.mult)),
                            [f"sg{hf}", f"ps{bU}"], [f"aT{s}_{fc}_{hf}"])

            dcount = [0]

            def down(g):
                s = g % 2
                for dc in range(KC):
                    for hf in range(2):
                        bk = 4 + dcount[0] % 4
                        dcount[0] += 1
                        tsl = slice(hf * 512, (hf + 1) * 512)
                        for fc in range(2):
                            P.op("pe", (lambda fc=fc, dc=dc, tsl=tsl, bk=bk: nc.tensor.matmul(
                                ps[bk][:], lhsT=wd[s][:, fc, dc * 128:(dc + 1) * 128], rhs=aT[s][:, fc, tsl],
                                start=(fc == 0), stop=(fc == 1))), [f"wd{s}", f"aT{s}_{fc}_{hf}"], [f"ps{bk}"], sig=(fc == 1))
                        P.op("dve", (lambda dc=dc, tsl=tsl, bk=bk: nc.vector.scalar_tensor_tensor(
                            out=x[:, dc, tsl], in0=ps[bk][:], scalar=gcol(k)[:, dc:dc + 1], in1=x[:, dc, tsl],
                            op0=ALU.mult, op1=ALU.add)), [f"ps{bk}", "cols", "x"], ["x"])

            load_gu(0); load_d(0)
            if NG > 1:
                load_gu(1); load_d(1)
            gateup(0)
            for g in range(NG):
                if g + 1 < NG:
                    gateup(g + 1)
                if g + 2 < NG:
                    load_gu(g + 2)
                down(g)
                if g + 2 < NG:
                    load_d(g + 2)

        with contextlib.ExitStack() as st:
            h = sb(st, "h1", [128, KC, T], BF16)
            modulate(st, 0, h)
            if "h1" in debug:
                dbg["h1"] = dout("dbg_h1", [128, KC, T], BF16)
                P.dma("sp", dbg["h1"], h[:], [f"h_{kc}" for kc in range(KC)], ["dbg_h1"])
            ffn(st, 0, h, w1g_d, w1u_d, w1d_d)
            P.barrier()

        if stage <= 1:
            P.dma("sp", outT_d.rearrange("(kc p) t -> p kc t", p=128), x[:], ["x"], ["outT"])
            P.finish()
            xst.close()
            return nc

        ex1_in = [dint(f"ex1_in{p}", [512, T], BF16) for p in range(4)]
        ex1_out = [dint(f"ex1_out{p}", [4 * 512, T], BF16) for p in range(4)]
        ex2_in = [dint(f"ex2_in{q}", [512, T], BF16) for q in range(4)]
        ex2_out = dint("ex2_out", [4 * 2048, T], BF16)
        xspill = dint("xspill", [128, KC, T])
        with contextlib.ExitStack() as st:
            h2 = sb(st, "h2", [128, KC, T], BF16)
            modulate(st, 1, h2)
            for p in range(4):
                P.dma("sp", ex1_in[p].rearrange("(kc p) t -> p kc t", p=128), h2[:, 4 * p:4 * p + 4, :],
                      [f"h_{kc}" for kc in range(4 * p, 4 * p + 4)], [f"ex1in{p}"])
                P.op("pool", (lambda p=p: nc.gpsimd.collective_compute(
                    "AllGather", ALU.bypass, replica_groups=GROUPS4, ins=[ex1_in[p]], outs=[ex1_out[p]])),
                    [f"ex1in{p}"], [f"ex1out{p}"], dma=True, grp=f"cc_ex1_{p}", inc=1)
            P.dma("sp", xspill, X["x"][:], ["x"], ["xspill"])
            P.barrier(skip_cc=True)
        xst.close()

        def out_from_spill():
            with contextlib.ExitStack() as st:
                xt = sb(st, "xtmp", [128, 4, T])
                ov = outT_d.rearrange("(kc p) t -> p kc t", p=128)
                for i4 in range(4):
                    P.dma("sp", xt[:], xspill[:, 4 * i4:4 * i4 + 4, :], ["xspill"], ["xtmp"])
                    P.dma("sp", ov[:, 4 * i4:4 * i4 + 4, :], xt[:], ["xtmp"], ["outT"])
                P.finish()

        if "stopC" in debug:
            out_from_spill()
            return nc

        pos_d = din("pos", [1, S], I32)
        wA_d = din("wA", [2, D, 384])
        wD_d = din("wD", [2, D, 512])
        wab_d = din("wab", [D, 4])
        qkg_d = din("qkg", [128, 2])
        convc_d = din("convc", [128, 2, 3, 4])
        abp_d = din("abp", [4, 2])
        don_d = din("don", [128, 1])
        invf_d = din("invf", [128, 1])
        amask_d = din("amask", [128, 256], BF16)
        idx2_d = din("idx2", [128, 16], I32)
        wo_d = din("wo", [D, D])
        ISQ = float(128 ** -0.5)
        TWO_PI = float(2 * np.pi)
        C1 = 6.28125
        C2 = float(2 * np.pi - 6.28125)

        with contextlib.ExitStack() as mst:
            identb = sb(mst, "identb", [128, 128], BF16)
            onesb = sb(mst, "onesb", [128, 128], BF16)
            amask = sb(mst, "amask", [128, 256], BF16)
            qkg = sb(mst, "qkg", [128, 2])
            convc = sb(mst, "convc", [128, 2, 3, 4])
            abp = sb(mst, "abp", [4, 2])
            don = sb(mst, "don", [128, 1])
            invf = sb(mst, "invf", [128, 1])
            hb = [sb(mst, f"hb{i}", [128, KC, 512], BF16) for i in range(2)]
            ostg = sb(mst, "ostg", [128, T], BF16)
            P.op("act", lambda: nc.scalar.copy(out=identb[:], in_=ident), ["cf"], ["identb"])
            P.op("act", lambda: nc.scalar.copy(out=onesb[:], in_=ones), ["cf"], ["onesb"])
            P.dma("sp", amask[:], amask_d, [], ["amask"])
            P.dma("sp", qkg[:], qkg_d, [], ["qkg"])
            P.dma("sp", convc[:], convc_d, [], ["convc"])
            P.dma("sp", abp[:], abp_d, [], ["abp"])
            P.dma("sp", don[:], don_d, [], ["don"])
            P.dma("sp", invf[:], invf_d, [], ["invf"])

            def load_hb(blk):
                t = hb[blk % 2]
                q, hs = blk // 2, (blk % 2) * 512
                for p in range(4):
                    P.dma("sp", t[:, 4 * p:4 * p + 4, :],
                          ex1_out[p][q * 512:(q + 1) * 512, hs:hs + 512].rearrange("(kc p) t -> p kc t", p=128),
                          [f"ex1out{p}"], [f"hb{blk % 2}_{p}"], grp=f"d_hb{blk % 2}")
                return t

            def inproj(bank, w, c0, nco, hbt, hkey, wkey):
                for kc in range(KC):
                    P.op("pe", (lambda kc=kc: nc.tensor.matmul(ps[bank][0:nco, :], lhsT=w[:, kc, c0:c0 + nco], rhs=hbt[:, kc, :],
                                                               start=(kc == 0), stop=(kc == KC - 1))),
                         [wkey, f"{hkey}_{kc // 4}"], [f"ps{bank}"], sig=(kc == KC - 1))

            with contextlib.ExitStack() as rst:
                cosT = sb(rst, "cosT", [128, S])
                sinT = sb(rst, "sinT", [128, S])
                with contextlib.ExitStack() as st:
                    posi = sb(st, "posi", [128, T], I32)
                    ang = sb(st, "ang", [128, T])
                    kf_ = sb(st, "kf_", [128, T])
                    ki = sb(st, "ki", [128, T], I32)
                    for qb in range(4):
                        cs = slice(qb * T, (qb + 1) * T)
                        P.dma("sp", posi[:], pos_d[0:1, cs].partition_broadcast(128) if False else pos_d[0:1, cs].to_broadcast([128, T]), [], ["posi"])
                        P.op("dve", lambda: nc.vector.tensor_copy(out=ang[:], in_=posi[:]), ["posi"], ["ang"])
                        P.op("dve", lambda: nc.vector.tensor_scalar(out=ang[:], in0=ang[:], scalar1=invf[:, 0:1], scalar2=None, op0=ALU.mult),
                             ["ang", "invf"], ["ang"])
                        P.op("dve", lambda: nc.vector.tensor_scalar(out=ki[:], in0=ang[:], scalar1=1.0 / TWO_PI, scalar2=None, op0=ALU.mult),
                             ["ang"], ["ki"])
                        P.op("dve", lambda: nc.vector.tensor_copy(out=kf_[:], in_=ki[:]), ["ki"], ["kf_"])
                        P.op("dve", lambda: nc.vector.scalar_tensor_tensor(out=ang[:], in0=kf_[:], scalar=-C1, in1=ang[:], op0=ALU.mult, op1=ALU.add),
                             ["kf_", "ang"], ["ang"])
                        P.op("dve", lambda: nc.vector.scalar_tensor_tensor(out=ang[:], in0=kf_[:], scalar=-C2, in1=ang[:], op0=ALU.mult, op1=ALU.add),
                             ["kf_", "ang"], ["ang"])
                        P.op("dve", lambda: nc.vector.tensor_scalar(out=ang[:], in0=ang[:], scalar1=float(-np.pi), scalar2=float(np.pi),
                                                                    op0=ALU.max, op1=ALU.min), ["ang"], ["ang"])
                        P.op("act", (lambda cs=cs: nc.scalar.activation(out=sinT[:, cs], in_=ang[:], func=AF.Sin)), ["ang"], ["sinT"])
                        P.op("act", lambda: nc.scalar.activation(out=kf_[:], in_=ang[:], func=AF.Abs), ["ang"], ["kf_"])
                        P.op("act", (lambda cs=cs: nc.scalar.activation(out=cosT[:, cs], in_=kf_[:], func=AF.Sin, scale=-1.0, bias=epsc[:, 1:2])),
                             ["kf_", "epsc"], ["cosT"])
                    P.barrier(skip_cc=True)
                if "stopRope" in debug:
                    dbg["cos"] = dout("dbg_cos", [128, S]); dbg["sin"] = dout("dbg_sin", [128, S])
                    P.dma("sp", dbg["cos"], cosT[:], ["cosT"], ["dbg_cos"]); P.dma("sp", dbg["sin"], sinT[:], ["sinT"], ["dbg_sin"])
                    out_from_spill()
                    return nc

                for hl in range(2):
                    with contextlib.ExitStack() as st:
                        wA = sb(st, f"wA{hl}", [128, KC, 384], BF16)
                        qT = sb(st, f"qT{hl}", [128, S], BF16)
                        kT = sb(st, f"kT{hl}", [128, S], BF16)
                        vT = sb(st, f"vT{hl}", [128, S], BF16)
                        sqt_ = [sb(st, f"sqt{hl}_{i}", [128, 512]) for i in range(2)]
                        rst__ = [sb(st, f"rst{hl}_{i}", [128, 512]) for i in range(2)]
                        qn_ = [sb(st, f"qn{hl}_{i}", [128, 512]) for i in range(2)]
                        t1_ = [sb(st, f"t1{hl}_{i}", [128, 512]) for i in range(2)]
                        t2_ = [sb(st, f"t2{hl}_{i}", [128, 512]) for i in range(2)]
                        vd = sb(st, f"vd{hl}", [128, 32, 128], BF16)
                        accn = sb(st, f"accn{hl}", [128, S])
                        accd = sb(st, f"accd{hl}", [128, S])
                        pt = [sb(st, f"pt{hl}_{i}", [128, 256], BF16) for i in range(4)]
                        P.dma("pool", wA[:], wA_d[hl].rearrange("(kc p) f -> p kc f", p=128), [], ["wA"])
                        for blk in range(8):
                            hbt = load_hb(blk)
                            cs = slice(blk * 512, (blk + 1) * 512)
                            for t in range(3):
                                inproj(t, wA, t * 128, 128, hbt, f"hb{blk % 2}", "wA")
                            P.op("act", (lambda cs=cs: nc.scalar.copy(out=vT[:, cs], in_=ps[2][:])), ["ps2"], ["vT"])
                            for t in range(2):
                                P.op("act", (lambda t=t: nc.scalar.activation(out=sqt_[t][:], in_=ps[t][:], func=AF.Square)), [f"ps{t}"], [f"sqt{t}"])
                            for t in range(2):
                                P.op("pe", (lambda t=t: nc.tensor.matmul(ps[3 + t][:], lhsT=ones, rhs=sqt_[t][:], start=True, stop=True)), [f"sqt{t}", "cf"], [f"ps{3 + t}"])
                            for t in range(2):
                                P.op("act", (lambda t=t: nc.scalar.activation(out=rst__[t][:], in_=ps[3 + t][:], func=AF.Ln, bias=epsc[:, 0:1], scale=1.0 / 128)),
                                     [f"ps{3 + t}", "epsc"], [f"rst{t}"])
                            for t in range(2):
                                P.op("act", (lambda t=t: nc.scalar.activation(out=rst__[t][:], in_=rst__[t][:], func=AF.Exp, scale=-0.5)), [f"rst{t}"], [f"rst{t}"])
                            for t in range(2):
                                P.op("dve", (lambda t=t: nc.vector.scalar_tensor_tensor(
                                    out=qn_[t][:], in0=ps[t][:], scalar=qkg[:, t:t + 1], in1=rst__[t][:], op0=ALU.mult, op1=ALU.mult)),
                                    [f"ps{t}", "qkg", f"rst{t}"], [f"qn{t}"])
                            for t in range(2):
                                P.op("pe", (lambda t=t: nc.tensor.matmul(ps[3 + t][:], lhsT=cf[:, 2, :], rhs=qn_[t][:], start=True, stop=True)), [f"qn{t}", "cf"], [f"ps{3 + t}"])
                            for t in range(2):
                                P.op("dve", (lambda cs=cs, t=t: nc.vector.tensor_tensor(out=t1_[t][:], in0=qn_[t][:], in1=cosT[:, cs], op=ALU.mult)), [f"qn{t}", "cosT"], [f"t1{t}"])
                            for t in range(2):
                                P.op("dve", (lambda cs=cs, t=t: nc.vector.tensor_tensor(out=t2_[t][:], in0=ps[3 + t][:], in1=sinT[:, cs], op=ALU.mult)), [f"ps{3 + t}", "sinT"], [f"t2{t}"])
                            for t in range(2):
                                dst = qT if t == 0 else kT
                                P.op("dve", (lambda cs=cs, dst=dst, t=t: nc.vector.tensor_tensor(out=dst[:, cs], in0=t1_[t][:], in1=t2_[t][:], op=ALU.add)),
                                     [f"t1{t}", f"t2{t}"], ["qT" if t == 0 else "kT"])
                        if f"qk{hl}" in debug:
                            dbg["qT"] = dout("dbg_qT", [128, S], BF16); dbg["kT"] = dout("dbg_kT", [128, S], BF16)
                            P.dma("sp", dbg["qT"], qT[:], ["qT"], ["dbg_qT"]); P.dma("sp", dbg["kT"], kT[:], ["kT"], ["dbg_kT"])
                        if "stopInproj" in debug:
                            out_from_spill()
                            return nc
                        kcount = 0
                        for d in ((1,) if "d1only" in debug else (1, 4, 16)):
                            nbk = 32 // d
                            for bi0 in range(0, 32, 4):
                                for u in range(4):
                                    r, n = divmod(bi0 + u, nbk)
                                    a0 = r + d * 128 * n
                                    P.op("pe", (lambda u=u, a0=a0, d=d: nc.tensor.matmul(
                                        ps[6][:, u * 128:(u + 1) * 128], lhsT=vT[:, a0:a0 + d * 127 + 1:d], rhs=identb[:],
                                        start=True, stop=True)), ["vT", "identb"], ["ps6"], sig=(u == 3))
                                P.op("act", (lambda bi0=bi0: nc.scalar.copy(out=vd[:, bi0:bi0 + 4, :],
                                                                            in_=ps[6][:].rearrange("p (u e) -> p u e", e=128))),
                                     ["ps6"], [f"vd{bi0}"])
                            blocks = [(r, n) for r in range(d) for n in range(nbk)]

                            def emit_S(r, n, kc_):
                                a0 = r + d * 128 * n
                                nq = 256 if n + 1 < nbk else 128
                                sbank = 4 + kc_ % 2
                                ptt = pt[kc_ % 4]
                                pkey = f"pt{kc_ % 4}"
                                sap = ps[sbank][:, 0:nq]
                                skey = f"ps{sbank}"
                                P.op("pe", (lambda: nc.tensor.matmul(
                                    sap, lhsT=kT[:, a0:a0 + d * 127 + 1:d], rhs=qT[:, a0:a0 + d * (nq - 1) + 1:d],
                                    start=True, stop=False)), ["kT", "qT"], [skey], sig=False)
                                P.op("pe", (lambda: nc.tensor.matmul(
                                    sap, lhsT=identb[:], rhs=amask[:, 0:nq], start=False, stop=True)), ["identb", "amask"], [skey])
                                P.op("act", (lambda: nc.scalar.activation(out=ptt[:, 0:nq], in_=sap, func=AF.Exp, scale=ISQ)),
                                     [skey], [pkey])

                            def emit_PV(r, n, kc_):
                                a0 = r + d * 128 * n
                                bi = r * nbk + n
                                ptt = pt[kc_ % 4]
                                pkey = f"pt{kc_ % 4}"
                                nb_, db_ = n % 2, 2 + n % 2
                                P.op("pe", (lambda: nc.tensor.matmul(
                                    ps[nb_][:, 0:128], lhsT=vd[:, bi, :], rhs=ptt[:, 0:128],
                                    start=(n == 0), stop=True)), [f"vd{bi - bi % 4}", pkey], [f"ps{nb_}"])
                                P.op("pe", (lambda: nc.tensor.matmul(
                                    ps[db_][:, 0:128], lhsT=onesb[:], rhs=ptt[:, 0:128],
                                    start=(n == 0), stop=True)), ["onesb", pkey], [f"ps{db_}"])
                                if n + 1 < nbk:
                                    nb2, db2 = (n + 1) % 2, 2 + (n + 1) % 2
                                    P.op("pe", (lambda: nc.tensor.matmul(
                                        ps[nb2][:, 0:128], lhsT=vd[:, bi, :], rhs=ptt[:, 128:256],
                                        start=True, stop=False)), [f"vd{bi - bi % 4}", pkey], [f"ps{nb2}"], sig=False)
                                    P.op("pe", (lambda: nc.tensor.matmul(
                                        ps[db2][:, 0:128], lhsT=onesb[:], rhs=ptt[:, 128:256],
                                        start=True, stop=False)), ["onesb", pkey], [f"ps{db2}"], sig=False)
                                tsl = slice(a0, a0 + d * 127 + 1, d)
                                if d == 1:
                                    P.op("act", (lambda: nc.scalar.copy(out=accn[:, tsl], in_=ps[nb_][:, 0:128])),
                                         [f"ps{nb_}"], ["accn"])
                                    P.op("dve", (lambda: nc.vector.tensor_copy(out=accd[:, tsl], in_=ps[db_][:, 0:128])),
                                         [f"ps{db_}"], ["accd"])
                                else:
                                    P.op("dve", (lambda: nc.vector.tensor_tensor(
                                        out=accn[:, tsl], in0=accn[:, tsl], in1=ps[nb_][:, 0:128], op=ALU.add)),
                                        [f"ps{nb_}", "accn"], ["accn"])
                                    P.op("dve", (lambda: nc.vector.tensor_tensor(
                                        out=accd[:, tsl], in0=accd[:, tsl], in1=ps[db_][:, 0:128], op=ALU.add)),
                                        [f"ps{db_}", "accd"], ["accd"])

                            emit_S(*blocks[0], kcount)
                            for i_, (r, n) in enumerate(blocks):
                                if i_ + 1 < len(blocks):
                                    emit_S(*blocks[i_ + 1], kcount + i_ + 1)
                                emit_PV(r, n, kcount + i_)
                            kcount += len(blocks)
                        for qb in range(4):
                            cs = slice(qb * T, (qb + 1) * T)
                            P.op("dve", (lambda cs=cs: nc.vector.reciprocal(out=accd[:, cs], in_=accd[:, cs])), ["accd"], ["accd"])
                            P.op("dve", (lambda cs=cs: nc.vector.tensor_tensor(out=ostg[:], in0=accn[:, cs], in1=accd[:, cs], op=ALU.mult)),
                                 ["accn", "accd"], ["ostg"])
                            P.dma("sp", ex2_in[qb][hl * 128:(hl + 1) * 128, :], ostg[:], ["ostg"], [f"ex2in{qb}"], grp="w_ex2in")
                        P.barrier()
            if "stopAttn" in debug:
                out_from_spill()
                return nc
            with contextlib.ExitStack() as dst_:
                abrow = sb(dst_, "abrow", [4, S])
                G4 = sb(dst_, "G4", [4, S])
                wabt = sb(dst_, "wabt", [128, KC, 4], BF16)
                negexp = sb(dst_, "negexp", [4, 1])
                idt8 = sb(dst_, "idt8", [64, 8, 64])
                U64 = cf[0:64, 3, 0:64]
                pm_sl = cf[0:64, 4, 0:64]
                pm_ut = cf[0:64, 5, 0:64]
                id64 = cf[0:64, 0, 0:64]
                P.dma("pool", wabt[:], wab_d.rearrange("(kc p) f -> p kc f", p=128), [], ["wabt"])
                for i8 in range(8):
                    P.op("dve", (lambda i8=i8: nc.vector.tensor_copy(out=idt8[:, i8, :], in_=id64)), ["cf"], ["idt8"])
                for hl in range(2):
                    with contextlib.ExitStack() as st:
                        qf = sb(st, f"qf{hl}", [128, S])
                        kf = sb(st, f"kf{hl}", [128, S])
                        vf = sb(st, f"vf{hl}", [128, S])
                        sz = sb(st, f"sz{hl}", [128, S])
                        ist = contextlib.ExitStack()
                        wDt = sb(ist, f"wDt{hl}", [128, KC, 512], BF16)
                        raw = [sb(ist, f"raw{hl}_{t}", [128, 515]) for t in range(3)]
                        ctmp_ = [sb(ist, f"ctmp{hl}_{i}", [128, 512]) for i in range(3)]
                        ctm2_ = [sb(ist, f"ctm2{hl}_{i}", [128, 512]) for i in range(2)]
                        sqd_ = [sb(ist, f"sqd{hl}_{i}", [128, 512]) for i in range(2)]
                        rsd_ = [sb(ist, f"rsd{hl}_{i}", [128, 512]) for i in range(2)]
                        P.dma("pool", wDt[:], wD_d[hl].rearrange("(kc p) f -> p kc f", p=128), [], ["wDt"])
                        for t in range(3):
                            P.op("dve", (lambda t=t: nc.vector.memset(raw[t][:, 0:3], 0.0)), [], [f"raw{t}"])
                        for blk in range(8):
                            hbt = load_hb(blk)
                            cs = slice(blk * 512, (blk + 1) * 512)
                            for t in range(4):
                                inproj(t, wDt, t * 128, 128, hbt, f"hb{blk % 2}", "wDt")
                            if hl == 0:
                                for kc in range(KC):
                                    P.op("pe", (lambda kc=kc, hbt=hbt: nc.tensor.matmul(ps[6][0:4, :], lhsT=wabt[:, kc, 0:4], rhs=hbt[:, kc, :],
                                                                                       start=(kc == 0), stop=(kc == KC - 1))),
                                         ["wabt", f"hb{blk % 2}_{kc // 4}"], ["ps6"], sig=(kc == KC - 1))
                            cw = lambda t, j: convc[:, hl, t, j:j + 1]
                            for t in range(3):
                                P.op("act", (lambda t=t: nc.scalar.copy(out=raw[t][:, 3:515], in_=ps[t][:])), [f"ps{t}"], [f"raw{t}"])
                            for t in range(3):
                                P.op("act", (lambda t=t: nc.scalar.activation(out=ctmp_[t][:], in_=raw[t][:, 3:515], func=AF.Copy, scale=cw(t, 3))),
                                     [f"raw{t}", "convc"], [f"ctmp{t}"])
                            if hl == 0:
                                P.op("act", (lambda cs=cs: nc.scalar.copy(out=abrow[:, cs], in_=ps[6][0:4, :])), ["ps6"], ["abrow"])
                            for j in (2, 1, 0):
                                for t in range(3):
                                    P.op("dve", (lambda t=t, j=j: nc.vector.scalar_tensor_tensor(
                                        out=ctmp_[t][:], in0=raw[t][:, j:j + 512], scalar=cw(t, j), in1=ctmp_[t][:], op0=ALU.mult, op1=ALU.add)),
                                        [f"raw{t}", "convc", f"ctmp{t}"], [f"ctmp{t}"])
                            for t in range(3):
                                P.op("act", (lambda t=t: nc.scalar.copy(out=raw[t][:, 0:3], in_=raw[t][:, 512:515])), [f"raw{t}"], [f"raw{t}"])
                            P.op("act", (lambda cs=cs: nc.scalar.activation(out=sz[:, cs], in_=ps[3][:], func=AF.Silu)), ["ps3"], ["sz"])
                            P.op("act", (lambda cs=cs: nc.scalar.activation(out=vf[:, cs], in_=ctmp_[2][:], func=AF.Silu)), ["ctmp2"], ["vf"])
                            for t in range(2):
                                P.op("act", (lambda t=t: nc.scalar.activation(out=ctm2_[t][:], in_=ctmp_[t][:], func=AF.Silu)), [f"ctmp{t}"], [f"ctm2{t}"])
                            for t in range(2):
                                P.op("act", (lambda t=t: nc.scalar.activation(out=sqd_[t][:], in_=ctm2_[t][:], func=AF.Square)), [f"ctm2{t}"], [f"sqd{t}"])
                            for t in range(2):
                                P.op("pe", (lambda t=t: nc.tensor.matmul(ps[4 + t][:], lhsT=ones, rhs=sqd_[t][:], start=True, stop=True)), [f"sqd{t}", "cf"], [f"ps{4 + t}"])
                            for t in range(2):
                                P.op("act", (lambda t=t: nc.scalar.activation(out=rsd_[t][:], in_=ps[4 + t][:], func=AF.Ln, bias=epsc[:, 0:1], scale=1.0)),
                                     [f"ps{4 + t}", "epsc"], [f"rsd{t}"])
                            for t in range(2):
                                P.op("act", (lambda t=t: nc.scalar.activation(out=rsd_[t][:], in_=rsd_[t][:], func=AF.Exp, scale=-0.5)), [f"rsd{t}"], [f"rsd{t}"])
                            for t in range(2):
                                dstt = qf if t == 0 else kf
                                P.op("dve", (lambda cs=cs, dstt=dstt, t=t: nc.vector.tensor_tensor(out=dstt[:, cs], in0=ctm2_[t][:], in1=rsd_[t][:], op=ALU.mult)),
                                     [f"ctm2{t}", f"rsd{t}"], ["qf" if t == 0 else "kf"])
                        if hl == 0:
                            P.op("act", lambda: nc.scalar.activation(out=negexp[:], in_=abp[:, 0:1], func=AF.Exp), ["abp"], ["negexp"])
                            P.op("dve", lambda: nc.vector.tensor_scalar(out=negexp[:], in0=negexp[:], scalar1=-1.0, scalar2=None, op0=ALU.mult),
                                 ["negexp"], ["negexp"])
                            P.op("act", lambda: nc.scalar.activation(out=G4[:], in_=abrow[:], func=AF.Exp, bias=abp[:, 1:2], scale=1.0), ["abrow", "abp"], ["G4"])
                            P.op("act", lambda: nc.scalar.activation(out=G4[:], in_=G4[:], func=AF.Ln, bias=epsc[0:4, 2:3], scale=1.0), ["G4", "epsc"], ["G4"])
                            P.op("dve", lambda: nc.vector.tensor_scalar(out=G4[:], in0=G4[:], scalar1=negexp[:, 0:1], scalar2=None, op0=ALU.mult),
                                 ["G4", "negexp"], ["G4"])
                            P.op("act", lambda: nc.scalar.activation(out=abrow[:], in_=abrow[:], func=AF.Sigmoid), ["abrow"], ["abrow"])
                            if "gb" in debug:
                                dbg["G4"] = dout("dbg_G4", [4, S]); dbg["S4"] = dout("dbg_S4", [4, S]); S4 = abrow
                                P.dma("sp", dbg["G4"], G4[:], ["G4"], ["dbg_G4"]); P.dma("sp", dbg["S4"], abrow[:], ["abrow"], ["dbg_S4"])
                        if f"dqkv{hl}" in debug:
                            for nm, tt in (("qf", qf), ("kf", kf), ("vf", vf)):
                                dbg[nm] = dout("dbg_" + nm, [128, S])
                                P.dma("sp", dbg[nm], tt[:], [nm], ["dbg_" + nm])
                        P.barrier()
                        ist.close()
                        gb = sb(st, f"gb{hl}", [64, 128])
                        gct = sb(st, f"gct{hl}", [64, 64])
                        glb = sb(st, f"glb{hl}", [128, 64])
                        egc = sb(st, f"egc{hl}", [64, 64])
                        bgc = sb(st, f"bgc{hl}", [64, 64])
                        ekt = sb(st, f"ekt{hl}", [64, 64])
                        decb = sb(st, f"decb{hl}", [128, 64])
                        for c in range(64):
                            P.op("pe", (lambda c=c: nc.tensor.matmul(ps[0][0:64, 2 * c:2 * c + 1], lhsT=G4[0:4, c * 64:(c + 1) * 64],
                                                                     rhs=cf[0:4, 0, hl:hl + 1], start=True, stop=True)), ["G4", "cf"], ["ps0"], sig=False)
                            P.op("pe", (lambda c=c: nc.tensor.matmul(ps[0][0:64, 2 * c + 1:2 * c + 2], lhsT=abrow[0:4, c * 64:(c + 1) * 64],
                                                                     rhs=cf[0:4, 0, 2 + hl:3 + hl], start=True, stop=True)), ["abrow", "cf"], ["ps0"], sig=(c == 63))
                        P.op("dve", lambda: nc.vector.tensor_copy(out=gb[:], in_=ps[0][0:64, 0:128]), ["ps0"], ["gb"])
                        P.op("pe", lambda: nc.tensor.matmul(ps[1][0:64, 0:64], lhsT=U64, rhs=gb[:, 0:128:2], start=True, stop=True), ["gb", "cf"], ["ps1"])
                        P.op("pe", lambda: nc.tensor.matmul(ps[2][:, 0:64], lhsT=cf[0:64, 1, :], rhs=gb[:, 0:128:2], start=True, stop=True), ["gb", "cf"], ["ps2"])
                        P.op("dve", lambda: nc.vector.tensor_copy(out=gct[:], in_=ps[1][0:64, 0:64]), ["ps1"], ["gct"])
                        P.op("dve", lambda: nc.vector.tensor_copy(out=glb[:], in_=ps[2][:, 0:64]), ["ps2"], ["glb"])
                        P.op("act", lambda: nc.scalar.activation(out=egc[:], in_=gct[:], func=AF.Exp), ["gct"], ["egc"])
                        P.op("dve", lambda: nc.vector.tensor_tensor(out=bgc[:], in0=egc[:], in1=gb[:, 1:128:2], op=ALU.mult), ["egc", "gb"], ["bgc"])
                        P.op("dve", lambda: nc.vector.tensor_tensor(out=ekt[:], in0=glb[0:64, :], in1=gct[:], op=ALU.subtract), ["glb", "gct"], ["ekt"])
                        P.op("act", lambda: nc.scalar.activation(out=ekt[:], in_=ekt[:], func=AF.Exp), ["ekt"], ["ekt"])
                        P.op("act", lambda: nc.scalar.activation(out=decb[:], in_=glb[:], func=AF.Exp), ["glb"], ["decb"])
                        eg = sb(st, f"eg{hl}", [128, 512])
                        tl = sb(st, f"tl{hl}", [64, 512])
                        tu = sb(st, f"tu{hl}", [64, 512])
                        al = [sb(st, f"al{hl}_{i}", [64, 512]) for i in range(2)]
                        bm = [sb(st, f"bm{hl}_{i}", [64, 512]) for i in range(2)]
                        nn = [sb(st, f"nn{hl}_{i}", [64, 512]) for i in range(2)]
                        kbg = sb(st, f"kbg{hl}", [64, 8, 128], BF16)
                        nbf = sb(st, f"nbf{hl}", [64, 512], BF16)
                        vb = sb(st, f"vb{hl}", [64, 8, 128], BF16)
                        qdT = [sb(st, f"qdT{hl}_{i}", [128, 512], BF16) for i in range(2)]
                        itT = [sb(st, f"itT{hl}_{i}", [64, 512], BF16) for i in range(2)]
                        ktb = [sb(st, f"ktb{hl}_{i}", [64, 8, 128], BF16) for i in range(2)]
                        ub = [sb(st, f"ub{hl}_{i}", [64, 8, 128]) for i in range(2)]
                        wTb = [sb(st, f"wTb{hl}_{i}", [128, 512], BF16) for i in range(2)]
                        Sst = sb(st, f"Sst{hl}", [128, 128])
                        vn = [sb(st, f"vn{hl}_{i}", [64, 128], BF16) for i in range(2)]
                        Sbf = sb(st, f"Sbf{hl}", [128, 128], BF16)
                        on = sb(st, f"on{hl}", [64, 128])
                        osq = sb(st, f"osq{hl}", [64, 128])
                        ssc = sb(st, f"ssc{hl}", [64, 2])
                        P.op("dve", lambda: nc.vector.memset(Sst[:], 0.0), [], ["Sst"])
                        P.op("dve", lambda: nc.vector.memset(Sbf[:], 0.0), [], ["Sbf"])
                        pbank = [0]

                        def nbk_():
                            b = pbank[0] % 4
                            pbank[0] += 1
                            return b

                        binfo = {}

                        def prep_gen(bt):
                            pp = bt % 2
                            c0 = bt * 8
                            tc = lambda ci, bt=bt: slice(bt * 512 + ci * 64, bt * 512 + (ci + 1) * 64)
                            cc = lambda ci: slice(ci * 64, (ci + 1) * 64)
                            bs = slice(bt * 512, (bt + 1) * 512)
                            yield
                            bA = nbk_()
                            for ci in range(8):
                                P.op("pe", (lambda ci=ci, bA=bA: nc.tensor.matmul(
                                    ps[bA][:, cc(ci)], lhsT=gb[:, 2 * (c0 + ci):2 * (c0 + ci) + 1].to_broadcast([64, 128]), rhs=U64,
                                    start=True, stop=True)), ["gb", "cf"], [f"ps{bA}"], sig=(ci == 7))
                            P.op("act", (lambda bA=bA: nc.scalar.activation(out=eg[:], in_=ps[bA][:], func=AF.Exp)), [f"ps{bA}"], ["eg"])
                            P.op("dve", (lambda bs=bs, pp=pp: nc.vector.scalar_tensor_tensor(out=qdT[pp][:], in0=qf[:, bs], scalar=ISQ, in1=eg[:],
                                                                                            op0=ALU.mult, op1=ALU.mult)), ["qf", "eg"], [f"qdT{pp}"])
                            for ci in range(8):
                                P.op("dve", (lambda ci=ci, bA=bA: nc.vector.scalar_tensor_tensor(
                                    out=tl[:, cc(ci)], in0=ps[bA][0:64, cc(ci)], scalar=gct[:, c0 + ci:c0 + ci + 1], in1=pm_sl,
                                    op0=ALU.subtract, op1=ALU.add)), [f"ps{bA}", "gct", "cf"], ["tl"])
                                P.op("dve", (lambda ci=ci, bA=bA: nc.vector.scalar_tensor_tensor(
                                    out=tu[:, cc(ci)], in0=ps[bA][0:64, cc(ci)], scalar=gct[:, c0 + ci:c0 + ci + 1], in1=pm_ut,
                                    op0=ALU.subtract, op1=ALU.subtract)), [f"ps{bA}", "gct", "cf"], ["tu"])
                            P.op("act", lambda: nc.scalar.activation(out=tl[:], in_=tl[:], func=AF.Exp, scale=-1.0), ["tl"], ["tl"])
                            P.op("act", lambda: nc.scalar.activation(out=tu[:], in_=tu[:], func=AF.Exp), ["tu"], ["tu"])
                            yield
                            bB = nbk_()
                            for ci in range(8):
                                P.op("pe", (lambda ci=ci, bB=bB: nc.tensor.matmul(ps[bB][0:64, cc(ci)], lhsT=kf[:, tc(ci)], rhs=kf[:, tc(ci)],
                                                                                  start=True, stop=True)), ["kf"], [f"ps{bB}"], sig=(ci == 7))
                            for ci in range(8):
                                P.op("dve", (lambda ci=ci, bB=bB: nc.vector.scalar_tensor_tensor(
                                    out=al[0][:, cc(ci)], in0=ps[bB][0:64, cc(ci)], scalar=gb[:, 2 * (c0 + ci) + 1:2 * (c0 + ci) + 2], in1=tl[:, cc(ci)],
                                    op0=ALU.mult, op1=ALU.mult)), [f"ps{bB}", "gb", "tl"], ["al0"])
                            yield
                            bC = nbk_()
                            for ci in range(8):
                                P.op("pe", (lambda ci=ci, bC=bC: nc.tensor.matmul(ps[bC][0:64, cc(ci)], lhsT=kf[:, tc(ci)], rhs=qf[:, tc(ci)],
                                                                                  start=True, stop=True)), ["kf", "qf"], [f"ps{bC}"], sig=(ci == 7))
                            P.op("dve", (lambda bC=bC, pp=pp: nc.vector.scalar_tensor_tensor(out=itT[pp][:], in0=ps[bC][0:64, :], scalar=ISQ, in1=tu[:],
                                                                                            op0=ALU.mult, op1=ALU.mult)), [f"ps{bC}", "tu"], [f"itT{pp}"])
                            yield
                            bD = nbk_()
                            for ci in range(8):
                                P.op("pe", (lambda ci=ci, bD=bD: nc.tensor.matmul(ps[bD][0:64, cc(ci)], lhsT=al[0][:, cc(ci)], rhs=id64,
                                                                                  start=True, stop=True)), ["al0", "cf"], [f"ps{bD}"], sig=(ci == 7))
                            P.op("act", (lambda bD=bD: nc.scalar.copy(out=bm[0][:], in_=ps[bD][0:64, :])), [f"ps{bD}"], ["bm0"])
                            P.op("dve", (lambda bD=bD: nc.vector.scalar_tensor_tensor(out=nn[0][:], in0=ps[bD][0:64, :], scalar=-1.0,
                                                                                     in1=idt8[:].rearrange("p a b -> p (a b)"), op0=ALU.mult, op1=ALU.add)),
                                 [f"ps{bD}", "idt8"], ["nn0"])
                            yield
                            cur = 0
                            for s_ in range(1, 6):
                                nx = 1 - cur
                                bE = nbk_()
                                for ci in range(8):
                                    P.op("pe", (lambda ci=ci, bE=bE, cur=cur: nc.tensor.matmul(ps[bE][0:64, cc(ci)], lhsT=bm[cur][:, cc(ci)], rhs=al[cur][:, cc(ci)],
                                                                                              start=True, stop=True)), [f"bm{cur}", f"al{cur}"], [f"ps{bE}"], sig=(ci == 7))
                                P.op("act", (lambda bE=bE, nx=nx: nc.scalar.copy(out=al[nx][:], in_=ps[bE][0:64, :])), [f"ps{bE}"], [f"al{nx}"])
                                if s_ < 5:
                                    bF = nbk_()
                                    for ci in range(8):
                                        P.op("pe", (lambda ci=ci, bF=bF, cur=cur: nc.tensor.matmul(ps[bF][0:64, cc(ci)], lhsT=al[cur][:, cc(ci)], rhs=bm[cur][:, cc(ci)],
                                                                                                  start=True, stop=True)), [f"bm{cur}", f"al{cur}"], [f"ps{bF}"], sig=(ci == 7))
                                    P.op("dve", (lambda bF=bF, nx=nx: nc.vector.tensor_copy(out=bm[nx][:], in_=ps[bF][0:64, :])), [f"ps{bF}"], [f"bm{nx}"])
                                bG_ = nbk_()
                                for ci in range(8):
                                    P.op("pe", (lambda ci=ci, bG_=bG_, cur=cur, nx=nx: nc.tensor.matmul(ps[bG_][0:64, cc(ci)], lhsT=al[nx][:, cc(ci)], rhs=nn[cur][:, cc(ci)],
                                                                                                       start=True, stop=True)), [f"al{nx}", f"nn{cur}"], [f"ps{bG_}"], sig=(ci == 7))
                                P.op("dve", (lambda bG_=bG_, cur=cur, nx=nx: nc.vector.tensor_tensor(out=nn[nx][:], in0=nn[cur][:], in1=ps[bG_][0:64, :], op=ALU.add)),
                                     [f"ps{bG_}", f"nn{cur}"], [f"nn{nx}"])
                                cur = nx
                                yield
                            nfin = nn[cur]
                            nkey = f"nn{cur}"
                            P.op("act", (lambda nfin=nfin: nc.scalar.copy(out=nbf[:], in_=nfin[:])), [nkey], ["nbf"])
                            yield
                            for half in range(2):
                                bH = nbk_()
                                for u in range(4):
                                    ci = half * 4 + u
                                    P.op("pe", (lambda ci=ci, u=u, bH=bH: nc.tensor.matmul(ps[bH][0:64, u * 128:(u + 1) * 128], lhsT=kf[:, tc(ci)], rhs=ident,
                                                                                          start=True, stop=True)), ["kf", "cf"], [f"ps{bH}"], sig=(u == 3))
                                for u in range(4):
                                    ci = half * 4 + u
                                    P.op("dve", (lambda ci=ci, u=u, bH=bH: nc.vector.tensor_scalar(
                                        out=kbg[:, ci, :], in0=ps[bH][0:64, u * 128:(u + 1) * 128], scalar1=bgc[:, c0 + ci:c0 + ci + 1], scalar2=None, op0=ALU.mult)),
                                        [f"ps{bH}", "bgc"], ["kbg"])
                                    P.op("act", (lambda ci=ci, u=u, bH=bH, pp=pp: nc.scalar.activation(
                                        out=ktb[pp][:, ci, :], in_=ps[bH][0:64, u * 128:(u + 1) * 128], func=AF.Copy, scale=ekt[:, c0 + ci:c0 + ci + 1])),
                                        [f"ps{bH}", "ekt"], [f"ktb{pp}"])
                                bH = nbk_()
                                for u in range(4):
                                    ci = half * 4 + u
                                    P.op("pe", (lambda ci=ci, u=u, bH=bH: nc.tensor.matmul(ps[bH][0:64, u * 128:(u + 1) * 128], lhsT=vf[:, tc(ci)], rhs=ident,
                                                                                          start=True, stop=True)), ["vf", "cf"], [f"ps{bH}"], sig=(u == 3))
                                for u in range(4):
                                    ci = half * 4 + u
                                    P.op("dve", (lambda ci=ci, u=u, bH=bH: nc.vector.tensor_scalar(
                                        out=vb[:, ci, :], in0=ps[bH][0:64, u * 128:(u + 1) * 128], scalar1=gb[:, 2 * (c0 + ci) + 1:2 * (c0 + ci) + 2], scalar2=None, op0=ALU.mult)),
                                        [f"ps{bH}", "gb"], ["vb"])
                            yield
                            for half in range(2):
                                bU_ = nbk_()
                                for u in range(4):
                                    ci = half * 4 + u
                                    P.op("pe", (lambda ci=ci, u=u, bU_=bU_: nc.tensor.matmul(ps[bU_][0:64, u * 128:(u + 1) * 128], lhsT=nbf[:, cc(ci)], rhs=vb[:, ci, :],
                                                                                            start=True, stop=True)), ["nbf", "vb"], [f"ps{bU_}"], sig=(u == 3))
                                P.op("act", (lambda half=half, bU_=bU_, pp=pp: nc.scalar.copy(out=ub[pp][:, half * 4:half * 4 + 4, :],
                                                                                              in_=ps[bU_][0:64, :].rearrange("p (u e) -> p u e", e=128))),
                                     [f"ps{bU_}"], [f"ub{pp}"])
                            bW = nbk_()
                            for ci in range(8):
                                P.op("pe", (lambda ci=ci, bW=bW: nc.tensor.matmul(ps[bW][:, cc(ci)], lhsT=kbg[:, ci, :], rhs=nbf[:, cc(ci)],
                                                                                  start=True, stop=True)), ["nbf", "kbg"], [f"ps{bW}"], sig=(ci == 7))
                            P.op("dve", (lambda bW=bW, pp=pp: nc.vector.tensor_copy(out=wTb[pp][:], in_=ps[bW][:])), [f"ps{bW}"], [f"wTb{pp}"])
                            binfo[bt] = (nfin, nkey)
                            yield

                        def scan_gen(bt):
                            pp = bt % 2
                            c0 = bt * 8
                            cc = lambda ci: slice(ci * 64, (ci + 1) * 64)
                            for ci in range(8):
                                c = c0 + ci
                                vv = vn[c % 2]
                                vk = f"vn{c % 2}"
                                P.op("pe", (lambda ci=ci, pp=pp: nc.tensor.matmul(ps[4][0:64, 0:128], lhsT=wTb[pp][:, cc(ci)], rhs=Sbf[:], start=True, stop=True)),
                                     [f"wTb{pp}", "Sbf"], ["ps4"])
                                P.op("dve", (lambda ci=ci, pp=pp, vv=vv: nc.vector.tensor_tensor(out=vv[:], in0=ub[pp][:, ci, :], in1=ps[4][0:64, 0:128], op=ALU.subtract)),
                                     ["ps4", f"ub{pp}"], [vk])
                                yield
                                P.op("pe", (lambda ci=ci, pp=pp: nc.tensor.matmul(ps[5][0:64, 0:128], lhsT=qdT[pp][:, cc(ci)], rhs=Sbf[:], start=True, stop=False)),
                                     [f"qdT{pp}", "Sbf"], ["ps5"], sig=False)
                                P.op("pe", (lambda ci=ci, pp=pp, vv=vv: nc.tensor.matmul(ps[5][0:64, 0:128], lhsT=itT[pp][:, cc(ci)], rhs=vv[:], start=False, stop=True)),
                                     [f"itT{pp}", vk], ["ps5"])
                                P.op("pe", (lambda ci=ci, pp=pp, vv=vv: nc.tensor.matmul(ps[6][:, 0:128], lhsT=ktb[pp][:, ci, :], rhs=vv[:], start=True, stop=True)),
                                     [f"ktb{pp}", vk], ["ps6"])
                                P.op("dve", (lambda c=c: nc.vector.scalar_tensor_tensor(out=Sbf[:], in0=Sst[:], scalar=decb[:, c:c + 1], in1=ps[6][:, 0:128],
                                                                                        op0=ALU.mult, op1=ALU.add)), ["ps6", "decb", "Sst"], ["Sbf"])
                                P.op("dve", (lambda c=c: nc.vector.scalar_tensor_tensor(out=Sst[:], in0=Sst[:], scalar=decb[:, c:c + 1], in1=ps[6][:, 0:128],
                                                                                        op0=ALU.mult, op1=ALU.add)), ["ps6", "decb", "Sst"], ["Sst"])
                                yield
                                P.op("act", lambda: nc.scalar.activation(out=osq[:], in_=ps[5][0:64, 0:128], func=AF.Square, accum_out=ssc[:, 0:1]),
                                     ["ps5"], ["osq", "ssc"])
                                P.op("act", lambda: nc.scalar.activation(out=ssc[:, 1:2], in_=ssc[:, 0:1], func=AF.Sqrt, bias=epsc[0:64, 0:1], scale=1.0 / 128),
                                     ["ssc", "epsc"], ["ssc"])
                                P.op("dve", lambda: nc.vector.reciprocal(out=ssc[:, 1:2], in_=ssc[:, 1:2]), ["ssc"], ["ssc"])
                                P.op("dve", lambda: nc.vector.tensor_scalar(out=on[:], in0=ps[5][0:64, 0:128], scalar1=ssc[:, 1:2], scalar2=None, op0=ALU.mult),
                                     ["ps5", "ssc"], ["on"])
                                P.op("pe", lambda: nc.tensor.matmul(ps[7][:, 0:64], lhsT=on[:], rhs=id64, start=True, stop=True), ["on", "cf"], ["ps7"])
                                ocs = slice((c % 16) * 64, (c % 16) * 64 + 64)
                                gcs = slice(c * 64, (c + 1) * 64)
                                P.op("dve", (lambda ocs=ocs, gcs=gcs: nc.vector.scalar_tensor_tensor(out=ostg[:, ocs], in0=ps[7][:, 0:64], scalar=don[:, 0:1], in1=sz[:, gcs],
                                                                                                     op0=ALU.mult, op1=ALU.mult)), ["ps7", "don", "sz"], ["ostg"])
                                if c % 16 == 15:
                                    qb = c // 16
                                    P.dma("sp", ex2_in[qb][(2 + hl) * 128:(3 + hl) * 128, :], ostg[:], ["ostg"], [f"ex2in{qb}"], grp="w_ex2in")
                                    if hl == 1:
                                        P.op("pool", (lambda qb=qb: nc.gpsimd.collective_compute(
                                            "AllGather", ALU.bypass, replica_groups=GROUPS4, ins=[ex2_in[qb]],
                                            outs=[ex2_out[qb * 2048:(qb + 1) * 2048, :]])),
                                            [f"ex2in{qb}"], ["ex2out"], dma=True, grp=f"cc_ex2_{qb}", inc=1)
                                yield

                        for _ in prep_gen(0):
                            pass
                        for bt in range(8):
                            sg_ = scan_gen(bt)
                            pg_ = prep_gen(bt + 1) if bt + 1 < 8 else None
                            alive_s, alive_p = True, pg_ is not None
                            while alive_s or alive_p:
                                if alive_s:
                                    try:
                                        next(sg_)
                                    except StopIteration:
                                        alive_s = False
                                if alive_p:
                                    try:
                                        next(pg_)
                                    except StopIteration:
                                        alive_p = False
                        P.barrier()
            if "stopDelta" in debug:
                out_from_spill()
                return nc
            P.barrier()
        xst2 = contextlib.ExitStack()
        X["x"] = sb(xst2, "x2", [128, KC, T])
        x = X["x"]
        P.dma("sp", x[:], xspill, ["xspill"], ["x"])
        with contextlib.ExitStack() as st:
            idx2 = sb(st, "idx2", [128, 16], I32)
            og = [sb(st, f"og{j}", [128, T], BF16) for j in range(16)]
            wo = [sb(st, f"wo{i}", [128, KC, 256], BF16) for i in range(2)]
            P.dma("sp", idx2[:], idx2_d, [], ["idx2"])
            for j in range(16):
                P.op("pool", (lambda j=j: nc.gpsimd.indirect_dma_start(
                    out=og[j][:], out_offset=None, in_=ex2_out, in_offset=bass.IndirectOffsetOnAxis(ap=idx2[:, j:j + 1], axis=0))),
                    ["ex2out", "idx2"], [f"og{j}"], dma=True, grp="gath2")
            if "og" in debug:
                dbg["og"] = dout("dbg_og", [16, 128, T], BF16)
                for j in range(16):
                    P.dma("sp", dbg["og"][j], og[j][:], [f"og{j}"], ["dbg_og"])
            for g in range(8):
                s = g % 2
                P.dma("pool", wo[s][:], wo_d[:, g * 256:(g + 1) * 256].rearrange("(kc p) f -> p kc f", p=128), [], [f"wo{s}"])
                for dcl in range(2):
                    dc = 2 * g + dcl
                    for hf in range(2):
                        bk = (dcl * 2 + hf) % 4
                        tsl = slice(hf * 512, (hf + 1) * 512)
                        for j in range(16):
                            P.op("pe", (lambda j=j, dcl=dcl, tsl=tsl, bk=bk, s=s: nc.tensor.matmul(
                                ps[bk][:], lhsT=wo[s][:, j, dcl * 128:(dcl + 1) * 128], rhs=og[j][:, tsl],
                                start=(j == 0), stop=(j == 15))), [f"wo{s}", f"og{j}"], [f"ps{bk}"], sig=(j == 15))
                        P.op("dve", (lambda dc=dc, tsl=tsl, bk=bk: nc.vector.scalar_tensor_tensor(
                            out=x[:, dc, tsl], in0=ps[bk][:], scalar=gcol(1)[:, dc:dc + 1], in1=x[:, dc, tsl],
                            op0=ALU.mult, op1=ALU.add)), [f"ps{bk}", "cols", "x"], ["x"])
            P.barrier()
        if "x2" in debug:
            dbg["x2"] = dout("dbg_x2", [128, KC, T])
            P.dma("sp", dbg["x2"], x[:], ["x"], ["dbg_x2"])
        if stage >= 3:
            with contextlib.ExitStack() as st:
                h3 = sb(st, "h3", [128, KC, T], BF16)
                modulate(st, 2, h3)
                ffn(st, 2, h3, w2g_d, w2u_d, w2d_d)
                P.barrier()
        P.dma("sp", outT_d.rearrange("(kc p) t -> p kc t", p=128), x[:], ["x"], ["outT"])
        P.finish()
        xst2.close()
    return nc


def _consts():
    c = np.zeros((128, 6, 128), np.float32)
    c[:, 0, :] = np.eye(128, dtype=np.float32)
    c[:, 1, :] = 1.0
    permT = np.zeros((128, 128), np.float32)
    for m in range(16):
        permT[m + 16, m] = -1.0
        permT[m, m + 16] = 1.0
    c[:, 2, :] = permT
    U = np.zeros((128, 128), np.float32)
    U[:64, :64] = np.triu(np.ones((64, 64), np.float32))
    c[:, 3, :] = U
    sl = np.zeros((128, 128), np.float32)
    i = np.arange(64)[:, None]; j = np.arange(64)[None, :]
    sl[:64, :64] = np.where(i > j, 0.0, 1e30)
    c[:, 4, :] = sl
    ut = np.zeros((128, 128), np.float32)
    ut[:64, :64] = np.where(j >= i, 0.0, 1e30)
    c[:, 5, :] = ut
    return c


def make_in_maps(inputs, dff=DFF):
    x = np.asarray(inputs["x"], np.float32)
    c = np.asarray(inputs["c"], np.float32)
    w_ada = np.asarray(inputs["w_ada"])[0]
    b_ada = np.asarray(inputs["b_ada"])[0]
    gains = np.stack([np.asarray(inputs[k])[0].reshape(KC, 128).T for k in ("ffn1_norm", "mix_norm", "ffn2_norm")], axis=1)
    cT = np.ascontiguousarray(c.reshape(2, KC, 128).transpose(2, 1, 0))
    cf = _consts()
    maps = []
    for cid in range(NCORES):
        b, tq = cid // 4, cid % 4
        sel = np.zeros((128, 2), np.float32); sel[:, b] = 1.0
        m = {
            "xT": np.ascontiguousarray(x[b, tq * T:(tq + 1) * T, :].T),
            "cT": cT, "sel": sel,
            "wada": np.ascontiguousarray(w_ada[:, tq * 4608:(tq + 1) * 4608]),
            "bada": np.ascontiguousarray(b_ada[tq * 4608:(tq + 1) * 4608].reshape(36, 128).T),
            "gains": np.ascontiguousarray(gains),
            "w1g": np.asarray(inputs["ffn1_w_gate"])[0][:, :dff], "w1u": np.asarray(inputs["ffn1_w_up"])[0][:, :dff],
            "w1d": np.asarray(inputs["ffn1_w_down"])[0][:dff],
            "w2g": np.asarray(inputs["ffn2_w_gate"])[0][:, :dff], "w2u": np.asarray(inputs["ffn2_w_up"])[0][:, :dff],
            "w2d": np.asarray(inputs["ffn2_w_down"])[0][:dff],
            "cf32": cf,
        }
        hp = tq
        w_in = np.asarray(inputs["w_in"])[0]
        heads = [2 * hp, 2 * hp + 1]
        m["pos"] = np.ascontiguousarray(np.asarray(inputs["positions"])[b].reshape(1, S).astype(np.int32))
        m["wA"] = np.stack([np.concatenate([w_in[:, t * 1024 + h * 128:t * 1024 + (h + 1) * 128] for t in range(3)], axis=1) for h in heads])
        m["wD"] = np.stack([np.concatenate([w_in[:, 3072 + t * 1024 + h * 128:3072 + t * 1024 + (h + 1) * 128] for t in range(4)], axis=1) for h in heads])
        m["wab"] = np.ascontiguousarray(np.stack([w_in[:, 7168 + heads[0]], w_in[:, 7168 + heads[1]],
                                                  w_in[:, 7176 + heads[0]], w_in[:, 7176 + heads[1]]], axis=1))
        m["qkg"] = np.ascontiguousarray(np.stack([np.asarray(inputs["q_norm"])[0], np.asarray(inputs["k_norm"])[0]], axis=1))
        cw = np.asarray(inputs["conv_w"])[0]
        convc = np.zeros((128, 2, 3, 4), np.float32)
        for hl, h in enumerate(heads):
            for t in range(3):
                convc[:, hl, t, :] = cw[:, t * 1024 + h * 128:t * 1024 + (h + 1) * 128].T
        m["convc"] = convc
        abp = np.zeros((4, 2), np.float32)
        for hl, h in enumerate(heads):
            abp[hl, 0] = np.asarray(inputs["a_log"])[0, h]
            abp[hl, 1] = np.asarray(inputs["dt_bias"])[0, h]
        m["abp"] = abp
        m["don"] = np.ascontiguousarray(np.asarray(inputs["delta_out_norm"])[0].reshape(128, 1))
        invf = np.zeros((128, 1), np.float32)
        invf[:32, 0] = np.tile((np.float32(500000.0) ** (-np.arange(16, dtype=np.float32) / np.float32(16))).astype(np.float32), 2)
        m["invf"] = invf
        kj = np.arange(128)[:, None]; qi = np.arange(128)[None, :]
        am = np.concatenate([np.where(qi >= kj, 0.0, -30000.0), np.where(kj >= qi, 0.0, -30000.0)], axis=1).astype(np.float32)
        m["amask"] = am.astype(ml_dtypes.bfloat16)
        m["idx2"] = (tq * 2048 + np.arange(16)[None, :] * 128 + np.arange(128)[:, None]).astype(np.int32)
        w_out = np.asarray(inputs["w_out"])[0]
        rows = []
        for r in range(4):
            for k in range(4):
                base = (2 * r + k) * 128 if k < 2 else 1024 + (2 * r + k - 2) * 128
                rows.append(w_out[base:base + 128])
        m["wo"] = np.ascontiguousarray(np.concatenate(rows, axis=0))
        maps.append(m)
    return maps


def assemble(results, key="outT"):
    out = np.zeros((2, S, D), np.float32)
    for cid in range(NCORES):
        b, tq = cid // 4, cid % 4
        out[b, tq * T:(tq + 1) * T, :] = np.asarray(results[cid][key]).T
    return out


def kernel(**inputs):
    nc = build(stage=3)
    maps = make_in_maps(inputs)
    res = run_bass_kernel_spmd(nc, maps, core_ids=list(range(NCORES)))
    return assemble(res.results)
```
